# Optimizing a Trainium2 kernel written in Bass

```python
import math
import jax
import jax.numpy as jnp
from jax import lax
import numpy as np

D_MODEL = 1024
BATCH = 8
SEQ = 8192
DEPTH = 2
DEC_BATCH = 16
DEC_SEQ = 16
PAST_LEN = 1024

CHUNK = 64
N_EVEN = (DEPTH + 1) // 2
N_ODD = DEPTH // 2
RMS_EPS = 1e-6
L2_EPS = 1e-6

S5_WIDTH = D_MODEL // 2
S5_GROUP = 16
S5_GROUPS = S5_WIDTH // S5_GROUP
S5_P = 64

GDN_DK = 128
GDN_DV = 128
GDN_HEADS = (D_MODEL - S5_WIDTH) // GDN_DV
GDN_QK = GDN_HEADS * GDN_DK
GDN_V = GDN_HEADS * GDN_DV
GDN_CONV = 4
GDN_CONV_CH = 2 * GDN_QK + GDN_V

OFF_QKV = S5_WIDTH
OFF_Z = OFF_QKV + GDN_CONV_CH
OFF_B = OFF_Z + GDN_V
OFF_A = OFF_B + GDN_HEADS
IN_COLS = OFF_A + GDN_HEADS
D_MIX_AB = S5_WIDTH + GDN_V

SWA_HEADS = 16
SWA_KV_HEADS = 4
SWA_GROUPS = SWA_HEADS // SWA_KV_HEADS
SWA_HEAD_DIM = 64
WINDOW = 128
WIN_CHUNKS = -(-WINDOW // CHUNK)
KV_WIN = min(WINDOW, PAST_LEN)

D_FF = -(-8 * D_MODEL // (3 * 256)) * 256

kernel_name = "hybrid_s5_gdn_swa_stream_step"


def rmsnorm(x, w):
    xf = x.astype(jnp.float32)
    y = xf * lax.rsqrt(jnp.mean(xf * xf, axis=-1, keepdims=True) + RMS_EPS)
    return (y * w.astype(jnp.float32)).astype(x.dtype)


def l2norm(x):
    return x * lax.rsqrt(jnp.sum(x * x, axis=-1, keepdims=True) + L2_EPS)


def swiglu(h, w_gate, w_up, w_down):
    return (jax.nn.silu(h @ w_gate) * (h @ w_up)) @ w_down


def s5_mixer(u, x0_re, x0_im, lam_re, lam_im, log_dt, b_re, b_im, c_re, c_im, d, w_glu, b_glu, chunk):
    f32 = jnp.float32
    bsz, L, _ = u.shape
    n = L // chunk
    uf = u.astype(f32).reshape(bsz, L, S5_GROUPS, S5_GROUP)
    lam = lax.complex(lam_re.astype(f32), lam_im.astype(f32))
    dt = jnp.exp(log_dt.astype(f32))[:, None]
    lam_bar = jnp.exp(lam * dt)
    b_bar = ((lam_bar - 1.0) / lam)[..., None] * lax.complex(b_re.astype(f32), b_im.astype(f32))
    c = lax.complex(c_re.astype(f32), c_im.astype(f32))
    a_blk = jnp.broadcast_to(lam_bar, (bsz, chunk, S5_GROUPS, S5_P))

    def combine(e1, e2):
        a1, b1 = e1
        a2, b2 = e2
        return a1 * a2, a2 * b1 + b2

    def step(carry, u_blk):
        bu = jnp.einsum('gpc,btgc->btgp', b_bar, u_blk.astype(jnp.complex64))
        bu = bu.at[:, 0].add(lam_bar * carry)
        _, xs = lax.associative_scan(combine, (a_blk, bu), axis=1)
        y_blk = jnp.einsum('gcp,btgp->btgc', c, xs).real
        return xs[:, -1], y_blk

    x0 = lax.complex(x0_re.astype(f32), x0_im.astype(f32))
    u_blocks = jnp.moveaxis(uf.reshape(bsz, n, chunk, S5_GROUPS, S5_GROUP), 1, 0)
    x_last, y = lax.scan(step, x0, u_blocks)
    y = jnp.moveaxis(y, 0, 1).reshape(bsz, L, S5_GROUPS, S5_GROUP)
    y = y + d.astype(f32).reshape(S5_GROUPS, S5_GROUP) * uf
    y = y.reshape(bsz, L, S5_WIDTH).astype(u.dtype)
    z = jax.nn.gelu(y)
    out = z * jax.nn.sigmoid(z @ w_glu + b_glu)
    return out, x_last.real, x_last.imag


def causal_conv(x, buf, w):
    L = x.shape[1]
    xp = jnp.concatenate([buf.astype(x.dtype), x], axis=1)
    y = xp[:, 0:L] * w[0]
    for j in range(1, GDN_CONV):
        y = y + xp[:, j:j + L] * w[j]
    return y, xp[:, -(GDN_CONV - 1):]


def gated_delta_chunked(q, k, v, g, beta, s0, chunk):
    bsz, L, H, _ = q.shape
    n = L // chunk

    def blk(t):
        t = t.reshape((bsz, n, chunk, H) + t.shape[3:])
        return jnp.moveaxis(t, 3, 1)

    q, k, v, g, beta = blk(q), blk(k), blk(v), blk(g), blk(beta)
    gc = jnp.cumsum(g, axis=-1)
    idx = jnp.arange(chunk)
    incl = idx[:, None] >= idx[None, :]
    strict = idx[:, None] > idx[None, :]
    decay = jnp.exp(jnp.where(incl, gc[..., :, None] - gc[..., None, :], -jnp.inf))
    kb = k * beta[..., None]
    a_mat = jnp.where(strict, jnp.einsum('bhnid,bhnjd->bhnij', kb, k) * decay, 0.0)
    eye = jnp.eye(chunk, dtype=q.dtype)
    t_inv = lax.linalg.triangular_solve(a_mat + eye, jnp.broadcast_to(eye, a_mat.shape),
                                        left_side=True, lower=True, unit_diagonal=True)
    gexp = jnp.exp(gc)[..., None]
    u_val = t_inv @ (v * beta[..., None])
    w_key = t_inv @ (kb * gexp)
    q_dec = q * gexp
    attn = jnp.einsum('bhnid,bhnjd->bhnij', q, k) * decay
    k_dec = k * jnp.exp(gc[..., -1:] - gc)[..., None]
    g_last = jnp.exp(gc[..., -1])

    def step(s, xs):
        u_c, w_c, q_c, a_c, k_c, gl = xs
        v_new = u_c - w_c @ s
        o = q_c @ s + a_c @ v_new
        s = s * gl[..., None, None] + jnp.swapaxes(k_c, -1, -2) @ v_new
        return s, o

    xs = tuple(jnp.moveaxis(t, 2, 0) for t in (u_val, w_key, q_dec, attn, k_dec, g_last))
    s_fin, o = lax.scan(step, s0, xs)
    o = jnp.moveaxis(jnp.moveaxis(o, 0, 2), 1, 3).reshape(bsz, L, H, -1)
    return o, s_fin


def gdn_mixer(qkv, z, b, a, conv_buf, s0, conv_w, a_log, dt_bias, norm_w, chunk):
    f32 = jnp.float32
    bsz, L, _ = qkv.shape
    y, new_buf = causal_conv(qkv, conv_buf, conv_w)
    y = jax.nn.silu(y).astype(f32)
    q = l2norm(y[..., :GDN_QK].reshape(bsz, L, GDN_HEADS, GDN_DK)) * (GDN_DK ** -0.5)
    k = l2norm(y[..., GDN_QK:2 * GDN_QK].reshape(bsz, L, GDN_HEADS, GDN_DK))
    v = y[..., 2 * GDN_QK:].reshape(bsz, L, GDN_HEADS, GDN_DV)
    beta = jax.nn.sigmoid(b.astype(f32))
    g = -jnp.exp(a_log.astype(f32)) * jax.nn.softplus(a.astype(f32) + dt_bias.astype(f32))
    o, s_fin = gated_delta_chunked(q, k, v, g, beta, s0.astype(f32), chunk)
    o = rmsnorm(o, norm_w) * jax.nn.silu(z.astype(f32).reshape(bsz, L, GDN_HEADS, GDN_DV))
    return o.reshape(bsz, L, GDN_V).astype(qkv.dtype), s_fin, new_buf


def mixer_ab(h, x0_re, x0_im, s0, conv_buf, w_in, lam_re, lam_im, log_dt, b_re, b_im, c_re, c_im,
             d, w_glu, b_glu, conv_w, a_log, dt_bias, norm_w, w_out, chunk):
    proj = h @ w_in
    a_out, x_re, x_im = s5_mixer(proj[..., :OFF_QKV], x0_re, x0_im, lam_re, lam_im, log_dt,
                                 b_re, b_im, c_re, c_im, d, w_glu, b_glu, chunk)
    b_out, s_fin, new_buf = gdn_mixer(proj[..., OFF_QKV:OFF_Z], proj[..., OFF_Z:OFF_B],
                                      proj[..., OFF_B:OFF_A], proj[..., OFF_A:IN_COLS],
                                      conv_buf, s0, conv_w, a_log, dt_bias, norm_w, chunk)
    out = jnp.concatenate([a_out, b_out], axis=-1) @ w_out
    return out, x_re, x_im, s_fin, new_buf


def sink_softmax(scores, sinks):
    sk = jnp.broadcast_to(sinks.astype(jnp.float32).reshape(SWA_KV_HEADS, SWA_GROUPS, 1, 1),
                          scores.shape[:-1] + (1,))
    return jax.nn.softmax(jnp.concatenate([scores, sk], axis=-1), axis=-1)[..., :-1]


def swa_prompt(h, wq, wk, wv, sinks, wo):
    bsz, L, _ = h.shape
    n = L // CHUNK
    span = (WIN_CHUNKS + 1) * CHUNK
    q = (h @ wq).reshape(bsz, n, CHUNK, SWA_KV_HEADS, SWA_GROUPS, SWA_HEAD_DIM)
    k = (h @ wk).reshape(bsz, L, SWA_KV_HEADS, SWA_HEAD_DIM)
    v = (h @ wv).reshape(bsz, L, SWA_KV_HEADS, SWA_HEAD_DIM)
    pad = ((0, 0), (WIN_CHUNKS, 0), (0, 0), (0, 0), (0, 0))
    kp = jnp.pad(k.reshape(bsz, n, CHUNK, SWA_KV_HEADS, SWA_HEAD_DIM), pad)
    vp = jnp.pad(v.reshape(bsz, n, CHUNK, SWA_KV_HEADS, SWA_HEAD_DIM), pad)
    kband = jnp.concatenate([kp[:, j:j + n] for j in range(WIN_CHUNKS + 1)], axis=2)
    vband = jnp.concatenate([vp[:, j:j + n] for j in range(WIN_CHUNKS + 1)], axis=2)
    key_chunk = jnp.arange(n)[:, None] - WIN_CHUNKS + jnp.arange(span)[None, :] // CHUNK
    valid = key_chunk >= 0
    scores = jnp.einsum('bnqkgd,bnskd->bnkgqs', q, kband).astype(jnp.float32) * (SWA_HEAD_DIM ** -0.5)
    scores = jnp.where(valid[None, :, None, None, None, :], scores, -1e30)
    p = sink_softmax(scores, sinks).astype(vband.dtype)
    o = jnp.einsum('bnkgqs,bnskd->bnqkgd', p, vband).reshape(bsz, L, SWA_HEADS * SWA_HEAD_DIM)
    return o @ wo, k[:, -KV_WIN:], v[:, -KV_WIN:]


def swa_sample(h, cache_k, cache_v, wq, wk, wv, sinks, wo):
    bsz, L, _ = h.shape
    q = (h @ wq).reshape(bsz, L, SWA_KV_HEADS, SWA_GROUPS, SWA_HEAD_DIM)
    k_new = (h @ wk).reshape(bsz, L, SWA_KV_HEADS, SWA_HEAD_DIM)
    v_new = (h @ wv).reshape(bsz, L, SWA_KV_HEADS, SWA_HEAD_DIM)
    k_all = jnp.concatenate([cache_k.astype(k_new.dtype), k_new], axis=1)
    v_all = jnp.concatenate([cache_v.astype(v_new.dtype), v_new], axis=1)
    scores = jnp.einsum('bqkgd,bskd->bkgqs', q, k_all).astype(jnp.float32) * (SWA_HEAD_DIM ** -0.5)
    p = sink_softmax(scores, sinks).astype(v_all.dtype)
    o = jnp.einsum('bkgqs,bskd->bqkgd', p, v_all).reshape(bsz, L, SWA_HEADS * SWA_HEAD_DIM)
    return o @ wo, k_all[:, -KV_WIN:], v_all[:, -KV_WIN:]


def trunk(x, s5_re, s5_im, gdn_s, gdn_conv, kv_k, kv_v, p, is_prompt):
    chunk = CHUNK if is_prompt else x.shape[1]
    n_re, n_im, n_s, n_conv, n_k, n_v = [], [], [], [], [], []
    for layer in range(DEPTH):
        j = layer // 2
        h = rmsnorm(x, p['norm_mix'][layer])
        if layer % 2 == 0:
            mix, x_re, x_im, s_fin, buf = mixer_ab(
                h, s5_re[j], s5_im[j], gdn_s[j], gdn_conv[j], p['w_in'][j],
                p['s5_lam_re'][j], p['s5_lam_im'][j], p['s5_log_dt'][j], p['s5_b_re'][j], p['s5_b_im'][j],
                p['s5_c_re'][j], p['s5_c_im'][j], p['s5_d'][j], p['s5_w_glu'][j], p['s5_b_glu'][j],
                p['gdn_conv_w'][j], p['gdn_a_log'][j], p['gdn_dt_bias'][j], p['gdn_norm_w'][j],
                p['w_out_ab'][j], chunk)
            n_re.append(x_re)
            n_im.append(x_im)
            n_s.append(s_fin)
            n_conv.append(buf)
        else:
            if is_prompt:
                mix, ck, cv = swa_prompt(h, p['swa_wq'][j], p['swa_wk'][j], p['swa_wv'][j],
                                         p['swa_sinks'][j], p['swa_wo'][j])
            else:
                mix, ck, cv = swa_sample(h, kv_k[j], kv_v[j], p['swa_wq'][j], p['swa_wk'][j],
                                         p['swa_wv'][j], p['swa_sinks'][j], p['swa_wo'][j])
            n_k.append(ck)
            n_v.append(cv)
        x = x + mix
        x = x + swiglu(rmsnorm(x, p['norm_ffn'][layer]), p['ffn_w_gate'][layer],
                       p['ffn_w_up'][layer], p['ffn_w_down'][layer])
    y = rmsnorm(x, p['norm_final'])
    return y, jnp.stack(n_re), jnp.stack(n_im), jnp.stack(n_s), jnp.stack(n_conv), jnp.stack(n_k), jnp.stack(n_v)


def setup_inputs(seed: int = 0) -> dict:
    key = jax.random.key(seed)
    ks = iter(jax.random.split(key, 48))
    f32 = jnp.float32

    def nrm(shape, scale):
        return scale * jax.random.normal(next(ks), shape, f32)

    def unif(shape, lo, hi):
        return jax.random.uniform(next(ks), shape, f32, lo, hi)

    x_prompt = nrm((BATCH, SEQ, D_MODEL), 1.0)
    x_sample = nrm((DEC_BATCH, DEC_SEQ, D_MODEL), 1.0)
    state_s5_re = nrm((N_EVEN, DEC_BATCH, S5_GROUPS, S5_P), 0.5)
    state_s5_im = nrm((N_EVEN, DEC_BATCH, S5_GROUPS, S5_P), 0.5)
    state_gdn = nrm((N_EVEN, DEC_BATCH, GDN_HEADS, GDN_DK, GDN_DV), 0.1)
    state_gdn_conv = nrm((N_EVEN, DEC_BATCH, GDN_CONV - 1, GDN_CONV_CH), 1.0)
    cache_swa_k = nrm((N_ODD, DEC_BATCH, KV_WIN, SWA_KV_HEADS, SWA_HEAD_DIM), 1.0)
    cache_swa_v = nrm((N_ODD, DEC_BATCH, KV_WIN, SWA_KV_HEADS, SWA_HEAD_DIM), 1.0)
    norm_mix = 1.0 + nrm((DEPTH, D_MODEL), 0.02)
    norm_ffn = 1.0 + nrm((DEPTH, D_MODEL), 0.02)
    norm_final = 1.0 + nrm((D_MODEL,), 0.02)
    w_in = nrm((N_EVEN, D_MODEL, IN_COLS), D_MODEL ** -0.5)
    n_idx = jnp.arange(S5_P, dtype=f32)
    s5_lam_re = -0.5 + nrm((N_EVEN, S5_GROUPS, S5_P), 0.01)
    s5_lam_im = math.pi * n_idx + nrm((N_EVEN, S5_GROUPS, S5_P), 0.01)
    s5_log_dt = unif((N_EVEN, S5_GROUPS), math.log(1e-3), math.log(1e-1))
    s5_b_re = nrm((N_EVEN, S5_GROUPS, S5_P, S5_GROUP), (2 * S5_GROUP) ** -0.5)
    s5_b_im = nrm((N_EVEN, S5_GROUPS, S5_P, S5_GROUP), (2 * S5_GROUP) ** -0.5)
    s5_c_re = nrm((N_EVEN, S5_GROUPS, S5_GROUP, S5_P), (2 * S5_P) ** -0.5)
    s5_c_im = nrm((N_EVEN, S5_GROUPS, S5_GROUP, S5_P), (2 * S5_P) ** -0.5)
    s5_d = nrm((N_EVEN, S5_WIDTH), 0.5)
    s5_w_glu = nrm((N_EVEN, S5_WIDTH, S5_WIDTH), S5_WIDTH ** -0.5)
    s5_b_glu = nrm((N_EVEN, S5_WIDTH), 0.01)
    gdn_conv_w = nrm((N_EVEN, GDN_CONV, GDN_CONV_CH), GDN_CONV ** -0.5)
    gdn_a_log = jnp.log(unif((N_EVEN, GDN_HEADS), 1.0, 16.0))
    dt0 = jnp.exp(unif((N_EVEN, GDN_HEADS), math.log(1e-3), math.log(1e-1)))
    gdn_dt_bias = dt0 + jnp.log(-jnp.expm1(-dt0))
    gdn_norm_w = 1.0 + nrm((N_EVEN, GDN_DV), 0.02)
    w_out_ab = nrm((N_EVEN, D_MIX_AB, D_MODEL), D_MIX_AB ** -0.5)
    swa_wq = nrm((N_ODD, D_MODEL, SWA_HEADS * SWA_HEAD_DIM), D_MODEL ** -0.5)
    swa_wk = nrm((N_ODD, D_MODEL, SWA_KV_HEADS * SWA_HEAD_DIM), D_MODEL ** -0.5)
    swa_wv = nrm((N_ODD, D_MODEL, SWA_KV_HEADS * SWA_HEAD_DIM), D_MODEL ** -0.5)
    swa_sinks = nrm((N_ODD, SWA_HEADS), 0.5)
    swa_wo = nrm((N_ODD, SWA_HEADS * SWA_HEAD_DIM, D_MODEL), (SWA_HEADS * SWA_HEAD_DIM) ** -0.5)
    ffn_w_gate = nrm((DEPTH, D_MODEL, D_FF), D_MODEL ** -0.5)
    ffn_w_up = nrm((DEPTH, D_MODEL, D_FF), D_MODEL ** -0.5)
    ffn_w_down = nrm((DEPTH, D_FF, D_MODEL), D_FF ** -0.5)
    return {"x_prompt": x_prompt, "x_sample": x_sample,
            "state_s5_re": state_s5_re, "state_s5_im": state_s5_im, "state_gdn": state_gdn,
            "state_gdn_conv": state_gdn_conv, "cache_swa_k": cache_swa_k, "cache_swa_v": cache_swa_v,
            "norm_mix": norm_mix, "norm_ffn": norm_ffn, "norm_final": norm_final, "w_in": w_in,
            "s5_lam_re": s5_lam_re, "s5_lam_im": s5_lam_im, "s5_log_dt": s5_log_dt,
            "s5_b_re": s5_b_re, "s5_b_im": s5_b_im, "s5_c_re": s5_c_re, "s5_c_im": s5_c_im,
            "s5_d": s5_d, "s5_w_glu": s5_w_glu, "s5_b_glu": s5_b_glu,
            "gdn_conv_w": gdn_conv_w, "gdn_a_log": gdn_a_log, "gdn_dt_bias": gdn_dt_bias,
            "gdn_norm_w": gdn_norm_w, "w_out_ab": w_out_ab,
            "swa_wq": swa_wq, "swa_wk": swa_wk, "swa_wv": swa_wv, "swa_sinks": swa_sinks, "swa_wo": swa_wo,
            "ffn_w_gate": ffn_w_gate, "ffn_w_up": ffn_w_up, "ffn_w_down": ffn_w_down}


def reference(x_prompt, x_sample, state_s5_re, state_s5_im, state_gdn, state_gdn_conv, cache_swa_k, cache_swa_v,
              norm_mix, norm_ffn, norm_final, w_in, s5_lam_re, s5_lam_im, s5_log_dt, s5_b_re, s5_b_im,
              s5_c_re, s5_c_im, s5_d, s5_w_glu, s5_b_glu, gdn_conv_w, gdn_a_log, gdn_dt_bias, gdn_norm_w,
              w_out_ab, swa_wq, swa_wk, swa_wv, swa_sinks, swa_wo, ffn_w_gate, ffn_w_up, ffn_w_down):
    p = dict(norm_mix=norm_mix, norm_ffn=norm_ffn, norm_final=norm_final, w_in=w_in,
             s5_lam_re=s5_lam_re, s5_lam_im=s5_lam_im, s5_log_dt=s5_log_dt, s5_b_re=s5_b_re, s5_b_im=s5_b_im,
             s5_c_re=s5_c_re, s5_c_im=s5_c_im, s5_d=s5_d, s5_w_glu=s5_w_glu, s5_b_glu=s5_b_glu,
             gdn_conv_w=gdn_conv_w, gdn_a_log=gdn_a_log, gdn_dt_bias=gdn_dt_bias, gdn_norm_w=gdn_norm_w,
             w_out_ab=w_out_ab, swa_wq=swa_wq, swa_wk=swa_wk, swa_wv=swa_wv, swa_sinks=swa_sinks,
             swa_wo=swa_wo, ffn_w_gate=ffn_w_gate, ffn_w_up=ffn_w_up, ffn_w_down=ffn_w_down)
    bsz = x_prompt.shape[0]
    f32 = jnp.float32
    z_s5 = jnp.zeros((N_EVEN, bsz, S5_GROUPS, S5_P), f32)
    z_gdn = jnp.zeros((N_EVEN, bsz, GDN_HEADS, GDN_DK, GDN_DV), f32)
    z_conv = jnp.zeros((N_EVEN, bsz, GDN_CONV - 1, GDN_CONV_CH), f32)
    y_prompt, p_s5_re, p_s5_im, p_gdn, p_gdn_conv, p_swa_k, p_swa_v = trunk(
        x_prompt, z_s5, z_s5, z_gdn, z_conv, None, None, p, True)
    y_sample, s_s5_re, s_s5_im, s_gdn, s_gdn_conv, s_swa_k, s_swa_v = trunk(
        x_sample, state_s5_re, state_s5_im, state_gdn, state_gdn_conv, cache_swa_k, cache_swa_v, p, False)
    return (y_prompt, y_sample, p_s5_re, p_s5_im, p_gdn, p_gdn_conv, p_swa_k, p_swa_v,
            s_s5_re, s_s5_im, s_gdn, s_gdn_conv, s_swa_k, s_swa_v)
```

```python
import contextlib
import numpy as np
import concourse.bass as bass
import concourse.mybir as mybir
from concourse.bass_utils import run_bass_kernel_spmd

F32 = mybir.dt.float32
BF16 = mybir.dt.bfloat16
ALU = mybir.AluOpType
AF = mybir.ActivationFunctionType
AX = mybir.AxisListType

import os
_S5STOP = int(os.environ.get('S5STOP', '0'))
_RISK = int(os.environ.get('RISK', '0'))
_ILV = int(os.environ.get('ILV', '3'))
_PRENG = os.environ.get('PRENG', 'dve')
_SWASTOP = int(os.environ.get('SWASTOP', '0'))
_ATT = int(os.environ.get('ATT', '9'))
NCORES = 8
DM = 1024
SEQ = 8192
TT = 512
DEC_SEQ = 16
IN_COLS = 2568
DFF = 2816
EPS = 1e-6


class _Op:
    __slots__ = ("eng", "fn", "deps", "sig", "seq", "dsem", "dval", "n")

    def __init__(self, eng, fn):
        self.eng = eng
        self.fn = fn
        self.deps = []
        self.sig = False
        self.seq = 0
        self.dsem = None
        self.dval = 0
        self.n = 0


class Prog:
    ENGS = ("pe", "act", "dve", "pool", "sp")

    def __init__(self, nc):
        self.nc = nc
        self.q = {e: [] for e in self.ENGS}
        self.st = {}
        self.dcount = {}
        self.groups = {}

    def _expand(self, keys):
        out = []
        for k in keys:
            out.extend(self.groups.get(k, (k,)))
        return out

    def add(self, eng, fn, r=(), w=(), dma=None):
        op = _Op(eng, fn)
        r0, w0 = r, w
        r, w = self._expand(r), self._expand(w)
        deps = []
        for k in r:
            s = self.st.get(k)
            if s is not None and s[0] is not None:
                deps.append(s[0])
        for k in w:
            s = self.st.get(k)
            if s is not None:
                if s[0] is not None:
                    deps.append(s[0])
                deps.extend(s[1])
        is_dma = dma is not None
        op.n = self.nops = getattr(self, "nops", 0) + 1
        rawset = set()
        for k in r:
            s_ = self.st.get(k)
            if s_ is not None and s_[0] is not None:
                rawset.add(id(s_[0]))
        latest = {}
        for d in deps:
            if d.dsem is not None:
                continue
            if d.eng == eng and not is_dma and id(d) not in rawset:
                continue
            if d.eng not in latest or d.n > latest[d.eng].n:
                latest[d.eng] = d
        deps = [d for d in deps if d.dsem is not None or latest.get(d.eng) is d]
        if dma in ("init", "ldst", "ldkv"):
            dma = dma[0] + "_" + w0[0]
        elif dma == "out":
            dma = "o_" + (r0[0] if len(r0) else "dram")
        seen = set()
        for d in deps:
            if id(d) in seen or d is op:
                continue
            seen.add(id(d))
            if d.dsem is None and d.eng == eng and not is_dma:
                if eng == "pe":
                    continue
                israw = False
                for k in r:
                    s = self.st.get(k)
                    if s is not None and s[0] is d:
                        israw = True
                if not israw:
                    continue
            op.deps.append((d, self.dcount[d.dsem] if d.dsem is not None else 0))
            if d.dsem is None:
                d.sig = True
        for k in r:
            s = self.st.setdefault(k, [None, []])
            s[1].append(op)
        for k in w:
            self.st[k] = [op, []]
        if is_dma:
            op.dsem = dma
            self.dcount[dma] = self.dcount.get(dma, 0) + 16
            op.dval = self.dcount[dma]
        self.q[eng].append(op)
        return op

    def emit(self, final_eng="sp"):
        nc = self.nc
        with contextlib.ExitStack() as es:
            esem = {e: es.enter_context(nc.semaphore("S_" + e)) for e in self.ENGS}
            dsem = {n: es.enter_context(nc.semaphore("D_" + n)) for n in self.dcount}
            for e in self.ENGS:
                c = 0
                for op in self.q[e]:
                    if op.sig:
                        c += 1
                        op.seq = c
            block = es.enter_context(nc.Block())

            def run(e, eng):
                waited = {}
                for op in self.q[e]:
                    for d, dv in op.deps:
                        if d.dsem is not None:
                            key, sem, val = ("d", d.dsem), dsem[d.dsem], dv
                        else:
                            key, sem, val = ("e", d.eng), esem[d.eng], d.seq
                        if waited.get(key, 0) >= val:
                            continue
                        waited[key] = val
                        eng.wait_ge(sem, val)
                    ins = op.fn(eng)
                    if op.dsem is not None:
                        ins.then_inc(dsem[op.dsem], 16)
                    elif op.sig:
                        ins.then_inc(esem[e], 1)
                if e == final_eng:
                    for n, cnt in self.dcount.items():
                        eng.wait_ge(dsem[n], cnt)

            @block.tensor
            def _(eng):
                run("pe", eng)

            @block.scalar
            def _(eng):
                run("act", eng)

            @block.vector
            def _(eng):
                run("dve", eng)

            @block.gpsimd
            def _(eng):
                run("pool", eng)

            @block.sync
            def _(eng):
                run("sp", eng)


def _consts():
    c = {}
    c["ident"] = np.eye(128, dtype=np.float32)
    j = np.arange(128)[:, None] % 64
    i = np.arange(64)[None, :]
    c["mUs"] = (i > j).astype(np.float32)
    c["mUi"] = (i >= j).astype(np.float32)
    c["eye64"] = (i == j).astype(np.float32)
    sel = np.zeros((128, 8, 128), np.float32)
    for r in range(8):
        sel[r, r, :] = 1.0
    c["sel"] = sel.reshape(128, 8 * 128)
    c["ones"] = np.ones((128, 128), np.float32)
    m = np.ones((128, 512), np.float32)
    m[:, ::64] = 0.0
    c["cmask"] = m
    selp = np.zeros((128, 4, 2), np.float32)
    for h in range(4):
        selp[h, h, 0] = 1.0
        selp[4 + h, h, 1] = 1.0
    c["selp"] = selp.reshape(128, 8)
    gm = np.zeros((128, 4), np.float32)
    gm[0:4, 0] = 1.0
    gm[4:8, 1] = 1.0
    gm[4:8, 2] = -1.0
    c["gm"] = gm
    mb = np.zeros((128, 4), np.float32)
    mb[64:, 1] = -30000.0
    mb[:64, 2] = -30000.0
    mb[16:, 3] = -30000.0
    c["mb"] = mb
    off = {}
    cols = 0
    for k, v in c.items():
        off[k] = (cols, v.shape[1])
        cols += v.shape[1]
    arr = np.concatenate([c[k] for k in c], axis=1)
    return arr, off


_CARR, _COFF = _consts()

W_SPECS = [
    ("norm_mix", (2, 1024)), ("norm_ffn", (2, 1024)), ("norm_final", (1024,)), ("w_in", (1, 1024, IN_COLS)),
    ("s5_lam_re", (1, 32, 64)), ("s5_lam_im", (1, 32, 64)), ("s5_log_dt", (1, 32)),
    ("s5_b_re", (1, 32, 64, 16)), ("s5_b_im", (1, 32, 64, 16)), ("s5_c_re", (1, 32, 16, 64)), ("s5_c_im", (1, 32, 16, 64)),
    ("s5_d", (1, 512)), ("s5_w_glu", (1, 512, 512)), ("s5_b_glu", (1, 512)),
    ("gdn_conv_w", (1, 4, 1536)), ("gdn_a_log", (1, 4)), ("gdn_dt_bias", (1, 4)), ("gdn_norm_w", (1, 128)),
    ("w_out_ab", (1, 1024, 1024)), ("swa_wq", (1, 1024, 1024)), ("swa_wk", (1, 1024, 256)), ("swa_wv", (1, 1024, 256)),
    ("swa_sinks", (1, 16)), ("swa_wo", (1, 1024, 1024)),
    ("ffn_w_gate", (2, 1024, DFF)), ("ffn_w_up", (2, 1024, DFF)), ("ffn_w_down", (2, DFF, 1024)),
]
IN_SPECS = [
    ("xp", (SEQ, DM)), ("xs", (2, DEC_SEQ, DM)),
    ("st_s5_re", (2, 32, 64)), ("st_s5_im", (2, 32, 64)), ("st_gdn", (2, 4, 128, 128)), ("st_conv", (2, 3, 1536)),
    ("st_k", (2, 128, 256)), ("st_v", (2, 128, 256)),
    ("consts", _CARR.shape),
]
OUT_SPECS = [
    ("y_p", (SEQ, DM)), ("y_s", (2, DEC_SEQ, DM)),
    ("p_s5_re", (32, 64)), ("p_s5_im", (32, 64)), ("p_gdn", (4, 128, 128)), ("p_conv", (3, 1536)),
    ("p_k", (128, 256)), ("p_v", (128, 256)),
    ("s_s5_re", (2, 32, 64)), ("s_s5_im", (2, 32, 64)), ("s_gdn", (2, 4, 128, 128)), ("s_conv", (2, 3, 1536)),
    ("s_k", (2, 128, 256)), ("s_v", (2, 128, 256)),
]


class Builder:
    def __init__(self, stages=99):
        self.stages = stages
        self.nc = bass.Bass("TRN2", target_bir_lowering=False)
        nc = self.nc
        self.D = {}
        for n, s in IN_SPECS + W_SPECS:
            self.D[n] = nc.dram_tensor(n, list(s), F32, kind="ExternalInput").ap()
        for n, s in OUT_SPECS:
            self.D[n] = nc.dram_tensor(n, list(s), F32, kind="ExternalOutput").ap()
        self.es = contextlib.ExitStack()
        self.P = Prog(nc)
        self.P.groups.update({"hT": ["hT_q", "hT_d"], "ys": ["ys_0", "ys_1", "ys_2", "ys_3"], "ub": ["ub_0", "ub_1", "ub_23"],
                              "xres": ["xres0", "xres1", "xres2", "xres3"], "hidA": [f"hid{i}" for i in range(22)], "S32": ["S32p0", "S32p1"], "Sb": ["Sbp0", "Sbp1"], "tok": ["tok0", "tok1"], "egl": ["egl0", "egl1"]})
        self.nslot = 4
        self.slot_i = 0
        self.bank_i = 0

    def sb(self, name, shape, dt):
        return self.es.enter_context(self.nc.sbuf_tensor(name, list(shape), dt))

    def psum(self, name, shape, dt):
        return self.es.enter_context(self.nc.psum_tensor(name, list(shape), dt))

    def alloc(self):
        sb = self.sb
        self.cst = sb("cst", [128, _CARR.shape[1]], F32)
        self.identb = sb("identb", [128, 128], BF16)
        self.xres = sb("xres", [128, 4, DM], F32)
        self.xn = sb("xn", [128, DM], BF16)
        self.ss = sb("ss", [128, 8], F32)
        self.hT = sb("hT", [128, 8, TT], BF16)
        self.nw = sb("nw", [128, 5, 8], F32)
        self.ub = sb("ub", [128, 4, TT], BF16)
        self.qkvb = sb("qkvb", [128, 12, 3 + TT], BF16)
        self.cv32 = sb("cv32", [128, 12, 3], F32)
        self.zs = sb("zs", [128, 4, TT], BF16)
        self.ba = sb("ba", [8, TT], F32)
        self.mixT = sb("mixT", [128, 8, TT], BF16)
        self.hid = sb("hid", [128, 22, TT], BF16)
        self.sg = sb("sg", [128, TT], F32)
        self.slots = [sb(f"wslot{i}", [128, 4096], BF16) for i in range(self.nslot)]
        self.ps = [self.psum(f"ps{i}", [128, 512], F32) for i in range(7)]
        self.pT = self.psum("pT", [128, 1024], BF16)

    def cview(self, name, rows=128):
        o, n = _COFF[name]
        return self.cst[0:rows, o:o + n]

    def wload(self, src3):
        i = self.slot_i
        self.slot_i = (i + 1) % self.nslot
        kc, n = src3.shape[1], src3.shape[2]
        assert kc * n <= 4096
        view = self.slots[i][:, 0:kc * n].rearrange("p (k n) -> p k n", n=n)
        key = f"wslot{i}"
        self.P.add("pool", lambda e: e.dma_start(out=view, in_=src3), w=[key], dma=key)
        return view, key

    def bank(self, lo=0, hi=4):
        b = lo + self.bank_i % (hi - lo)
        self.bank_i += 1
        return b

    def setup(self):
        P, D = self.P, self.D
        P.add("sp", lambda e: e.dma_start(out=self.cst[:], in_=D["consts"][:, :]), w=["cst"], dma="init")
        idv = self.cview("ident")
        P.add("dve", lambda e: e.tensor_copy(out=self.identb[:], in_=idv), r=["cst"], w=["identb"])
        srcs = [D["norm_mix"][0], D["norm_ffn"][0], D["norm_mix"][1], D["norm_ffn"][1]]
        for i, s in enumerate(srcs):
            P.add("sp", lambda e, i=i, s=s: e.dma_start(out=self.nw[:, i, :], in_=s.rearrange("(c p) -> p c", p=128),
                                                        allow_slow_non_contiguous=True), w=["nw"], dma="init")

    def norm_T(self, nblk, widx):
        P = self.P
        P.add("dve", lambda e: e.memset(self.ss[:], 0.0), w=["ss"])
        for b in range(nblk):
            P.add("act", lambda e, b=b: e.activation(out=self.xn[:], in_=self.xres[:, b, :], func=AF.Square,
                                                     accum_out=self.ss[:, b:b + 1]), r=[f"xres{b}"], w=["xn", "ss"])
        P.add("act", lambda e: e.activation(out=self.ss[:, 4:4 + nblk], in_=self.ss[:, 0:nblk], func=AF.Sqrt, scale=1.0 / DM, bias=EPS), r=["ss"], w=["ss"])
        P.add("dve", lambda e: e.reciprocal(out=self.ss[:, 4:4 + nblk], in_=self.ss[:, 4:4 + nblk]), r=["ss"], w=["ss"])
        for b in range(nblk):
            P.add("act", lambda e, b=b: e.activation(out=self.xn[:], in_=self.xres[:, b, :], func=AF.Copy,
                                                     scale=self.ss[:, 4 + b:5 + b]), r=[f"xres{b}", "ss"], w=["xn"])
            for c in range(8):
                P.add("pe", lambda e, c=c: e.transpose(out=self.pT[:, c * 128:(c + 1) * 128], in_=self.xn[:, c * 128:(c + 1) * 128],
                                                       identity=self.identb[:]), r=["xn", "identb"], w=["pT"])
            P.add("dve", lambda e, b=b: e.tensor_tensor(
                out=self.hT[:, :, b * 128:(b + 1) * 128], in0=self.pT[:].rearrange("p (c t) -> p c t", t=128),
                in1=self.nw[:, widx, :].unsqueeze(2).to_broadcast([128, 8, 128]), op=ALU.mult), r=["pT", "nw"], w=["hT"])

    def linear_fm(self, W2, col0, ncols, rhs_fn, rkeys, ntok, evac, piece=512, banks=(0, 4)):
        P = self.P
        K = W2.shape[0]
        kc = K // 128
        W3 = W2.rearrange("(c p) n -> p c n", p=128)
        for p0 in range(col0, col0 + ncols, piece):
            pc = min(piece, col0 + ncols - p0)
            view, key = self.wload(W3[:, :, p0:p0 + pc])
            for c0 in range(0, pc, 128):
                m = min(128, pc - c0)
                b = self.bank(*banks)
                for k in range(kc):
                    P.add("pe", lambda e, b=b, k=k, c0=c0, m=m, view=view: e.matmul(
                        self.ps[b][0:m, 0:ntok], lhsT=view[:, k, c0:c0 + m], rhs=rhs_fn(k), start=(k == 0), stop=(k == kc - 1)),
                        r=[key] + rkeys, w=[f"ps{b}"])
                evac((p0 + c0) // 128, m, b)

    def linear_tm_res(self, W2, act_fn, akeys, nblk):
        P = self.P
        K = W2.shape[0]
        kc = K // 128
        W3 = W2.rearrange("(c p) n -> p c n", p=128)
        sets = [[0, 1, 2, 3], [4, 5, 6, 3]]
        for ch in range(2):
            banks = sets[ch]
            for k0 in range(0, kc, 8):
                k1 = min(kc, k0 + 8)
                view, key = self.wload(W3[:, k0:k1, ch * 512:(ch + 1) * 512])
                for b in range(nblk):
                    for k in range(k0, k1):
                        P.add("pe", lambda e, b=b, k=k, k0=k0, view=view, banks=banks: e.matmul(
                            self.ps[banks[b]][:, :], lhsT=act_fn(k, b), rhs=view[:, k - k0, :], start=(k == 0), stop=(k == kc - 1)),
                            r=[key] + akeys, w=[f"ps{banks[b]}"])
            for b in range(nblk):
                P.add("dve", lambda e, b=b, ch=ch, banks=banks: e.tensor_tensor(
                    out=self.xres[:, b, ch * 512:(ch + 1) * 512], in0=self.xres[:, b, ch * 512:(ch + 1) * 512],
                    in1=self.ps[banks[b]][:, :], op=ALU.add), r=[f"xres{b}", f"ps{banks[b]}"], w=[f"xres{b}"])

    def ffn(self, layer, nblk, ntok):
        P, D = self.P, self.D
        self.norm_T(nblk, 1 + 2 * layer)
        Wg, Wu, Wd = D["ffn_w_gate"][layer], D["ffn_w_up"][layer], D["ffn_w_down"][layer]
        Wg3 = Wg.rearrange("(c p) n -> p c n", p=128)
        Wu3 = Wu.rearrange("(c p) n -> p c n", p=128)
        for p0 in range(0, DFF, 512):
            pc = min(512, DFF - p0)
            vg, kg = self.wload(Wg3[:, :, p0:p0 + pc])
            vu, ku = self.wload(Wu3[:, :, p0:p0 + pc])
            for c0 in range(0, pc, 128):
                f = (p0 + c0) // 128
                bg = self.bank(0, 6)
                bu = self.bank(0, 6)
                for k in range(8):
                    P.add("pe", lambda e, k=k, c0=c0, bg=bg, vg=vg: e.matmul(
                        self.ps[bg][:, 0:ntok], lhsT=vg[:, k, c0:c0 + 128], rhs=self.hT[:, k, 0:ntok], start=(k == 0), stop=(k == 7)),
                        r=[kg, "hT"], w=[f"ps{bg}"])
                for k in range(8):
                    P.add("pe", lambda e, k=k, c0=c0, bu=bu, vu=vu: e.matmul(
                        self.ps[bu][:, 0:ntok], lhsT=vu[:, k, c0:c0 + 128], rhs=self.hT[:, k, 0:ntok], start=(k == 0), stop=(k == 7)),
                        r=[ku, "hT"], w=[f"ps{bu}"])
                P.add("act", lambda e, bg=bg: e.activation(out=self.sg[:, 0:ntok], in_=self.ps[bg][:, 0:ntok], func=AF.Silu),
                      r=[f"ps{bg}"], w=["sg"])
                P.add("dve", lambda e, bu=bu, f=f: e.tensor_tensor(out=self.hid[:, f, 0:ntok], in0=self.sg[:, 0:ntok],
                                                                    in1=self.ps[bu][:, 0:ntok], op=ALU.mult),
                      r=["sg", f"ps{bu}"], w=[f"hid{f}"])
        hk = [f"hid{f}" for f in range(22)]
        self.linear_tm_res(Wd, lambda k, b: self.hid[:, k, b * 128:(b + 1) * 128], hk, nblk)

    def alloc_l0(self):
        sb = self.sb
        self.s5p = sb("s5p", [128, 16, 24], F32)
        self.Bt = sb("Bt", [128, 2, 16, 128], BF16)
        self.Ct = sb("Ct", [128, 2, 16, 32], BF16)
        self.rot = sb("rot", [128, 2, 16, 64], F32)
        self.Bt1 = sb("Bt1", [128, 2, 16, 128], BF16)
        self.Ct1 = sb("Ct1", [128, 2, 16, 32], BF16)
        self.K0T = sb("K0T", [128, 4, 128], BF16)
        self.xst = sb("xst", [128, 2, 16], F32)
        self.s5tt = sb("s5tt", [128, 6, 512], F32)
        self.s5t = [self.s5tt[:, i, :] for i in range(6)]
        self.g5tt = sb("g5tt", [128, 6, 512], F32)
        self.g5t = [self.g5tt[:, i, :] for i in range(6)]
        self.nfin = self.s5tt[:, 0:2, :].rearrange("p a t -> p (a t)")
        self.xrb = sb("xrb", [128, 2, 2, 512], BF16)
        self.ys = sb("ys", [128, 4, TT], F32)
        self.Bz = self.xres[:].rearrange("p b d -> p (b d)").rearrange("p (i g c) -> p i g c", i=2, g=16)
        self.Cz = self.ys[:].rearrange("p a t -> p (a t)")[:, 0:1024].rearrange("p (i g c) -> p i g c", i=2, g=16)
        self.dcol = sb("dcol", [128, 4], F32)
        self.bglu = sb("bglu", [128, 4], F32)
        self.Braw = self.sg[:].rearrange("p (i g c) -> p i g c", i=2, g=16)
        self.zb = self.ub

    def s5_setup(self):
        P, D = self.P, self.D
        p = self.s5p
        PI = float(np.pi)
        for nm, col in (("s5_lam_re", 0), ("s5_lam_im", 1)):
            P.add("sp", lambda e, nm=nm, col=col: e.dma_start(
                out=p[:, :, col], in_=D[nm][0].rearrange("(gp gl) p -> (gl p) gp", gl=2), allow_slow_non_contiguous=True),
                w=["s5p"], dma="init")
        ld = D["s5_log_dt"][0].rearrange("(gp gl) -> gl gp", gl=2)
        for gl in range(2):
            P.add("sp", lambda e, gl=gl: e.dma_start(out=p[gl * 64:(gl + 1) * 64, :, 2], in_=ld[gl].partition_broadcast(64),
                                                    allow_slow_non_contiguous=True), w=["s5p"], dma="init")

        def c(i):
            return p[:, :, i]
        k = ["s5p"]
        P.add("act", lambda e: e.activation(out=c(3), in_=c(2), func=AF.Exp), r=k, w=k)
        P.add("dve", lambda e: e.tensor_tensor(out=c(4), in0=c(0), in1=c(3), op=ALU.mult), r=k, w=k)
        P.add("dve", lambda e: e.tensor_tensor(out=c(5), in0=c(1), in1=c(3), op=ALU.mult), r=k, w=k)
        P.add("act", lambda e: e.activation(out=c(6), in_=c(4), func=AF.Exp), r=k, w=k)
        P.add("act", lambda e: e.activation(out=c(7), in_=c(5), func=AF.Sin, scale=1.0 / 16), r=k, w=k)
        P.add("act", lambda e: e.activation(out=c(9), in_=c(5), func=AF.Sin, scale=1.0 / 8), r=k, w=k)
        P.add("dve", lambda e: e.tensor_tensor(out=c(8), in0=c(7), in1=c(7), op=ALU.mult), r=k, w=k)
        P.add("dve", lambda e: e.tensor_scalar(out=c(10), in0=c(8), scalar1=-2.0, scalar2=1.0, op0=ALU.mult, op1=ALU.add), r=k, w=k)
        for _ in range(3):
            P.add("dve", lambda e: e.tensor_tensor(out=c(15), in0=c(10), in1=c(10), op=ALU.mult), r=k, w=k)
            P.add("dve", lambda e: e.tensor_tensor(out=c(16), in0=c(9), in1=c(9), op=ALU.mult), r=k, w=k)
            P.add("dve", lambda e: e.tensor_tensor(out=c(8), in0=c(9), in1=c(10), op=ALU.mult), r=k, w=k)
            P.add("dve", lambda e: e.tensor_scalar(out=c(9), in0=c(8), scalar1=2.0, scalar2=None, op0=ALU.mult), r=k, w=k)
            P.add("dve", lambda e: e.tensor_tensor(out=c(10), in0=c(15), in1=c(16), op=ALU.subtract), r=k, w=k)
        P.add("dve", lambda e: e.tensor_tensor(out=c(11), in0=c(6), in1=c(10), op=ALU.mult), r=k, w=k)
        P.add("dve", lambda e: e.tensor_tensor(out=c(12), in0=c(6), in1=c(9), op=ALU.mult), r=k, w=k)
        P.add("dve", lambda e: e.tensor_scalar(out=c(13), in0=c(11), scalar1=-1.0, scalar2=None, op0=ALU.add), r=k, w=k)
        P.add("dve", lambda e: e.tensor_tensor(out=c(14), in0=c(0), in1=c(0), op=ALU.mult), r=k, w=k)
        P.add("dve", lambda e: e.tensor_tensor(out=c(15), in0=c(1), in1=c(1), op=ALU.mult), r=k, w=k)
        P.add("dve", lambda e: e.tensor_tensor(out=c(14), in0=c(14), in1=c(15), op=ALU.add), r=k, w=k)
        P.add("dve", lambda e: e.reciprocal(out=c(14), in_=c(14)), r=k, w=k)
        P.add("dve", lambda e: e.tensor_tensor(out=c(15), in0=c(13), in1=c(0), op=ALU.mult), r=k, w=k)
        P.add("dve", lambda e: e.tensor_tensor(out=c(16), in0=c(12), in1=c(1), op=ALU.mult), r=k, w=k)
        P.add("dve", lambda e: e.tensor_tensor(out=c(15), in0=c(15), in1=c(16), op=ALU.add), r=k, w=k)
        P.add("dve", lambda e: e.tensor_tensor(out=c(17), in0=c(15), in1=c(14), op=ALU.mult), r=k, w=k)
        P.add("dve", lambda e: e.tensor_tensor(out=c(15), in0=c(12), in1=c(0), op=ALU.mult), r=k, w=k)
        P.add("dve", lambda e: e.tensor_tensor(out=c(16), in0=c(13), in1=c(1), op=ALU.mult), r=k, w=k)
        P.add("dve", lambda e: e.tensor_tensor(out=c(15), in0=c(15), in1=c(16), op=ALU.subtract), r=k, w=k)
        P.add("dve", lambda e: e.tensor_tensor(out=c(18), in0=c(15), in1=c(14), op=ALU.mult), r=k, w=k)
        if _S5STOP == 1:
            return
        for i, nm in enumerate(("s5_b_re", "s5_b_im")):
            P.add("sp", lambda e, i=i, nm=nm: e.dma_start(out=self.Braw[:, i, :, :],
                                                          in_=D[nm][0].rearrange("(gp gl) p c -> (gl p) gp c", gl=2)),
                  w=["sg"], dma="init")
        P.add("dve", lambda e: e.memset(self.Bz[:], 0.0), w=["xres"])
        P.add("dve", lambda e: e.memset(self.Cz[:], 0.0), w=["ys"])
        fre = p[:, :, 17:18].to_broadcast([128, 16, 16])
        fim = p[:, :, 18:19].to_broadcast([128, 16, 16])
        t0 = self.s5t[0][:, 0:256].rearrange("p (g c) -> p g c", c=16)
        t1 = self.s5t[1][:, 0:256].rearrange("p (g c) -> p g c", c=16)
        t2 = self.s5t[2][:, 0:256].rearrange("p (g c) -> p g c", c=16)
        bb = [self.s5t[3][:, 0:256].rearrange("p (g c) -> p g c", c=16), self.s5t[4][:, 0:256].rearrange("p (g c) -> p g c", c=16)]
        kk = ["s5p", "sg"]
        P.add("dve", lambda e: e.tensor_tensor(out=t0, in0=self.Braw[:, 0], in1=fre, op=ALU.mult), r=kk, w=["s5t0"])
        P.add("dve", lambda e: e.tensor_tensor(out=t1, in0=self.Braw[:, 1], in1=fim, op=ALU.mult), r=kk, w=["s5t1"])
        P.add("dve", lambda e: e.tensor_tensor(out=bb[0], in0=t0, in1=t1, op=ALU.subtract), r=["s5t0", "s5t1"], w=["s5t3"])
        for gl in range(2):
            for r in range(4):
                P.add("dve", lambda e, gl=gl, r=r: e.tensor_copy(
                    out=self.Bz[gl * 64:(gl + 1) * 64, 0].rearrange("p (cb r) c -> p cb r c", r=4)[:, :, r, 32 * r + gl * 16:32 * r + gl * 16 + 16],
                    in_=bb[0][gl * 64:(gl + 1) * 64].rearrange("p (cb r) c -> p cb r c", r=4)[:, :, r, :]), r=["s5t3"], w=["xres"])
        P.add("dve", lambda e: e.tensor_tensor(out=t0, in0=self.Braw[:, 1], in1=fre, op=ALU.mult), r=kk, w=["s5t0"])
        P.add("dve", lambda e: e.tensor_tensor(out=t1, in0=self.Braw[:, 0], in1=fim, op=ALU.mult), r=kk, w=["s5t1"])
        P.add("dve", lambda e: e.tensor_tensor(out=bb[1], in0=t0, in1=t1, op=ALU.add), r=["s5t0", "s5t1"], w=["s5t4"])
        for gl in range(2):
            for r in range(4):
                P.add("dve", lambda e, gl=gl, r=r: e.tensor_copy(
                    out=self.Bz[gl * 64:(gl + 1) * 64, 1].rearrange("p (cb r) c -> p cb r c", r=4)[:, :, r, 32 * r + gl * 16:32 * r + gl * 16 + 16],
                    in_=bb[1][gl * 64:(gl + 1) * 64].rearrange("p (cb r) c -> p cb r c", r=4)[:, :, r, :]), r=["s5t4"], w=["xres"])
        if _S5STOP == 2:
            return
        idf = self.cview("ident")
        for i in range(2):
            for gp in range(16):
                P.add("pe", lambda e, i=i, gp=gp: e.matmul(self.ps[0][:, 0:128], lhsT=self.Bz[:, i, gp, :], rhs=idf, start=True, stop=True),
                      r=["xres", "cst"], w=["ps0"])
                P.add("dve", lambda e, i=i, gp=gp: e.tensor_copy(out=self.Bt[:, i, gp, :], in_=self.ps[0][:, 0:128]), r=["ps0"], w=["Bt"])
        if _S5STOP == 3:
            return
        for i, nm in enumerate(("s5_c_re", "s5_c_im")):
            for g in range(32):
                gp, gl = g // 2, g % 2
                P.add("sp", lambda e, i=i, nm=nm, g=g, gp=gp, gl=gl: e.dma_start(
                    out=self.Cz[gl * 64:(gl + 1) * 64, i, gp, gl * 16:(gl + 1) * 16], in_=D[nm][0][g].rearrange("c p -> p c"),
                    allow_slow_non_contiguous=True), w=["ys"], dma="init")
        P.add("dve", lambda e: e.tensor_copy(out=self.Ct[:, 0], in_=self.Cz[:, 0]), r=["ys"], w=["Ct"])
        P.add("dve", lambda e: e.tensor_scalar(out=self.Ct[:, 1], in0=self.Cz[:, 1], scalar1=-1.0, scalar2=None, op0=ALU.mult), r=["ys"], w=["Ct"])
        if _S5STOP == 4:
            return
        scrF = self.hid[:].rearrange("p a b -> p (a b)").bitcast(F32)
        rotfull = scrF[:, 0:4096].rearrange("p (i g t) -> p i g t", i=2, g=16)
        cs, sn = rotfull[:, 0], rotfull[:, 1]
        P.add("dve", lambda e: e.tensor_copy(out=cs[:, :, 0:1], in_=p[:, :, 10:11]), r=k, w=["hidA"])
        P.add("dve", lambda e: e.tensor_copy(out=sn[:, :, 0:1], in_=p[:, :, 9:10]), r=k, w=["hidA"])
        L = 1
        while L < 128:
            c1 = cs[:, :, L - 1:L].to_broadcast([128, 16, L])
            s1 = sn[:, :, L - 1:L].to_broadcast([128, 16, L])
            sc0 = self.g5tt[:, 0:4, :].rearrange("p a (g t) -> p (a g) t", g=4)[:, :, 0:L]
            sc1 = self.g5tt[:, 4:6, :].rearrange("p a (g t) -> p (a g) t", g=8)[:, :, 0:L]
            P.add("dve", lambda e, L=L, c1=c1, sc0=sc0: e.tensor_tensor(out=sc0, in0=cs[:, :, 0:L], in1=c1, op=ALU.mult), r=["hidA"], w=["g5t0"])
            P.add("dve", lambda e, L=L, s1=s1, sc1=sc1: e.tensor_tensor(out=sc1, in0=sn[:, :, 0:L], in1=s1, op=ALU.mult), r=["hidA"], w=["g5t4", "g5t5"])
            P.add("dve", lambda e, L=L, sc0=sc0, sc1=sc1: e.tensor_tensor(out=cs[:, :, L:2 * L], in0=sc0, in1=sc1, op=ALU.subtract),
                  r=["g5t0", "g5t4", "g5t5", "hidA"], w=["hidA"])
            P.add("dve", lambda e, L=L, s1=s1, sc0=sc0: e.tensor_tensor(out=sc0, in0=cs[:, :, 0:L], in1=s1, op=ALU.mult), r=["hidA"], w=["g5t0"])
            P.add("dve", lambda e, L=L, c1=c1, sc1=sc1: e.tensor_tensor(out=sc1, in0=sn[:, :, 0:L], in1=c1, op=ALU.mult), r=["hidA"], w=["g5t4", "g5t5"])
            P.add("dve", lambda e, L=L, sc0=sc0, sc1=sc1: e.tensor_tensor(out=sn[:, :, L:2 * L], in0=sc0, in1=sc1, op=ALU.add),
                  r=["g5t0", "g5t4", "g5t5", "hidA"], w=["hidA"])
            L *= 2
        for i in range(2):
            P.add("dve", lambda e, i=i: e.tensor_copy(out=self.rot[:, i], in_=rotfull[:, i].rearrange("p g (n two) -> p g n two", two=2)[:, :, :, 1]),
                  r=["hidA"], w=["rot"])
        P.add("dve", lambda e: e.tensor_tensor(out=c(22), in0=c(6), in1=c(6), op=ALU.mult), r=k, w=k)
        Czp = scrF[:, 0:4096].rearrange("p (i g c) -> p i g c", i=2, g=16)
        P.add("dve", lambda e: e.memset(Czp, 0.0), w=["hidA"])
        for i in range(2):
            for gl in range(2):
                for r in range(4):
                    P.add("dve", lambda e, i=i, gl=gl, r=r: e.tensor_scalar(
                        out=Czp[gl * 64:(gl + 1) * 64, i].rearrange("p (cb r) c -> p cb r c", r=4)[:, :, r, 32 * r + gl * 16:32 * r + gl * 16 + 16],
                        in0=self.Cz[gl * 64:(gl + 1) * 64, i].rearrange("p (cb r) c -> p cb r c", r=4)[:, :, r, gl * 16:gl * 16 + 16],
                        scalar1=(1.0 if i == 0 else -1.0), scalar2=None, op0=ALU.mult), r=["ys"], w=["hidA"])
        for cb in range(4):
            n = 0
            for r in range(4):
                for i in range(2):
                    P.add("pe", lambda e, cb=cb, r=r, i=i, n=n: e.matmul(self.ps[1][:, 0:128], lhsT=self.Bz[:, i, 4 * cb + r, :], rhs=Czp[:, i, 4 * cb + r, :],
                                                                       start=(n == 0), stop=(n == 7)), r=["xres", "hidA"], w=["ps1"])
                    n += 1
            P.add("dve", lambda e, cb=cb: e.tensor_copy(out=self.K0T[:, cb, :], in_=self.ps[1][:, 0:128]), r=["ps1"], w=["K0T"])
        lre = p[:, :, 11:12].to_broadcast([128, 16, 32])
        lim = p[:, :, 12:13].to_broadcast([128, 16, 32])
        u0 = self.s5t[0][:, 0:512].rearrange("p (g c) -> p g c", c=32)
        u1 = self.s5t[1][:, 0:512].rearrange("p (g c) -> p g c", c=32)
        P.add("dve", lambda e: e.tensor_tensor(out=u0, in0=self.Cz[:, 0], in1=lre, op=ALU.mult), r=["ys", "s5p"], w=["s5t0"])
        P.add("dve", lambda e: e.tensor_tensor(out=u1, in0=self.Cz[:, 1], in1=lim, op=ALU.mult), r=["ys", "s5p"], w=["s5t1"])
        P.add("dve", lambda e: e.tensor_tensor(out=self.Ct1[:, 0], in0=u0, in1=u1, op=ALU.subtract), r=["s5t0", "s5t1"], w=["Ct1"])
        P.add("dve", lambda e: e.tensor_tensor(out=u0, in0=self.Cz[:, 0], in1=lim, op=ALU.mult), r=["ys", "s5p"], w=["s5t0"])
        P.add("dve", lambda e: e.tensor_tensor(out=u1, in0=self.Cz[:, 1], in1=lre, op=ALU.mult), r=["ys", "s5p"], w=["s5t1"])
        P.add("dve", lambda e: e.tensor_tensor(out=u0, in0=u0, in1=u1, op=ALU.add), r=["s5t0", "s5t1"], w=["s5t0"])
        P.add("dve", lambda e: e.tensor_scalar(out=self.Ct1[:, 1], in0=u0, scalar1=-1.0, scalar2=None, op0=ALU.mult), r=["s5t0"], w=["Ct1"])
        lre16 = p[:, :, 11:12].to_broadcast([128, 16, 16])
        lim16 = p[:, :, 12:13].to_broadcast([128, 16, 16])
        t0 = self.s5t[0][:, 0:256].rearrange("p (g c) -> p g c", c=16)
        t1 = self.s5t[1][:, 0:256].rearrange("p (g c) -> p g c", c=16)
        t2 = self.s5t[2][:, 0:256].rearrange("p (g c) -> p g c", c=16)
        P.add("dve", lambda e: e.memset(self.Bz[:], 0.0), w=["xres"])
        for i in range(2):
            a_, b_ = (bb[0], bb[1]) if i == 0 else (bb[1], bb[0])
            P.add("dve", lambda e, a_=a_: e.tensor_tensor(out=t0, in0=a_, in1=lre16, op=ALU.mult), r=["s5t3", "s5t4", "s5p"], w=["s5t0"])
            P.add("dve", lambda e, b_=b_: e.tensor_tensor(out=t1, in0=b_, in1=lim16, op=ALU.mult), r=["s5t3", "s5t4", "s5p"], w=["s5t1"])
            P.add("dve", lambda e, i=i: e.tensor_tensor(out=t2, in0=t0, in1=t1, op=(ALU.subtract if i == 0 else ALU.add)), r=["s5t0", "s5t1"], w=["s5t2"])
            for gl in range(2):
                for r in range(4):
                    P.add("dve", lambda e, i=i, gl=gl, r=r: e.tensor_copy(
                        out=self.Bz[gl * 64:(gl + 1) * 64, i].rearrange("p (cb r) c -> p cb r c", r=4)[:, :, r, 32 * r + gl * 16:32 * r + gl * 16 + 16],
                        in_=t2[gl * 64:(gl + 1) * 64].rearrange("p (cb r) c -> p cb r c", r=4)[:, :, r, :]), r=["s5t2"], w=["xres"])
        for i in range(2):
            for gp in range(16):
                P.add("pe", lambda e, i=i, gp=gp: e.matmul(self.ps[0][:, 0:128], lhsT=self.Bz[:, i, gp, :], rhs=idf, start=True, stop=True),
                      r=["xres", "cst"], w=["ps0"])
                P.add("dve", lambda e, i=i, gp=gp: e.tensor_copy(out=self.Bt1[:, i, gp, :], in_=self.ps[0][:, 0:128]), r=["ps0"], w=["Bt1"])
        if _S5STOP == 5:
            return
        P.add("sp", lambda e: e.dma_start(out=self.dcol[:], in_=D["s5_d"][0].rearrange("(c p) -> p c", p=128), allow_slow_non_contiguous=True),
              w=["dcol"], dma="init")
        P.add("sp", lambda e: e.dma_start(out=self.bglu[:], in_=D["s5_b_glu"][0].rearrange("(c p) -> p c", p=128), allow_slow_non_contiguous=True),
              w=["bglu"], dma="init")

    def s5_iter(self, sub, cb, st, ntok, nval):
        P = self.P
        TS, NS = 128, 64
        t0 = sub * TS
        T = self.s5t if st == 0 else self.g5t
        tk = [("s5t" if st == 0 else "g5t") + str(i) for i in range(6)]
        bR, bI, bY = (0, 1, 4) if st == 0 else (2, 3, 5)
        kx = f"xrb{st}"
        xv = [self.xrb[:, st, i, 0:4 * (NS + 1)].rearrange("p (g n) -> p g n", n=NS + 1) for i in range(2)]
        cs = self.rot[:, 0, 4 * cb:4 * cb + 4, :].rearrange("p g t -> p (g t)")
        sn = self.rot[:, 1, 4 * cb:4 * cb + 4, :].rearrange("p g t -> p (g t)")
        u2 = self.ub[:, cb, t0:t0 + TS].rearrange("p (n two) -> p n two", two=2)
        ue, uo = u2[:, :, 0], u2[:, :, 1]
        Wd = 4 * NS
        for i in range(2):
            P.add("act", lambda e, i=i: e.activation(out=xv[i][:, :, 0:1], in_=self.xst[:, i, 4 * cb:4 * cb + 4].unsqueeze(2), func=AF.Copy),
                  r=["xst"], w=[kx])
        for i, bk in ((0, bR), (1, bI)):
            for r in range(4):
                gp = 4 * cb + r
                P.add("pe", lambda e, i=i, bk=bk, r=r, gp=gp: e.matmul(self.ps[bk][:, r * NS:(r + 1) * NS], lhsT=self.Bt1[:, i, gp, :], rhs=ue,
                                                                       start=True, stop=False), r=["Bt1", "ub"], w=[f"ps{bk}"])
                P.add("pe", lambda e, i=i, bk=bk, r=r, gp=gp: e.matmul(self.ps[bk][:, r * NS:(r + 1) * NS], lhsT=self.Bt[:, i, gp, :], rhs=uo,
                                                                       start=False, stop=True), r=["Bt", "ub"], w=[f"ps{bk}"])
        yield
        pR, pI = self.ps[bR][:, 0:Wd], self.ps[bI][:, 0:Wd]
        kR, kI = f"ps{bR}", f"ps{bI}"
        A = [t[:, 0:Wd] for t in T]
        P.add("dve", lambda e: e.tensor_tensor(out=A[0], in0=pR, in1=cs, op=ALU.mult), r=[kR, "rot"], w=[tk[0]])
        yield
        P.add("dve", lambda e: e.tensor_tensor(out=A[1], in0=pI, in1=sn, op=ALU.mult), r=[kI, "rot"], w=[tk[1]])
        yield
        P.add("dve", lambda e: e.tensor_tensor(out=A[2], in0=pI, in1=cs, op=ALU.mult), r=[kI, "rot"], w=[tk[2]])
        yield
        P.add("dve", lambda e: e.tensor_tensor(out=A[3], in0=pR, in1=sn, op=ALU.mult), r=[kR, "rot"], w=[tk[3]])
        yield
        P.add("dve", lambda e: e.tensor_tensor(out=A[0], in0=A[0], in1=A[1], op=ALU.add), r=[tk[0], tk[1]], w=[tk[0]])
        yield
        P.add("dve", lambda e: e.tensor_tensor(out=A[2], in0=A[2], in1=A[3], op=ALU.subtract), r=[tk[2], tk[3]], w=[tk[2]])
        yield
        for r in range(4):
            gp = 4 * cb + r
            rb = self.s5p[:, gp, 22:23].to_broadcast([128, NS])
            sl = slice(r * NS, (r + 1) * NS)
            P.add("dve", lambda e, rb=rb, sl=sl, gp=gp: e.tensor_tensor_scan(out=T[4][:, sl], data0=rb, data1=T[0][:, sl], initial=self.xst[:, 0, gp:gp + 1],
                                                                             op0=ALU.mult, op1=ALU.add), r=["s5p", tk[0], "xst"], w=[tk[4]])
            yield
            P.add("dve", lambda e, rb=rb, sl=sl, gp=gp: e.tensor_tensor_scan(out=T[5][:, sl], data0=rb, data1=T[2][:, sl], initial=self.xst[:, 1, gp:gp + 1],
                                                                             op0=ALU.mult, op1=ALU.add), r=["s5p", tk[2], "xst"], w=[tk[5]])
            yield
        P.add("dve", lambda e: e.tensor_tensor(out=A[0], in0=A[4], in1=cs, op=ALU.mult), r=[tk[4], "rot"], w=[tk[0]])
        yield
        P.add("dve", lambda e: e.tensor_tensor(out=A[1], in0=A[5], in1=sn, op=ALU.mult), r=[tk[5], "rot"], w=[tk[1]])
        yield
        P.add("dve", lambda e: e.tensor_tensor(out=A[2], in0=A[4], in1=sn, op=ALU.mult), r=[tk[4], "rot"], w=[tk[2]])
        yield
        P.add("dve", lambda e: e.tensor_tensor(out=A[3], in0=A[5], in1=cs, op=ALU.mult), r=[tk[5], "rot"], w=[tk[3]])
        yield
        P.add("dve", lambda e: e.tensor_tensor(out=A[0], in0=A[0], in1=A[1], op=ALU.subtract), r=[tk[0], tk[1]], w=[tk[0]])
        yield
        P.add("dve", lambda e: e.tensor_tensor(out=A[2], in0=A[2], in1=A[3], op=ALU.add), r=[tk[2], tk[3]], w=[tk[2]])
        yield
        for i, tt in ((0, 0), (1, 2)):
            P.add("act", lambda e, i=i, tt=tt: e.activation(out=xv[i][:, :, 1:NS + 1], in_=A[tt].rearrange("p (g n) -> p g n", n=NS), func=AF.Copy),
                  r=[tk[tt]], w=[kx])
        lastn = NS - 1
        if nval < ntok:
            lastn = (nval - 1 - t0 - 1) // 2
        if 0 <= lastn < NS:
            for i, tt in ((0, 0), (1, 2)):
                P.add("dve", lambda e, i=i, tt=tt, lastn=lastn: e.tensor_copy(
                    out=self.xst[:, i, 4 * cb:4 * cb + 4], in_=A[tt].rearrange("p (g n) -> p g n", n=NS)[:, :, lastn]), r=[tk[tt], kx], w=["xst"])
                yield
        pY = self.ps[bY]
        for r in range(4):
            gp = 4 * cb + r
            for i in range(2):
                P.add("pe", lambda e, r=r, gp=gp, i=i: e.matmul(pY[32 * r:32 * r + 32, 0:NS], lhsT=self.Ct[:, i, gp, :], rhs=xv[i][:, r, 1:NS + 1],
                                                                start=(i == 0), stop=(i == 1), tile_position=(0, 32 * r)), r=["Ct", kx], w=[f"ps{bY}"])
        P.add("pe", lambda e: e.matmul(pY[:, NS:2 * NS], lhsT=self.K0T[:, cb, :], rhs=ue, start=True, stop=False), r=["K0T", "ub"], w=[f"ps{bY}"])
        for r in range(4):
            gp = 4 * cb + r
            for i in range(2):
                P.add("pe", lambda e, r=r, gp=gp, i=i: e.matmul(pY[32 * r:32 * r + 32, NS:2 * NS], lhsT=self.Ct1[:, i, gp, :], rhs=xv[i][:, r, 0:NS],
                                                                start=False, stop=(i == 1), tile_position=(0, 32 * r)), r=["Ct1", kx], w=[f"ps{bY}"])
        yield
        y2 = self.ys[:, cb, t0:t0 + TS].rearrange("p (n two) -> p n two", two=2)
        P.add("dve", lambda e: e.scalar_tensor_tensor(out=y2[:, :, 1], in0=uo, scalar=self.dcol[:, cb:cb + 1], in1=pY[:, 0:NS],
                                                      op0=ALU.mult, op1=ALU.add), r=["ub", "dcol", f"ps{bY}"], w=[f"ys_{cb}"])
        yield
        P.add("dve", lambda e: e.scalar_tensor_tensor(out=y2[:, :, 0], in0=ue, scalar=self.dcol[:, cb:cb + 1], in1=pY[:, NS:2 * NS],
                                                      op0=ALU.mult, op1=ALU.add), r=["ub", "dcol", f"ps{bY}"], w=[f"ys_{cb}"])
        yield

    def s5_tile(self, ntok, nval, state_out=None):
        P, D = self.P, self.D
        nsub = ntok // 128
        for sub in range(nsub):
            for cb0 in (0, 2):
                gens = [self.s5_iter(sub, cb0, 0, ntok, nval), self.s5_iter(sub, cb0 + 1, 1, ntok, nval)]
                alive = [True, True]
                while any(alive):
                    for gi in range(2):
                        if alive[gi]:
                            try:
                                next(gens[gi])
                            except StopIteration:
                                alive[gi] = False
        if state_out is not None:
            for i in range(2):
                dst = state_out[i].rearrange("(gp gl) p -> (gl p) gp", gl=2)
                P.add("sp", lambda e, i=i, dst=dst: e.dma_start(out=dst, in_=self.xst[:, i, :], allow_slow_non_contiguous=True),
                      r=["xst"], w=[f"so{i}"], dma="out")
        ys_ = [self.ys[:, cb, 0:ntok] for cb in range(4)]
        ts_ = [self.s5t[cb][:, 0:ntok] for cb in range(4)]
        for cb in range(4):
            P.add("dve", lambda e, cb=cb: e.tensor_tensor(out=ts_[cb], in0=ys_[cb], in1=ys_[cb], op=ALU.mult), r=[f"ys_{cb}"], w=[f"s5t{cb}"])
        for cb in range(4):
            P.add("dve", lambda e, cb=cb: e.tensor_scalar(out=ts_[cb], in0=ts_[cb], scalar1=0.044715, scalar2=1.0, op0=ALU.mult, op1=ALU.add),
                  r=[f"s5t{cb}"], w=[f"s5t{cb}"])
        for cb in range(4):
            P.add("dve", lambda e, cb=cb: e.tensor_tensor(out=ts_[cb], in0=ts_[cb], in1=ys_[cb], op=ALU.mult), r=[f"s5t{cb}", f"ys_{cb}"], w=[f"s5t{cb}"])
        for cb in range(4):
            P.add("act", lambda e, cb=cb: e.activation(out=ts_[cb], in_=ts_[cb], func=AF.Sigmoid, scale=1.5957691216), r=[f"s5t{cb}"], w=[f"s5t{cb}"])
        for cb in range(4):
            P.add("dve", lambda e, cb=cb: e.tensor_tensor(out=ys_[cb], in0=ts_[cb], in1=ys_[cb], op=ALU.mult), r=[f"s5t{cb}", f"ys_{cb}"], w=[f"ys_{cb}"])
        for cb in range(4):
            P.add("act", lambda e, cb=cb: e.activation(out=self.zb[:, cb, 0:ntok], in_=ys_[cb], func=AF.Copy), r=[f"ys_{cb}"], w=["ub"])

        def evac(ci, m, b):
            P.add("act", lambda e: e.activation(out=self.sg[:, 0:ntok], in_=self.ps[b][:, 0:ntok], func=AF.Sigmoid, bias=self.bglu[:, ci:ci + 1]),
                  r=[f"ps{b}", "bglu"], w=["sg"])
            P.add("dve", lambda e: e.tensor_tensor(out=self.mixT[:, ci, 0:ntok], in0=self.sg[:, 0:ntok], in1=self.ys[:, ci, 0:ntok], op=ALU.mult),
                  r=["sg", f"ys_{ci}"], w=["mixT"])
        self.linear_fm(D["s5_w_glu"][0], 0, 512, lambda k: self.zb[:, k, 0:ntok], ["ub"], ntok, evac, banks=(4, 7))

    def proj_in(self, ntok, nval, conv_out=None):
        P, D = self.P, self.D

        def evac(ci, m, b):
            src = self.ps[b][0:m, 0:ntok]
            if ci < 4:
                P.add("act", lambda e: e.activation(out=self.ub[:, ci, 0:ntok], in_=src, func=AF.Copy), r=[f"ps{b}"], w=["ub"])
            elif ci < 16:
                P.add("act", lambda e: e.activation(out=self.qkvb[:, ci - 4, 3:3 + ntok], in_=src, func=AF.Copy), r=[f"ps{b}"], w=["qkvb"])
                if conv_out is not None:
                    P.add("dve", lambda e: e.tensor_copy(out=self.cv32[:, ci - 4, :], in_=self.ps[b][:, nval - 3:nval]), r=[f"ps{b}"], w=["cv32"])
                    P.add("sp", lambda e: e.dma_start(out=conv_out[:, (ci - 4) * 128:(ci - 3) * 128].rearrange("j p -> p j"), in_=self.cv32[:, ci - 4, :],
                                                      allow_slow_non_contiguous=True), r=["cv32"], w=[f"co{ci}"], dma="out")
            elif ci < 20:
                P.add("act", lambda e: e.activation(out=self.zs[:, ci - 16, 0:ntok], in_=src, func=AF.Silu), r=[f"ps{b}"], w=["zs"])
            else:
                P.add("act", lambda e: e.activation(out=self.ba[:, 0:ntok], in_=src, func=AF.Copy), r=[f"ps{b}"], w=["ba"])
        self.linear_fm(D["w_in"][0], 0, IN_COLS, lambda k: self.hT[:, k, 0:ntok], ["hT"], ntok, evac)

    def final_store(self, nblk, dst_fn):
        P = self.P
        P.add("sp", lambda e: e.dma_start(out=self.nfin, in_=self.D["norm_final"].partition_broadcast(128)), w=["s5t0", "s5t1"], dma="ldn")
        P.add("dve", lambda e: e.memset(self.ss[:], 0.0), w=["ss"])
        for b in range(nblk):
            P.add("act", lambda e, b=b: e.activation(out=self.xn[:], in_=self.xres[:, b, :], func=AF.Square,
                                                     accum_out=self.ss[:, b:b + 1]), r=[f"xres{b}"], w=["xn", "ss"])
        P.add("act", lambda e: e.activation(out=self.ss[:, 4:4 + nblk], in_=self.ss[:, 0:nblk], func=AF.Sqrt, scale=1.0 / DM, bias=EPS), r=["ss"], w=["ss"])
        P.add("dve", lambda e: e.reciprocal(out=self.ss[:, 4:4 + nblk], in_=self.ss[:, 4:4 + nblk]), r=["ss"], w=["ss"])
        for b in range(nblk):
            yo = self.yo2[b % 2]
            ky = f"ys_{2 * (b % 2)}"
            ky2 = f"ys_{2 * (b % 2) + 1}"
            P.add("dve", lambda e, b=b, yo=yo: e.scalar_tensor_tensor(out=yo, in0=self.xres[:, b, :], scalar=self.ss[:, 4 + b:5 + b],
                                                                      in1=self.nfin, op0=ALU.mult, op1=ALU.mult),
                  r=[f"xres{b}", "ss", "s5t0", "s5t1"], w=[ky, ky2])
            dst, rows = dst_fn(b)
            P.add("sp", lambda e, dst=dst, rows=rows, yo=yo: e.dma_start(out=dst, in_=yo[0:rows, :]), r=[ky, ky2], w=["ydram"], dma=f"outy{b % 2}")

    def build(self):
        P, D = self.P, self.D
        self.alloc()
        self.alloc_l0()
        self.yo2 = [self.ys[:, 0:2, :].rearrange("p a t -> p (a t)"), self.ys[:, 2:4, :].rearrange("p a t -> p (a t)")]
        self.alloc_gdn()
        self.alloc_swa()
        self.setup()
        if self.stages != -2:
            self.s5_setup()
        self.gdn_setup()
        self.swa_setup()
        seqs = [("p", 0)]
        if self.stages >= 2:
            seqs += [("s", 0), ("s", 1)]
        for kind, si in seqs:
            if kind == "p":
                ntiles, ntok, nval = SEQ // TT, TT, TT
                P.add("dve", lambda e: e.memset(self.xst[:], 0.0), w=["xst"])
                P.add("dve", lambda e: e.memset(self.qkvb[:, :, 0:3], 0.0), w=["qkvb"])
                P.add("dve", lambda e: e.memset(self.S32[:], 0.0), w=["S32"])
                P.add("dve", lambda e: e.memset(self.Sb[:], 0.0), w=["Sb"])
            else:
                ntiles, ntok, nval = 1, 128, DEC_SEQ
                for i, nm in enumerate(("st_s5_re", "st_s5_im")):
                    P.add("sp", lambda e, i=i, nm=nm, si=si: e.dma_start(
                        out=self.xst[:, i, :], in_=D[nm][si].rearrange("(gp gl) p -> (gl p) gp", gl=2), allow_slow_non_contiguous=True),
                        w=["xst"], dma="ldst")
                P.add("sp", lambda e, si=si: e.dma_start(out=self.S32[:], in_=D["st_gdn"][si].rearrange("h k v -> k h v")), w=["S32"], dma="ldst")
                P.add("act", lambda e: e.activation(out=self.Sb[:], in_=self.S32[:], func=AF.Copy), r=["S32"], w=["Sb"])
                if not (_RISK & 8):
                    self.swa_init_sample(si)
                for b in range(12):
                    P.add("sp", lambda e, si=si, b=b: e.dma_start(out=self.cv32[:, b, :], in_=D["st_conv"][si][:, b * 128:(b + 1) * 128].rearrange("j p -> p j"),
                                                                  allow_slow_non_contiguous=True), w=["cv32"], dma="ldst")
                P.add("dve", lambda e: e.tensor_copy(out=self.qkvb[:, :, 0:3], in_=self.cv32[:]), r=["cv32"], w=["qkvb"])
            if self.stages == 0:
                ntiles = 1
            if self.stages < 0:
                ntiles = 0
                continue
            nblk = ntok // 128
            for ti in range(ntiles):
                last = ti == ntiles - 1
                if kind == "p":
                    for b in range(4):
                        P.add("sp", lambda e, ti=ti, b=b: e.dma_start(out=self.xres[:, b, :], in_=D["xp"][ti * TT + b * 128:ti * TT + (b + 1) * 128, :]),
                              w=[f"xres{b}"], dma=f"ldx{b}")
                else:
                    P.add("dve", lambda e: e.memset(self.xres[:, 0, :], 0.0), w=["xres0"])
                    P.add("sp", lambda e, si=si: e.dma_start(out=self.xres[0:DEC_SEQ, 0, :], in_=D["xs"][si]), w=["xres0"], dma="ldx0")
                self.norm_T(nblk, 0)
                conv_out = None
                if last and not (_RISK & 2):
                    conv_out = D["p_conv"] if kind == "p" else D["s_conv"][si]
                self.proj_in(ntok, nval, conv_out)
                so = None
                if last:
                    so = (D["p_s5_re"], D["p_s5_im"]) if kind == "p" else (D["s_s5_re"][si], D["s_s5_im"][si])
                go = None
                if last:
                    go = D["p_gdn"] if kind == "p" else D["s_gdn"][si]
                self.s5_tile(ntok, nval, so)
                self.gdn_tile(ntok, nval, go)
                gens = []
                wts = []
                alive = []
                while any(alive):
                    for gi in range(0):
                        for _ in range(wts[gi]):
                            if not alive[gi]:
                                break
                            try:
                                next(gens[gi])
                            except StopIteration:
                                alive[gi] = False
                P.add("dve", lambda e, ntok=ntok: e.tensor_copy(out=self.qkvb[:, :, 0:3], in_=self.qkvb[:, :, ntok:ntok + 3]), r=["qkvb"], w=["qkvb"])
                self.linear_tm_res(D["w_out_ab"][0], lambda k, b: self.mixT[:, k, b * 128:(b + 1) * 128], ["mixT"], nblk)
                self.ffn(0, nblk, ntok)
                self.norm_T(nblk, 2)
                if not (_RISK & 8):
                    self.swa_tile(ntok, nval, ti == 0, kind, si, last)
                self.ffn(1, nblk, ntok)
                if kind == "p":
                    self.final_store(nblk, lambda b, ti=ti: (D["y_p"][ti * TT + b * 128:ti * TT + (b + 1) * 128, :], 128))
                else:
                    self.final_store(nblk, lambda b, si=si: (D["y_s"][si], DEC_SEQ))
        P.emit()
        self.es.close()
        return self.nc


_CACHE = {}


def _program(stages=99):
    if stages not in _CACHE:
        _CACHE[stages] = Builder(stages).build()
    return _CACHE[stages]


def kernel(x_prompt, x_sample, state_s5_re, state_s5_im, state_gdn, state_gdn_conv, cache_swa_k, cache_swa_v, **w):
    f = lambda a: np.ascontiguousarray(np.asarray(a, dtype=np.float32))
    nc = _program()
    wd = {n: f(w[n]) for n, _ in W_SPECS}
    in_maps = []
    for c in range(NCORES):
        m = dict(wd)
        m["xp"] = f(x_prompt[c])
        m["xs"] = f(x_sample[2 * c:2 * c + 2])
        m["st_s5_re"] = f(state_s5_re[0, 2 * c:2 * c + 2])
        m["st_s5_im"] = f(state_s5_im[0, 2 * c:2 * c + 2])
        m["st_gdn"] = f(state_gdn[0, 2 * c:2 * c + 2])
        m["st_conv"] = f(state_gdn_conv[0, 2 * c:2 * c + 2])
        m["st_k"] = f(np.asarray(cache_swa_k)[0, 2 * c:2 * c + 2].reshape(2, 128, 256))
        m["st_v"] = f(np.asarray(cache_swa_v)[0, 2 * c:2 * c + 2].reshape(2, 128, 256))
        m["consts"] = _CARR
        in_maps.append(m)
    res = run_bass_kernel_spmd(nc, in_maps, core_ids=list(range(NCORES)))
    R = res.results
    cat = lambda n: np.concatenate([np.asarray(r[n], dtype=np.float32) for r in R], axis=0)
    stk = lambda n: np.stack([np.asarray(r[n], dtype=np.float32) for r in R], axis=0)
    y_p = stk("y_p")
    y_s = cat("y_s")
    outs = [y_p, y_s,
            stk("p_s5_re")[None], stk("p_s5_im")[None], stk("p_gdn")[None], stk("p_conv")[None],
            stk("p_k").reshape(1, 8, 128, 4, 64), stk("p_v").reshape(1, 8, 128, 4, 64),
            cat("s_s5_re")[None], cat("s_s5_im")[None], cat("s_gdn")[None], cat("s_conv")[None],
            cat("s_k").reshape(1, 16, 128, 4, 64), cat("s_v").reshape(1, 16, 128, 4, 64)]
    return tuple(outs)


def _carve(hid, f0, n, dt):
    v = hid[:, f0:f0 + n, :].rearrange("p a b -> p (a b)")
    if dt == F32:
        v = v.bitcast(F32)
    return v, [f"hid{f}" for f in range(f0, f0 + n)]


def alloc_gdn(self):
    sb = self.sb
    self.cw = sb("cw", [128, 4, 12], F32)
    self.gp8 = sb("gp8", [8, 8], F32)
    self.gnw = sb("gnw", [128, 1], F32)
    self.S32 = sb("S32", [128, 4, 128], F32)
    self.Sb = sb("Sb", [128, 4, 128], BF16)
    self.g8t = [sb(f"g8t{i}", [8, TT], F32) for i in range(3)]
    self.bg = sb("bg", [8, TT], F32)
    self.tok = sb("tok", [128, 2, 8, 4], F32)
    self.egl = sb("egl", [128, 4, 8], F32)
    self.vnew = sb("vnew", [128, 2, 2, 128], BF16)
    self.rs8 = sb("rs8", [128, 2, 16], F32)
    self.eye64b = sb("eye64b", [128, 64], BF16)


def gdn_setup(self):
    P, D = self.P, self.D
    for j in range(4):
        P.add("sp", lambda e, j=j: e.dma_start(out=self.cw[:, j, :], in_=D["gdn_conv_w"][0][j].rearrange("(b p) -> p b", p=128),
                                               allow_slow_non_contiguous=True), w=["cw"], dma="init")
    P.add("dve", lambda e: e.memset(self.gp8[:], 0.0), w=["gp8"])
    P.add("sp", lambda e: e.dma_start(out=self.gp8[4:8, 0:1], in_=D["gdn_dt_bias"][0].rearrange("(h o) -> h o", o=1)), w=["gp8"], dma="init")
    P.add("sp", lambda e: e.dma_start(out=self.gp8[4:8, 1:2], in_=D["gdn_a_log"][0].rearrange("(h o) -> h o", o=1)), w=["gp8"], dma="init")
    P.add("sp", lambda e: e.dma_start(out=self.gnw[:], in_=D["gdn_norm_w"][0].rearrange("(p o) -> p o", o=1)), w=["gnw"], dma="init")
    gm = self.cview("gm", 8)
    P.add("act", lambda e: e.activation(out=self.gp8[:, 3:4], in_=self.gp8[:, 1:2], func=AF.Exp), r=["gp8"], w=["gp8"])
    P.add("dve", lambda e: e.tensor_tensor(out=self.gp8[:, 2:3], in0=self.gp8[:, 3:4], in1=gm[:, 2:3], op=ALU.mult), r=["gp8", "cst"], w=["gp8"])
    P.add("dve", lambda e: e.tensor_copy(out=self.eye64b[:], in_=self.cview("eye64")), r=["cst"], w=["eye64b"])


def gdn_pair(self, pr, R, ntok, nval):
    P, D = self.P, self.D
    nch = ntok // 64
    W = ntok
    heads = (2 * pr, 2 * pr + 1)
    T, tk = R["T"], R["tk"]
    ps = [self.ps[b] for b in R["banks"]]
    pk = [f"ps{b}" for b in R["banks"]]
    qkvc, k_qkvc, qd, k_qd, P2, k_P2, Q2, k_Q2 = R["qkvc"], R["k_qkvc"], R["qd"], R["k_qd"], R["P2"], R["k_P2"], R["Q2"], R["k_Q2"]
    attnT, k_at, vb, k_vb, kbg, k_kbg, kdec, k_kdec = R["attnT"], R["k_at"], R["vb"], R["k_vb"], R["kbg"], R["k_kbg"], R["kdec"], R["k_kdec"]
    Ttb, k_Ttb, wT, k_wT, on, k_on = R["Ttb"], R["k_Ttb"], R["wT"], R["k_wT"], R["on"], R["k_on"]
    vnew = self.vnew[:, pr]
    ktok, krs, kegl, kS32, kSb = f"tok{pr}", f"rs8{pr}", f"egl{pr}", f"S32p{pr}", f"Sbp{pr}"
    gm = self.cview("gm", 8)
    sel = self.cview("sel", 8).rearrange("k (r m) -> k r m", m=128)
    selp = self.cview("selp", 8).rearrange("k (h t) -> k h t", t=2)
    eye = self.cview("eye64")
    mUs = self.cview("mUs")
    mUi = self.cview("mUi")
    ones = self.cview("ones")
    idb = self.identb

    def v3(ap, inner=64):
        return ap[:, 0:nch * inner].rearrange("p (c i) -> p c i", i=inner)

    def blk_of(idx):
        return (idx // 2) * 4 + heads[idx % 2]

    def conv_pair(p):
        for j in range(4):
            for u_ in range(2):
                idx = 2 * p + u_
                blk = blk_of(idx)
                acc = T[3 * u_][:, 0:W]
                if j == 0:
                    P.add("dve", lambda e, blk=blk, acc=acc: e.tensor_scalar(out=acc, in0=self.qkvb[:, blk, c0:c0 + W], scalar1=self.cw[:, 0, blk:blk + 1],
                                                                              scalar2=None, op0=ALU.mult), r=["qkvb", "cw"], w=[tk[3 * u_]])
                else:
                    P.add("dve", lambda e, blk=blk, acc=acc, j=j: e.scalar_tensor_tensor(
                        out=acc, in0=self.qkvb[:, blk, c0 + j:c0 + j + W], scalar=self.cw[:, j, blk:blk + 1], in1=acc, op0=ALU.mult, op1=ALU.add),
                        r=["qkvb", "cw", tk[3 * u_]], w=[tk[3 * u_]])

    def mid_pair(p):
        for u_ in range(2):
            idx = 2 * p + u_
            acc, cq, sq = T[3 * u_][:, 0:W], T[3 * u_ + 1][:, 0:W], T[3 * u_ + 2][:, 0:W]
            if p == 2:
                P.add("act", lambda e, idx=idx, acc=acc: e.activation(out=qkvc[:, idx, 0:W], in_=acc, func=AF.Silu), r=[tk[3 * u_]], w=k_qkvc)
                continue
            P.add("act", lambda e, acc=acc, cq=cq: e.activation(out=cq, in_=acc, func=AF.Silu), r=[tk[3 * u_]], w=[tk[3 * u_ + 1]])
            P.add("act", lambda e, cq=cq, sq=sq: e.activation(out=sq, in_=cq, func=AF.Square), r=[tk[3 * u_ + 1]], w=[tk[3 * u_ + 2]])
            P.add("pe", lambda e, sq=sq, u_=u_: e.matmul(ps[u_][:, 0:W], lhsT=ones, rhs=sq, start=True, stop=True), r=["cst", tk[3 * u_ + 2]], w=[pk[u_]])
            sc = 128.0 if p == 0 else 1.0
            P.add("act", lambda e, sq=sq, u_=u_, sc=sc: e.activation(out=sq, in_=ps[u_][:, 0:W], func=AF.Sqrt, scale=sc, bias=sc * EPS),
                  r=[pk[u_]], w=[tk[3 * u_ + 2]])

    def fin_pair(p):
        for u_ in range(2):
            idx = 2 * p + u_
            cq, sq = T[3 * u_ + 1][:, 0:W], T[3 * u_ + 2][:, 0:W]
            P.add("dve", lambda e, sq=sq: e.reciprocal(out=sq, in_=sq), r=[tk[3 * u_ + 2]], w=[tk[3 * u_ + 2]])
        for u_ in range(2):
            idx = 2 * p + u_
            cq, sq = T[3 * u_ + 1][:, 0:W], T[3 * u_ + 2][:, 0:W]
            P.add("dve", lambda e, idx=idx, cq=cq, sq=sq: e.tensor_tensor(out=qkvc[:, idx, 0:W], in0=cq, in1=sq, op=ALU.mult),
                  r=[tk[3 * u_ + 1], tk[3 * u_ + 2]], w=k_qkvc)

    c0 = 0
    conv_pair(0)
    mid_pair(0)
    conv_pair(1)
    fin_pair(0)
    mid_pair(1)
    yield
    conv_pair(2)
    fin_pair(1)
    mid_pair(2)
    yield
    qT = [qkvc[:, 0, :], qkvc[:, 1, :]]
    kT = [qkvc[:, 2, :], qkvc[:, 3, :]]
    vT = [qkvc[:, 4, :], qkvc[:, 5, :]]
    for hh in range(2):
        h = heads[hh]
        rows = slice(64 * hh, 64 * hh + 64)
        P.add("pe", lambda e, h=h, hh=hh, rows=rows: e.matmul(ps[0][rows, 0:W], lhsT=sel[:, 4 + h, 0:64], rhs=self.bg[:, 0:W], start=True, stop=True,
                                                              tile_position=(0, 64 * hh)), r=["cst", "bg"], w=[pk[0]])
        P.add("pe", lambda e, h=h, hh=hh, rows=rows: e.matmul(ps[1][rows, 0:W], lhsT=sel[:, h, 0:64], rhs=self.bg[:, 0:W], start=True, stop=True,
                                                              tile_position=(0, 64 * hh)), r=["cst", "bg"], w=[pk[1]])
        for c in range(nch):
            P.add("pe", lambda e, h=h, hh=hh, rows=rows, c=c: e.matmul(ps[2][rows, 2 * c:2 * c + 2], lhsT=self.bg[:, c * 64:(c + 1) * 64],
                                                                       rhs=selp[:, h, :], start=True, stop=True, tile_position=(0, 64 * hh)),
                  r=["cst", "bg"], w=[pk[2]])
    tok = self.tok[:, pr]
    P.add("dve", lambda e: e.tensor_copy(out=tok[:, 0:nch, 0:2], in_=ps[2][:, 0:2 * nch].rearrange("p (c t) -> p c t", t=2)), r=[pk[2]], w=[ktok])
    E = T[0]
    P.add("dve", lambda e: e.tensor_tensor(out=v3(E), in0=v3(ps[0]), in1=tok[:, 0:nch, 1:2].to_broadcast([128, nch, 64]), op=ALU.subtract),
          r=[pk[0], ktok], w=[tk[0]])
    P.add("dve", lambda e: e.tensor_scalar(out=E[:, 0:W], in0=E[:, 0:W], scalar1=0.0, scalar2=None, op0=ALU.min), r=[tk[0]], w=[tk[0]])
    P.add("act", lambda e: e.activation(out=E[:, 0:W], in_=E[:, 0:W], func=AF.Exp), r=[tk[0]], w=[tk[0]])
    P.add("dve", lambda e: e.tensor_tensor(out=tok[:, 0:nch, 3:4], in0=v3(ps[0])[:, :, 63:64], in1=tok[:, 0:nch, 1:2], op=ALU.subtract),
          r=[pk[0], ktok], w=[ktok])
    P.add("act", lambda e: e.activation(out=tok[:, 0:nch, 3:4], in_=tok[:, 0:nch, 3:4], func=AF.Exp), r=[ktok], w=[ktok])
    P.add("act", lambda e: e.activation(out=tok[:, 0:nch, 2:3], in_=tok[:, 0:nch, 1:2], func=AF.Exp), r=[ktok], w=[ktok])
    P.add("dve", lambda e: e.tensor_tensor(out=tok[:, 0:nch, 2:3], in0=tok[:, 0:nch, 2:3], in1=tok[:, 0:nch, 0:1], op=ALU.mult), r=[ktok], w=[ktok])
    yield
    Bm, Am, Tt = T[1], T[2], T[5]
    mUs_b = mUs.unsqueeze(1).to_broadcast([128, nch, 64])
    mUi_b = mUi.unsqueeze(1).to_broadcast([128, nch, 64])
    eye_b = eye.unsqueeze(1).to_broadcast([128, nch, 64])
    for hh in range(2):
        rows = slice(64 * hh, 64 * hh + 64)
        for c in range(nch):
            cs_ = slice(c * 64, (c + 1) * 64)
            P.add("pe", lambda e, hh=hh, rows=rows, cs_=cs_: e.matmul(ps[2][rows, cs_], lhsT=kT[hh][:, cs_], rhs=kT[hh][:, cs_], start=True, stop=True,
                                                                      tile_position=(0, 64 * hh)), r=k_qkvc, w=[pk[2]])
    P.add("dve", lambda e: e.tensor_tensor(out=Bm[:, 0:W], in0=ps[2][:, 0:W], in1=E[:, 0:W], op=ALU.mult), r=[pk[2], tk[0]], w=[tk[1]])
    for hh in range(2):
        rows = slice(64 * hh, 64 * hh + 64)
        for c in range(nch):
            cs_ = slice(c * 64, (c + 1) * 64)
            P.add("pe", lambda e, hh=hh, rows=rows, cs_=cs_: e.matmul(ps[2][rows, cs_], lhsT=kT[hh][:, cs_], rhs=qT[hh][:, cs_], start=True, stop=True,
                                                                      tile_position=(0, 64 * hh)), r=k_qkvc, w=[pk[2]])
    P.add("dve", lambda e: e.tensor_tensor(out=v3(Bm), in0=v3(Bm), in1=mUs_b, op=ALU.mult), r=[tk[1], "cst"], w=[tk[1]])
    P.add("dve", lambda e: e.tensor_tensor(out=Bm[:, 0:W], in0=Bm[:, 0:W], in1=ps[1][:, 0:W], op=ALU.mult), r=[tk[1], pk[1]], w=[tk[1]])
    P.add("dve", lambda e: e.tensor_tensor(out=Am[:, 0:W], in0=ps[2][:, 0:W], in1=E[:, 0:W], op=ALU.mult), r=[pk[2], tk[0]], w=[tk[2]])
    P.add("dve", lambda e: e.tensor_tensor(out=v3(attnT), in0=v3(Am), in1=mUi_b, op=ALU.mult), r=[tk[2], "cst"], w=k_at)
    yield
    t0b = T[0].bitcast(BF16)
    Bm_b, Am_b = t0b[:, 0:512], t0b[:, 512:1024]
    P.add("act", lambda e: e.activation(out=Bm_b[:, 0:W], in_=Bm[:, 0:W], func=AF.Copy), r=[tk[1], *k_at], w=[tk[0]])
    for hh in range(2):
        rows = slice(64 * hh, 64 * hh + 64)
        for c in range(nch):
            cs_ = slice(c * 64, (c + 1) * 64)
            P.add("pe", lambda e, hh=hh, rows=rows, cs_=cs_: e.matmul(ps[2][rows, cs_], lhsT=Bm_b[rows, cs_], rhs=self.eye64b[rows, :], start=True, stop=True,
                                                                      tile_position=(64 * hh, 64 * hh)), r=[tk[0], "eye64b"], w=[pk[2]])
    P.add("dve", lambda e: e.tensor_copy(out=Am_b[:, 0:W], in_=ps[2][:, 0:W]), r=[pk[2]], w=[tk[0]])
    P.add("dve", lambda e: e.tensor_tensor(out=v3(Tt), in0=eye_b, in1=v3(Bm), op=ALU.subtract), r=[tk[1], "cst"], w=[tk[5]])
    P.add("act", lambda e: e.activation(out=Ttb[:, 0:W], in_=Tt[:, 0:W], func=AF.Copy), r=[tk[5]], w=k_Ttb)
    yield
    Pm, Qm, kP, kQ = Bm_b, Am_b, [tk[0]], [tk[0]]
    sets = [(T[3].bitcast(BF16)[:, 0:512], T[4].bitcast(BF16)[:, 0:512], tk[3:4], tk[4:5]),
            (P2.bitcast(BF16)[:, 0:512], Q2.bitcast(BF16)[:, 0:512], k_P2, k_Q2)]
    for lvl in range(5):
        Pn, Qn, kPn, kQn = sets[lvl % 2]
        for hh in range(2):
            rows = slice(64 * hh, 64 * hh + 64)
            for c in range(nch):
                cs_ = slice(c * 64, (c + 1) * 64)
                tp = (64 * hh, 64 * hh)
                P.add("pe", lambda e, rows=rows, cs_=cs_, tp=tp, Pm=Pm, Qm=Qm: e.matmul(ps[1][rows, cs_], lhsT=Pm[rows, cs_], rhs=Qm[rows, cs_],
                                                                                        start=True, stop=True, tile_position=tp),
                      r=[*kP, *kQ], w=[pk[1]])
                if lvl < 4:
                    P.add("pe", lambda e, rows=rows, cs_=cs_, tp=tp, Pm=Pm, Qm=Qm: e.matmul(ps[0][rows, cs_], lhsT=Qm[rows, cs_], rhs=Pm[rows, cs_],
                                                                                            start=True, stop=True, tile_position=tp),
                          r=[*kP, *kQ], w=[pk[0]])
        yield
        P.add("dve", lambda e, Qn=Qn: e.tensor_copy(out=Qn[:, 0:W], in_=ps[1][:, 0:W]), r=[pk[1]], w=kQn)
        if lvl < 4:
            P.add("act", lambda e, Pn=Pn: e.activation(out=Pn[:, 0:W], in_=ps[0][:, 0:W], func=AF.Copy), r=[pk[0]], w=kPn)
        yield
        for hh in range(2):
            rows = slice(64 * hh, 64 * hh + 64)
            for c in range(nch):
                cs_ = slice(c * 64, (c + 1) * 64)
                P.add("pe", lambda e, rows=rows, cs_=cs_, hh=hh, Qn=Qn: e.matmul(ps[2][rows, cs_], lhsT=Qn[rows, cs_], rhs=Ttb[rows, cs_],
                                                                                 start=True, stop=True, tile_position=(64 * hh, 64 * hh)),
                      r=[*kQn, *k_Ttb], w=[pk[2]])
        yield
        P.add("dve", lambda e: e.tensor_tensor(out=Tt[:, 0:W], in0=Tt[:, 0:W], in1=ps[2][:, 0:W], op=ALU.add), r=[tk[5], pk[2]], w=[tk[5]])
        P.add("act", lambda e: e.activation(out=Ttb[:, 0:W], in_=Tt[:, 0:W], func=AF.Copy), r=[tk[5]], w=k_Ttb)
        Pm, Qm, kP, kQ = Pn, Qn, kPn, kQn
        yield
    pT3 = self.pT[:, 0:nch * 128].rearrange("p (c d) -> p c d", d=128)
    for src, dsts in ((vT, "v"), (kT, "k")):
        for hh in range(2):
            rows = slice(64 * hh, 64 * hh + 64)
            for c in range(nch):
                P.add("pe", lambda e, src=src, hh=hh, rows=rows, c=c: e.transpose(out=self.pT[rows, c * 128:(c + 1) * 128], in_=src[hh][:, c * 64:(c + 1) * 64],
                                                                                   identity=idb[:], tile_position=(0, 64 * hh)),
                      r=[*k_qkvc, "identb"], w=["pT"])
        if dsts == "v":
            P.add("dve", lambda e: e.tensor_tensor(out=v3(vb, 128), in0=pT3, in1=tok[:, 0:nch, 0:1].to_broadcast([128, nch, 128]), op=ALU.mult),
                  r=["pT", ktok], w=k_vb)
        else:
            P.add("dve", lambda e: e.tensor_tensor(out=v3(kbg, 128), in0=pT3, in1=tok[:, 0:nch, 2:3].to_broadcast([128, nch, 128]), op=ALU.mult),
                  r=["pT", ktok], w=k_kbg)
            P.add("dve", lambda e: e.tensor_tensor(out=v3(kdec, 128), in0=pT3, in1=tok[:, 0:nch, 3:4].to_broadcast([128, nch, 128]), op=ALU.mult),
                  r=["pT", ktok], w=k_kdec)
    yield
    u = [T[1], T[2]]
    for hh in range(2):
        rows = slice(64 * hh, 64 * hh + 64)
        for c in range(nch):
            ub_, uc = (0, c) if c < 4 else (1, c - 4)
            P.add("pe", lambda e, hh=hh, rows=rows, c=c, ub_=ub_, uc=uc: e.matmul(
                ps[ub_][rows, uc * 128:(uc + 1) * 128], lhsT=Ttb[rows, c * 64:(c + 1) * 64], rhs=vb[rows, c * 128:(c + 1) * 128],
                start=True, stop=True, tile_position=(64 * hh, 64 * hh)), r=[*k_Ttb, *k_vb], w=[pk[ub_]])
    P.add("act", lambda e: e.activation(out=u[0][:, 0:min(4, nch) * 128], in_=ps[0][:, 0:min(4, nch) * 128], func=AF.Copy), r=[pk[0]], w=[tk[1]])
    if nch > 4:
        P.add("act", lambda e: e.activation(out=u[1][:, :], in_=ps[1][:, :], func=AF.Copy), r=[pk[1]], w=[tk[2]])
    wb = (2, 0)
    for hh in range(2):
        rows = slice(64 * hh, 64 * hh + 64)
        for c in range(nch):
            P.add("pe", lambda e, hh=hh, rows=rows, c=c: e.matmul(
                ps[wb[hh]][:, c * 64:(c + 1) * 64], lhsT=kbg[rows, c * 128:(c + 1) * 128], rhs=Ttb[rows, c * 64:(c + 1) * 64],
                start=True, stop=True, tile_position=(64 * hh, 0)), r=[*k_Ttb, *k_kbg], w=[pk[wb[hh]]])
    for hh in range(2):
        P.add("act", lambda e, hh=hh: e.activation(out=wT[:, hh, 0:W], in_=ps[wb[hh]][:, 0:W], func=AF.Copy), r=[pk[wb[hh]]], w=k_wT)
    yield
    eg = T[0]
    for hh in range(2):
        h = heads[hh]
        P.add("pe", lambda e, h=h: e.matmul(ps[1][:, 0:W], lhsT=sel[:, 4 + h, :], rhs=self.bg[:, 0:W], start=True, stop=True), r=["cst", "bg"], w=[pk[1]])
        P.add("act", lambda e: e.activation(out=eg[:, 0:W], in_=ps[1][:, 0:W], func=AF.Exp), r=[pk[1]], w=[tk[0]])
        P.add("dve", lambda e, hh=hh: e.tensor_tensor(out=qd[:, hh, 0:W], in0=qT[hh][:, 0:W], in1=eg[:, 0:W], op=ALU.mult), r=[*k_qkvc, tk[0]], w=k_qd)
        P.add("dve", lambda e, h=h: e.tensor_copy(out=self.egl[:, h, 0:nch], in_=v3(eg)[:, :, 63]), r=[tk[0]], w=[kegl])
    yield
    o_tm = [T[3], T[4]]
    for c in range(nch):
        slot = c % 2
        for hh in range(2):
            h = heads[hh]
            rows = slice(64 * hh, 64 * hh + 64)
            P.add("pe", lambda e, hh=hh, h=h, rows=rows, c=c: e.matmul(ps[0][rows, 0:128], lhsT=wT[:, hh, c * 64:(c + 1) * 64], rhs=self.Sb[:, h, :],
                                                                       start=True, stop=True, tile_position=(0, 64 * hh)), r=[*k_wT, kSb], w=[pk[0]])
        ub_, uc = (0, c) if c < 4 else (1, c - 4)
        P.add("dve", lambda e, slot=slot, ub_=ub_, uc=uc: e.tensor_tensor(out=vnew[:, slot, :], in0=u[ub_][:, uc * 128:(uc + 1) * 128],
                                                                           in1=ps[0][:, 0:128], op=ALU.subtract),
              r=[tk[1 + ub_], pk[0]], w=[f"vnew{pr}_{slot}"])
        for hh in range(2):
            h = heads[hh]
            rows = slice(64 * hh, 64 * hh + 64)
            P.add("pe", lambda e, hh=hh, h=h, rows=rows, c=c: e.matmul(ps[0][rows, 128:256], lhsT=qd[:, hh, c * 64:(c + 1) * 64], rhs=self.Sb[:, h, :],
                                                                       start=True, stop=False, tile_position=(0, 64 * hh)), r=[*k_qd, kSb], w=[pk[0]])
            P.add("pe", lambda e, hh=hh, rows=rows, c=c, slot=slot: e.matmul(ps[0][rows, 128:256], lhsT=attnT[rows, c * 64:(c + 1) * 64],
                                                                             rhs=vnew[rows, slot, :], start=False, stop=True,
                                                                             tile_position=(64 * hh, 64 * hh)), r=[*k_at, f"vnew{pr}_{slot}"], w=[pk[0]])
            P.add("pe", lambda e, hh=hh, rows=rows, c=c, slot=slot: e.matmul(ps[1 + hh][:, 0:128], lhsT=kdec[rows, c * 128:(c + 1) * 128],
                                                                             rhs=vnew[rows, slot, :], start=True, stop=True,
                                                                             tile_position=(64 * hh, 0)), r=[*k_kdec, f"vnew{pr}_{slot}"], w=[pk[1 + hh]])
        P.add("act", lambda e, ub_=ub_, uc=uc: e.activation(out=o_tm[ub_][:, uc * 128:(uc + 1) * 128], in_=ps[0][:, 128:256], func=AF.Copy),
              r=[pk[0]], w=[tk[3 + ub_]])
        for hh in range(2):
            h = heads[hh]
            P.add("dve", lambda e, hh=hh, h=h, c=c: e.scalar_tensor_tensor(out=self.S32[:, h, :], in0=self.S32[:, h, :], scalar=self.egl[:, h, c:c + 1],
                                                                           in1=ps[1 + hh][:, 0:128], op0=ALU.mult, op1=ALU.add),
                  r=[kS32, kegl, pk[1 + hh]], w=[kS32])
        P.add("act", lambda e, pr=pr: e.activation(out=self.Sb[:, 2 * pr:2 * pr + 2, :], in_=self.S32[:, 2 * pr:2 * pr + 2, :], func=AF.Copy), r=[kS32], w=[kSb])
        yield
    rs = self.rs8[:, pr]
    for ub_ in range(2 if nch > 4 else 1):
        ncc = min(4, nch)
        o3 = o_tm[ub_][:, 0:ncc * 128].rearrange("p (c d) -> p c d", d=128)
        sq3 = T[0][:, 0:ncc * 128].rearrange("p (c d) -> p c d", d=128)
        P.add("dve", lambda e, o3=o3, sq3=sq3: e.tensor_tensor(out=sq3, in0=o3, in1=o3, op=ALU.mult), r=[tk[3 + ub_]], w=[tk[0]])
        P.add("dve", lambda e, sq3=sq3, ub_=ub_, ncc=ncc: e.reduce_sum(out=rs[:, 4 * ub_:4 * ub_ + ncc], in_=sq3, axis=AX.X), r=[tk[0]], w=[krs])
        P.add("act", lambda e, ub_=ub_, ncc=ncc: e.activation(out=rs[:, 8 + 4 * ub_:8 + 4 * ub_ + ncc], in_=rs[:, 4 * ub_:4 * ub_ + ncc], func=AF.Sqrt,
                                                              scale=1.0 / 128, bias=EPS), r=[krs], w=[krs])
        P.add("dve", lambda e, ub_=ub_, ncc=ncc: e.reciprocal(out=rs[:, 8 + 4 * ub_:8 + 4 * ub_ + ncc], in_=rs[:, 8 + 4 * ub_:8 + 4 * ub_ + ncc]), r=[krs], w=[krs])
        P.add("dve", lambda e, o3=o3, ub_=ub_, ncc=ncc: e.tensor_tensor(
            out=on[:, ub_ * 512:ub_ * 512 + ncc * 128].rearrange("p (c d) -> p c d", d=128), in0=o3,
            in1=rs[:, 8 + 4 * ub_:8 + 4 * ub_ + ncc].unsqueeze(2).to_broadcast([128, ncc, 128]), op=ALU.mult), r=[tk[3 + ub_], krs], w=k_on)
    yield
    for hh in range(2):
        rows = slice(64 * hh, 64 * hh + 64)
        for c in range(nch):
            P.add("pe", lambda e, hh=hh, rows=rows, c=c: e.matmul(ps[hh][:, c * 64:(c + 1) * 64], lhsT=on[rows, c * 128:(c + 1) * 128], rhs=self.eye64b[rows, :],
                                                                  start=True, stop=True, tile_position=(64 * hh, 0)), r=[*k_on, "eye64b"], w=[pk[hh]])
        h = heads[hh]
        P.add("dve", lambda e, hh=hh, h=h: e.scalar_tensor_tensor(out=self.mixT[:, 4 + h, 0:W], in0=ps[hh][:, 0:W], scalar=self.gnw[:, 0:1],
                                                                  in1=self.zs[:, h, 0:W], op0=ALU.mult, op1=ALU.mult), r=[pk[hh], "gnw", "zs"], w=["mixT"])


def gdn_tile(self, ntok, nval, state_out=None):
    P, D = self.P, self.D
    nch = ntok // 64
    T = self.g5t
    tk = [f"g5t{i}" for i in range(6)]
    gm = self.cview("gm", 8)
    sel = self.cview("sel", 8).rearrange("k (r m) -> k r m", m=128)
    selp = self.cview("selp", 8).rearrange("k (h t) -> k h t", t=2)
    eye = self.cview("eye64")
    mUs = self.cview("mUs")
    mUi = self.cview("mUi")
    ones = self.cview("ones")
    idb = self.identb
    W = ntok

    def v3(ap, inner=64):
        return ap[:, 0:nch * inner].rearrange("p (c i) -> p c i", i=inner)

    g0, g1, g2 = self.g8t
    ba = self.ba
    P.add("act", lambda e: e.activation(out=g0[:, 0:W], in_=ba[:, 0:W], func=AF.Exp, bias=self.gp8[:, 0:1]), r=["ba", "gp8"], w=["g8t0"])
    P.add("act", lambda e: e.activation(out=g0[:, 0:W], in_=g0[:, 0:W], func=AF.Ln, bias=1.0), r=["g8t0"], w=["g8t0"])
    P.add("dve", lambda e: e.tensor_scalar(out=g1[:, 0:W], in0=g0[:, 0:W], scalar1=self.gp8[:, 2:3], scalar2=None, op0=ALU.mult), r=["g8t0", "gp8"], w=["g8t1"])
    P.add("act", lambda e: e.activation(out=g0[:, 0:W], in_=ba[:, 0:W], func=AF.Sigmoid), r=["ba", "g8t1"], w=["g8t0"])
    P.add("dve", lambda e: e.scalar_tensor_tensor(out=g2[:, 0:W], in0=g0[:, 0:W], scalar=gm[:, 0:1], in1=g1[:, 0:W], op0=ALU.mult, op1=ALU.add),
          r=["g8t0", "g8t1", "cst"], w=["g8t2"])
    if nval < ntok:
        P.add("dve", lambda e: e.memset(g2[:, nval:W], 0.0), w=["g8t2"])
    P.add("dve", lambda e: e.tensor_tensor_scan(out=g0[:, 0:W], data0=self.cview("cmask", 8)[:, 0:W], data1=g2[:, 0:W], initial=0.0,
                                                op0=ALU.mult, op1=ALU.add), r=["g8t2", "cst"], w=["g8t0"])
    P.add("dve", lambda e: e.tensor_scalar(out=g1[:, 0:W], in0=g0[:, 0:W], scalar1=gm[:, 1:2], scalar2=None, op0=ALU.mult), r=["g8t0", "cst"], w=["g8t1"])
    P.add("dve", lambda e: e.scalar_tensor_tensor(out=self.bg[:, 0:W], in0=g2[:, 0:W], scalar=gm[:, 0:1], in1=g1[:, 0:W], op0=ALU.mult, op1=ALU.add),
          r=["g8t2", "g8t1", "cst"], w=["bg"])


    hid = self.hid
    R0 = {"T": self.g5t, "tk": [f"g5t{i}" for i in range(6)], "banks": (0, 1, 2)}
    q_, R0["k_qkvc"] = _carve(hid, 0, 6, BF16)
    R0["qkvc"] = q_.rearrange("p (a t) -> p a t", t=TT)
    q_, R0["k_qd"] = _carve(hid, 6, 2, BF16)
    R0["qd"] = q_.rearrange("p (a t) -> p a t", t=TT)
    R0["P2"], R0["k_P2"] = _carve(hid, 8, 2, F32)
    R0["Q2"], R0["k_Q2"] = _carve(hid, 10, 2, F32)
    R0["attnT"], R0["k_at"] = _carve(hid, 12, 1, BF16)
    R0["vb"], R0["k_vb"] = _carve(hid, 13, 2, BF16)
    R0["kbg"], R0["k_kbg"] = _carve(hid, 17, 2, BF16)
    R0["kdec"], R0["k_kdec"] = _carve(hid, 19, 2, BF16)
    R0["Ttb"], R0["k_Ttb"] = _carve(hid, 21, 1, BF16)
    q_, R0["k_wT"] = _carve(hid, 8, 2, BF16)
    R0["wT"] = q_.rearrange("p (a t) -> p a t", t=TT)
    R0["on"], R0["k_on"] = _carve(hid, 10, 2, BF16)
    R1 = {"T": self.s5t, "tk": [f"s5t{i}" for i in range(6)], "banks": (4, 5, 6)}
    R1["qkvc"], R1["k_qkvc"] = self.hT[:, 0:6, :], ["hT_q"]
    R1["qd"], R1["k_qd"] = self.hT[:, 6:8, :], ["hT_d"]
    R1["P2"], R1["k_P2"] = self.ys[:, 0, :], ["ys_0"]
    R1["Q2"], R1["k_Q2"] = self.ys[:, 1, :], ["ys_1"]
    R1["wT"], R1["k_wT"] = self.ys[:, 2, :].bitcast(BF16).rearrange("p (a t) -> p a t", t=TT), ["ys_2"]
    R1["on"], R1["k_on"] = self.ys[:, 3, :].bitcast(BF16), ["ys_3"]
    R1["attnT"], R1["k_at"] = self.ub[:, 0, :], ["ub_0"]
    R1["Ttb"], R1["k_Ttb"] = self.ub[:, 1, :], ["ub_1"]
    R1["vb"], R1["k_vb"] = self.ub[:, 2:4, :].rearrange("p a t -> p (a t)"), ["ub_23"]
    R1["kbg"], R1["k_kbg"] = self.xrb[:, 0].rearrange("p a t -> p (a t)"), ["xrb0"]
    R1["kdec"], R1["k_kdec"] = self.xrb[:, 1].rearrange("p a t -> p (a t)"), ["xrb1"]
    gens = [gdn_pair(self, 0, R0, ntok, nval), gdn_pair(self, 1, R1, ntok, nval)]
    alive = [True, True]
    while any(alive):
        for gi in range(2):
            if alive[gi]:
                try:
                    next(gens[gi])
                except StopIteration:
                    alive[gi] = False
    if state_out is not None:
        P.add("sp", lambda e: e.dma_start(out=state_out.rearrange("h k v -> k h v"), in_=self.S32[:]), r=["S32"], w=["gdn_out"], dma="out")


Builder.alloc_gdn = alloc_gdn
Builder.gdn_setup = gdn_setup
Builder.gdn_tile = gdn_tile


def alloc_swa(self):
    sb = self.sb
    self.kTd = sb("kTd", [128, 4, 128 + TT], BF16)
    self.vtm = sb("vtm", [128, 5, 256], BF16)
    self.esink = sb("esink", [128, 8], F32)
    self.onesb = sb("onesb", [128, 64], BF16)
    self.rcp = self.sg[:, 0:256]


def swa_setup(self):
    P, D = self.P, self.D
    sk = D["swa_sinks"][0].rearrange("(m two) -> two m", two=2)
    for half in range(2):
        P.add("sp", lambda e, half=half: e.dma_start(out=self.esink[64 * half:64 * half + 64, :], in_=sk[half].partition_broadcast(64),
                                                     allow_slow_non_contiguous=True), w=["esink"], dma="init")
    P.add("act", lambda e: e.activation(out=self.esink[:], in_=self.esink[:], func=AF.Exp), r=["esink"], w=["esink"])
    P.add("dve", lambda e: e.memset(self.onesb[:], 1.0), w=["onesb"])


def swa_init_sample(self, si):
    P, D = self.P, self.D
    ck, kck = _carve(self.hid, 14, 1, BF16)
    ck = ck[:, 0:256]
    P.add("pool", lambda e: e.dma_start(out=self.vtm[:, 0, :], in_=D["st_v"][si]), w=["vtm"], dma="ldkv")
    P.add("pool", lambda e: e.dma_start(out=ck, in_=D["st_k"][si]), w=kck, dma="ldkv")
    for kv in range(4):
        for half in range(2):
            P.add("pe", lambda e, kv=kv, half=half: e.matmul(self.ps[0][64 * half:64 * half + 64, kv * 128:(kv + 1) * 128], lhsT=ck[:, kv * 64:(kv + 1) * 64],
                                                             rhs=self.identb[:], start=True, stop=True, tile_position=(0, 64 * half)),
                  r=[*kck, "identb"], w=["ps0"])
    P.add("act", lambda e: e.activation(out=self.kTd[:, :, 0:128], in_=self.ps[0][:, :].rearrange("p (k t) -> p k t", t=128), func=AF.Copy), r=["ps0"], w=["kTd"])


def swa_tile(self, ntok, nval, first, kind, si, last):
    P, D = self.P, self.D
    nblk = ntok // 128
    nch = ntok // 64 if kind == "p" else 1
    hid = self.hid
    qT, k_qT = _carve(hid, 0, 8, BF16)
    qT = qT.rearrange("p (a t) -> p a t", t=TT)
    ETs, k_ETs = [], []
    for j in range(2):
        et, ke = _carve(hid, 8 + 2 * j, 2, BF16)
        ETs.append(et.rearrange("p (a t) -> p a t", t=TT))
        k_ETs.append(ke)
    st32, k_st = self.sg[:].rearrange("p (a t) -> p a t", t=256), ["sg"]
    Wq, Wk, Wv = D["swa_wq"][0], D["swa_wk"][0], D["swa_wv"][0]

    def evq(ci, m, b):
        P.add("act", lambda e: e.activation(out=qT[:, ci, 0:ntok], in_=self.ps[b][:, 0:ntok], func=AF.Copy, scale=0.125), r=[f"ps{b}"], w=k_qT)
    self.linear_fm(Wq, 0, 1024, lambda k: self.hT[:, k, 0:ntok], ["hT"], ntok, evq)
    if _SWASTOP == 1:
        P.add('dve', lambda e: e.memset(self.mixT[:], 0.0), w=['mixT'])
        return
    vk, kk_ = self.wload(Wk.rearrange("(c p) n -> p c n", p=128))
    for kv in range(4):
        b = self.bank()
        for half in range(2):
            for k in range(8):
                P.add("pe", lambda e, kv=kv, b=b, half=half, k=k: e.matmul(self.ps[b][64 * half:64 * half + 64, 0:ntok], lhsT=vk[:, k, kv * 64:(kv + 1) * 64],
                                                                           rhs=self.hT[:, k, 0:ntok], start=(k == 0), stop=(k == 7), tile_position=(0, 64 * half)),
                      r=[kk_, "hT"], w=[f"ps{b}"])
        P.add("act", lambda e, kv=kv, b=b: e.activation(out=self.kTd[:, kv, 128:128 + ntok], in_=self.ps[b][:, 0:ntok], func=AF.Copy), r=[f"ps{b}"], w=["kTd"])
    if _SWASTOP == 2:
        P.add('dve', lambda e: e.memset(self.mixT[:], 0.0), w=['mixT'])
        return
    if last:
        ob = nblk - 1
        b = self.bank()
        for k in range(8):
            P.add("pe", lambda e, b=b, k=k: e.matmul(self.ps[b][:, 0:256], lhsT=self.hT[:, k, ob * 128:(ob + 1) * 128], rhs=vk[:, k, :], start=(k == 0), stop=(k == 7)),
                  r=[kk_, "hT"], w=[f"ps{b}"])
        P.add("dve", lambda e, b=b: e.tensor_copy(out=st32[:, 0, :], in_=self.ps[b][:, 0:256]), r=[f"ps{b}"], w=k_st)
    if _SWASTOP == 3:
        P.add('dve', lambda e: e.memset(self.mixT[:], 0.0), w=['mixT'])
        return
    vv, kv_ = self.wload(Wv.rearrange("(c p) n -> p c n", p=128))
    for blk in range(min(nblk, int(os.environ.get('VBLK', '9')))):
        b = self.bank()
        for k in range(8):
            P.add("pe", lambda e, b=b, k=k, blk=blk: e.matmul(self.ps[b][:, 0:256], lhsT=self.hT[:, k, blk * 128:(blk + 1) * 128], rhs=vv[:, k, :],
                                                              start=(k == 0), stop=(k == 7)), r=[kv_, "hT"], w=[f"ps{b}"])
        P.add("act", lambda e, b=b, blk=blk: e.activation(out=self.vtm[:, 1 + blk, :], in_=self.ps[b][:, 0:256], func=AF.Copy), r=[f"ps{b}"], w=["vtm"])
        if last and blk == nblk - 1 and not (_RISK & 64):
            P.add("act", lambda e, b=b: e.activation(out=st32[:, 1, :], in_=self.ps[b][:, 0:256], func=AF.Copy), r=[f"ps{b}"], w=k_st)
    if last and not (_RISK & 32):
        if kind == "p":
            P.add("sp", lambda e: e.dma_start(out=D["p_k"][:, :], in_=st32[:, 0, :]), r=k_st, w=["ok"], dma="out")
            P.add("sp", lambda e: e.dma_start(out=D["p_v"][:, :], in_=st32[:, 1, :]), r=k_st, w=["ov"], dma="out")
        else:
            n0 = 128 - DEC_SEQ
            for j, (nm, src) in enumerate((("s_k", "st_k"), ("s_v", "st_v"))):
                P.add("sp", lambda e, nm=nm, src=src: e.dma_start(out=D[nm][si][0:n0, :], in_=D[src][si][DEC_SEQ:128, :]), w=[f"o{nm}a"], dma="out")
                P.add("sp", lambda e, nm=nm, j=j: e.dma_start(out=D[nm][si][n0:128, :], in_=st32[0:DEC_SEQ, j, :]), r=k_st, w=[f"o{nm}b"], dma="out")
    if _SWASTOP == 4:
        P.add('dve', lambda e: e.memset(self.mixT[:], 0.0), w=['mixT'])
        return
    mbc = self.cview("mb")
    steps = []
    for c in range(nch):
        if kind == "p":
            lo, hi = 64 * c, 64 * c + 192
            if first:
                lo = max(lo, 128)
        else:
            lo, hi = 0, 128 + nval
        pieces = []
        for blk in range(lo // 128, (hi - 1) // 128 + 1):
            a = max(lo, blk * 128) - blk * 128
            b_ = min(hi, (blk + 1) * 128) - blk * 128
            pieces.append((blk, a, b_))
        for gq in range(2):
            steps.append((c, gq, pieces))
    pS = [[self.ps[0], self.ps[1]], [self.ps[2], self.ps[3]]]
    kS = [["ps0", "ps1"], ["ps2", "ps3"]]

    def scores_exp(it):
        c, gq, pieces = steps[it]
        ET, k_ET = ETs[it % 2], k_ETs[it % 2]
        for pi, (blk, a, b_) in enumerate(pieces):
            mcol = {(0, 128): 0, (0, 64): 1, (64, 128): 2, (0, 16): 3}[(a, b_)]
            for mi in range(4):
                m = 4 * gq + mi
                kv = m // 2
                for half in range(2):
                    P.add("pe", lambda e, pi=pi, blk=blk, m=m, kv=kv, half=half, mi=mi, c=c: e.matmul(
                        pS[pi][half][:, mi * 64:(mi + 1) * 64], lhsT=self.kTd[64 * half:64 * half + 64, kv, blk * 128:(blk + 1) * 128],
                        rhs=qT[64 * half:64 * half + 64, m, c * 64:(c + 1) * 64], start=True, stop=True, tile_position=(64 * half, 0)),
                        r=["kTd", *k_qT], w=[kS[pi][half]])
            for half in range(2):
                P.add("act", lambda e, pi=pi, mcol=mcol, half=half, ET=ET: e.activation(
                    out=ET[:, pi, half * 256:(half + 1) * 256], in_=pS[pi][half][:, 0:256], func=AF.Exp, bias=mbc[:, mcol:mcol + 1]),
                    r=[kS[pi][half], "cst"], w=k_ET)

    def pv_out(it):
        c, gq, pieces = steps[it]
        ET, k_ET = ETs[it % 2], k_ETs[it % 2]
        pOD = self.ps[4 + it % 2]
        kOD = f"ps{4 + it % 2}"
        np_ = len(pieces)
        for mi in range(4):
            m = 4 * gq + mi
            kv = m // 2
            for half in range(2):
                col = (half * 4 + mi) * 64
                for pi, (blk, a, b_) in enumerate(pieces):
                    P.add("pe", lambda e, pi=pi, blk=blk, kv=kv, half=half, col=col, mi=mi: e.matmul(
                        pOD[64 * half:64 * half + 64, mi * 64:(mi + 1) * 64], lhsT=self.vtm[:, blk, kv * 64:(kv + 1) * 64],
                        rhs=ET[:, pi, col:col + 64], start=(pi == 0), stop=(pi == np_ - 1), tile_position=(0, 64 * half)),
                        r=["vtm", *k_ET], w=[kOD])
                for pi, (blk, a, b_) in enumerate(pieces):
                    P.add("pe", lambda e, pi=pi, half=half, col=col, mi=mi: e.matmul(
                        pOD[64 * half:64 * half + 64, 256 + mi * 64:256 + (mi + 1) * 64], lhsT=self.onesb[:, :],
                        rhs=ET[:, pi, col:col + 64], start=(pi == 0), stop=(pi == np_ - 1), tile_position=(0, 64 * half)),
                        r=["onesb", *k_ET], w=[kOD])
        rc3 = self.rcp.rearrange("p (m i) -> p m i", i=64)
        P.add("dve", lambda e: e.tensor_tensor(out=rc3, in0=pOD[:, 256:512].rearrange("p (m i) -> p m i", i=64),
                                               in1=self.esink[:, 4 * gq:4 * gq + 4].unsqueeze(2).to_broadcast([128, 4, 64]), op=ALU.add),
              r=[kOD, "esink"], w=["sg"])
        P.add("dve", lambda e: e.reciprocal(out=self.rcp, in_=self.rcp), r=["sg"], w=["sg"])
        P.add("dve", lambda e: e.tensor_tensor(out=self.mixT[:, 4 * gq:4 * gq + 4, c * 64:(c + 1) * 64],
                                               in0=pOD[:, 0:256].rearrange("p (m i) -> p m i", i=64), in1=rc3, op=ALU.mult),
              r=[kOD, "sg"], w=["mixT"])

    scores_exp(0)
    for it in range(len(steps)):
        if it + 1 < len(steps):
            scores_exp(it + 1)
        pv_out(it)
    if kind != "p" and ntok > 64:
        P.add("dve", lambda e: e.memset(self.mixT[:, :, 64:ntok], 0.0), w=["mixT"])
    if _SWASTOP == 5:
        P.add('dve', lambda e: e.memset(self.mixT[:], 0.0), w=['mixT'])
        return
    P.add("dve", lambda e: e.tensor_copy(out=self.kTd[:, :, 0:128], in_=self.kTd[:, :, ntok:ntok + 128]), r=["kTd"], w=["kTd"])
    P.add("dve", lambda e: e.tensor_copy(out=self.vtm[:, 0, :], in_=self.vtm[:, nblk, :]), r=["vtm"], w=["vtm"])
    self.linear_tm_res(D["swa_wo"][0], lambda k, b: self.mixT[:, k, b * 128:(b + 1) * 128], ["mixT"], nblk)


Builder.alloc_swa = alloc_swa
Builder.swa_setup = swa_setup
Builder.swa_init_sample = swa_init_sample
Builder.swa_tile = swa_tile
```

```python
import contextlib
import numpy as np
import concourse.bass as bass
import concourse.mybir as mybir
from concourse.bass_utils import run_bass_kernel_spmd

F32 = mybir.dt.float32
BF16 = mybir.dt.bfloat16
ALU = mybir.AluOpType
AF = mybir.ActivationFunctionType
AX = mybir.AxisListType

import os
_S5STOP = int(os.environ.get('S5STOP', '0'))
_RISK = int(os.environ.get('RISK', '0'))
_ILV = int(os.environ.get('ILV', '3'))
_PRENG = os.environ.get('PRENG', 'dve')
_SWASTOP = int(os.environ.get('SWASTOP', '0'))
_ATT = int(os.environ.get('ATT', '9'))
NCORES = 8
DM = 1024
SEQ = 8192
TT = 512
DEC_SEQ = 16
IN_COLS = 2568
DFF = 2816
EPS = 1e-6


class _Op:
    __slots__ = ("eng", "fn", "deps", "sig", "seq", "dsem", "dval", "n")

    def __init__(self, eng, fn):
        self.eng = eng
        self.fn = fn
        self.deps = []
        self.sig = False
        self.seq = 0
        self.dsem = None
        self.dval = 0
        self.n = 0


class Prog:
    ENGS = ("pe", "act", "dve", "pool", "sp")

    def __init__(self, nc):
        self.nc = nc
        self.q = {e: [] for e in self.ENGS}
        self.st = {}
        self.dcount = {}
        self.groups = {}

    def _expand(self, keys):
        out = []
        for k in keys:
            out.extend(self.groups.get(k, (k,)))
        return out

    def add(self, eng, fn, r=(), w=(), dma=None):
        op = _Op(eng, fn)
        r0, w0 = r, w
        r, w = self._expand(r), self._expand(w)
        deps = []
        for k in r:
            s = self.st.get(k)
            if s is not None and s[0] is not None:
                deps.append(s[0])
        for k in w:
            s = self.st.get(k)
            if s is not None:
                if s[0] is not None:
                    deps.append(s[0])
                deps.extend(s[1])
        is_dma = dma is not None
        op.n = self.nops = getattr(self, "nops", 0) + 1
        rawset = set()
        for k in r:
            s_ = self.st.get(k)
            if s_ is not None and s_[0] is not None:
                rawset.add(id(s_[0]))
        latest = {}
        for d in deps:
            if d.dsem is not None:
                continue
            if d.eng == eng and not is_dma and id(d) not in rawset:
                continue
            if d.eng not in latest or d.n > latest[d.eng].n:
                latest[d.eng] = d
        deps = [d for d in deps if d.dsem is not None or latest.get(d.eng) is d]
        if dma in ("init", "ldst", "ldkv"):
            dma = dma[0] + "_" + w0[0]
        elif dma == "out":
            dma = "o_" + (r0[0] if len(r0) else "dram")
        seen = set()
        for d in deps:
            if id(d) in seen or d is op:
                continue
            seen.add(id(d))
            if d.dsem is None and d.eng == eng and not is_dma:
                if eng == "pe":
                    continue
                israw = False
                for k in r:
                    s = self.st.get(k)
                    if s is not None and s[0] is d:
                        israw = True
                if not israw:
                    continue
            op.deps.append((d, self.dcount[d.dsem] if d.dsem is not None else 0))
            if d.dsem is None:
                d.sig = True
        for k in r:
            s = self.st.setdefault(k, [None, []])
            s[1].append(op)
        for k in w:
            self.st[k] = [op, []]
        if is_dma:
            op.dsem = dma
            self.dcount[dma] = self.dcount.get(dma, 0) + 16
            op.dval = self.dcount[dma]
        self.q[eng].append(op)
        return op

    def emit(self, final_eng="sp"):
        nc = self.nc
        with contextlib.ExitStack() as es:
            esem = {e: es.enter_context(nc.semaphore("S_" + e)) for e in self.ENGS}
            dsem = {n: es.enter_context(nc.semaphore("D_" + n)) for n in self.dcount}
            for e in self.ENGS:
                c = 0
                for op in self.q[e]:
                    if op.sig:
                        c += 1
                        op.seq = c
            block = es.enter_context(nc.Block())

            def run(e, eng):
                waited = {}
                for op in self.q[e]:
                    for d, dv in op.deps:
                        if d.dsem is not None:
                            key, sem, val = ("d", d.dsem), dsem[d.dsem], dv
                        else:
                            key, sem, val = ("e", d.eng), esem[d.eng], d.seq
                        if waited.get(key, 0) >= val:
                            continue
                        waited[key] = val
                        eng.wait_ge(sem, val)
                    ins = op.fn(eng)
                    if op.dsem is not None:
                        ins.then_inc(dsem[op.dsem], 16)
                    elif op.sig:
                        ins.then_inc(esem[e], 1)
                if e == final_eng:
                    for n, cnt in self.dcount.items():
                        eng.wait_ge(dsem[n], cnt)

            @block.tensor
            def _(eng):
                run("pe", eng)

            @block.scalar
            def _(eng):
                run("act", eng)

            @block.vector
            def _(eng):
                run("dve", eng)

            @block.gpsimd
            def _(eng):
                run("pool", eng)

            @block.sync
            def _(eng):
                run("sp", eng)


def _consts():
    c = {}
    c["ident"] = np.eye(128, dtype=np.float32)
    j = np.arange(128)[:, None] % 64
    i = np.arange(64)[None, :]
    c["mUs"] = (i > j).astype(np.float32)
    c["mUi"] = (i >= j).astype(np.float32)
    c["eye64"] = (i == j).astype(np.float32)
    sel = np.zeros((128, 8, 128), np.float32)
    for r in range(8):
        sel[r, r, :] = 1.0
    c["sel"] = sel.reshape(128, 8 * 128)
    c["ones"] = np.ones((128, 128), np.float32)
    m = np.ones((128, 512), np.float32)
    m[:, ::64] = 0.0
    c["cmask"] = m
    selp = np.zeros((128, 4, 2), np.float32)
    for h in range(4):
        selp[h, h, 0] = 1.0
        selp[4 + h, h, 1] = 1.0
    c["selp"] = selp.reshape(128, 8)
    gm = np.zeros((128, 4), np.float32)
    gm[0:4, 0] = 1.0
    gm[4:8, 1] = 1.0
    gm[4:8, 2] = -1.0
    c["gm"] = gm
    mb = np.zeros((128, 4), np.float32)
    mb[64:, 1] = -30000.0
    mb[:64, 2] = -30000.0
    mb[16:, 3] = -30000.0
    c["mb"] = mb
    off = {}
    cols = 0
    for k, v in c.items():
        off[k] = (cols, v.shape[1])
        cols += v.shape[1]
    arr = np.concatenate([c[k] for k in c], axis=1)
    return arr, off


_CARR, _COFF = _consts()

W_SPECS = [
    ("norm_mix", (2, 1024)), ("norm_ffn", (2, 1024)), ("norm_final", (1024,)), ("w_in", (1, 1024, IN_COLS)),
    ("s5_lam_re", (1, 32, 64)), ("s5_lam_im", (1, 32, 64)), ("s5_log_dt", (1, 32)),
    ("s5_b_re", (1, 32, 64, 16)), ("s5_b_im", (1, 32, 64, 16)), ("s5_c_re", (1, 32, 16, 64)), ("s5_c_im", (1, 32, 16, 64)),
    ("s5_d", (1, 512)), ("s5_w_glu", (1, 512, 512)), ("s5_b_glu", (1, 512)),
    ("gdn_conv_w", (1, 4, 1536)), ("gdn_a_log", (1, 4)), ("gdn_dt_bias", (1, 4)), ("gdn_norm_w", (1, 128)),
    ("w_out_ab", (1, 1024, 1024)), ("swa_wq", (1, 1024, 1024)), ("swa_wk", (1, 1024, 256)), ("swa_wv", (1, 1024, 256)),
    ("swa_sinks", (1, 16)), ("swa_wo", (1, 1024, 1024)),
    ("ffn_w_gate", (2, 1024, DFF)), ("ffn_w_up", (2, 1024, DFF)), ("ffn_w_down", (2, DFF, 1024)),
]
IN_SPECS = [
    ("xp", (SEQ, DM)), ("xs", (2, DEC_SEQ, DM)),
    ("st_s5_re", (2, 32, 64)), ("st_s5_im", (2, 32, 64)), ("st_gdn", (2, 4, 128, 128)), ("st_conv", (2, 3, 1536)),
    ("st_k", (2, 128, 256)), ("st_v", (2, 128, 256)),
    ("consts", _CARR.shape),
]
OUT_SPECS = [
    ("y_p", (SEQ, DM)), ("y_s", (2, DEC_SEQ, DM)),
    ("p_s5_re", (32, 64)), ("p_s5_im", (32, 64)), ("p_gdn", (4, 128, 128)), ("p_conv", (3, 1536)),
    ("p_k", (128, 256)), ("p_v", (128, 256)),
    ("s_s5_re", (2, 32, 64)), ("s_s5_im", (2, 32, 64)), ("s_gdn", (2, 4, 128, 128)), ("s_conv", (2, 3, 1536)),
    ("s_k", (2, 128, 256)), ("s_v", (2, 128, 256)),
]


class Builder:
    def __init__(self, stages=99):
        self.stages = stages
        self.nc = bass.Bass("TRN2", target_bir_lowering=False)
        nc = self.nc
        self.D = {}
        for n, s in IN_SPECS + W_SPECS:
            self.D[n] = nc.dram_tensor(n, list(s), F32, kind="ExternalInput").ap()
        for n, s in OUT_SPECS:
            self.D[n] = nc.dram_tensor(n, list(s), F32, kind="ExternalOutput").ap()
        self.es = contextlib.ExitStack()
        self.P = Prog(nc)
        self.P.groups.update({"hT": ["hT_q", "hT_d"], "ys": ["ys_0", "ys_1", "ys_2", "ys_3"], "ub": ["ub_0", "ub_1", "ub_23"],
                              "xres": ["xres0", "xres1", "xres2", "xres3"], "hidA": [f"hid{i}" for i in range(22)], "S32": ["S32p0", "S32p1"], "Sb": ["Sbp0", "Sbp1"], "tok": ["tok0", "tok1"], "egl": ["egl0", "egl1"]})
        self.nslot = 4
        self.slot_i = 0
        self.bank_i = 0

    def sb(self, name, shape, dt):
        return self.es.enter_context(self.nc.sbuf_tensor(name, list(shape), dt))

    def psum(self, name, shape, dt):
        return self.es.enter_context(self.nc.psum_tensor(name, list(shape), dt))

    def alloc(self):
        sb = self.sb
        self.cst = sb("cst", [128, _CARR.shape[1]], F32)
        self.identb = sb("identb", [128, 128], BF16)
        self.xres = sb("xres", [128, 4, DM], F32)
        self.xn = sb("xn", [128, DM], BF16)
        self.ss = sb("ss", [128, 8], F32)
        self.hT = sb("hT", [128, 8, TT], BF16)
        self.nw = sb("nw", [128, 5, 8], F32)
        self.ub = sb("ub", [128, 4, TT], BF16)
        self.qkvb = sb("qkvb", [128, 12, 3 + TT], BF16)
        self.cv32 = sb("cv32", [128, 12, 3], F32)
        self.zs = sb("zs", [128, 4, TT], BF16)
        self.ba = sb("ba", [8, TT], F32)
        self.mixT = sb("mixT", [128, 8, TT], BF16)
        self.hid = sb("hid", [128, 22, TT], BF16)
        self.sg = sb("sg", [128, TT], F32)
        self.slots = [sb(f"wslot{i}", [128, 4096], BF16) for i in range(self.nslot)]
        self.ps = [self.psum(f"ps{i}", [128, 512], F32) for i in range(7)]
        self.pT = self.psum("pT", [128, 1024], BF16)

    def cview(self, name, rows=128):
        o, n = _COFF[name]
        return self.cst[0:rows, o:o + n]

    def wload(self, src3):
        i = self.slot_i
        self.slot_i = (i + 1) % self.nslot
        kc, n = src3.shape[1], src3.shape[2]
        assert kc * n <= 4096
        view = self.slots[i][:, 0:kc * n].rearrange("p (k n) -> p k n", n=n)
        key = f"wslot{i}"
        self.P.add("pool", lambda e: e.dma_start(out=view, in_=src3), w=[key], dma=key)
        return view, key

    def bank(self, lo=0, hi=4):
        b = lo + self.bank_i % (hi - lo)
        self.bank_i += 1
        return b

    def setup(self):
        P, D = self.P, self.D
        P.add("sp", lambda e: e.dma_start(out=self.cst[:], in_=D["consts"][:, :]), w=["cst"], dma="init")
        idv = self.cview("ident")
        P.add("dve", lambda e: e.tensor_copy(out=self.identb[:], in_=idv), r=["cst"], w=["identb"])
        srcs = [D["norm_mix"][0], D["norm_ffn"][0], D["norm_mix"][1], D["norm_ffn"][1]]
        for i, s in enumerate(srcs):
            P.add("sp", lambda e, i=i, s=s: e.dma_start(out=self.nw[:, i, :], in_=s.rearrange("(c p) -> p c", p=128),
                                                        allow_slow_non_contiguous=True), w=["nw"], dma="init")

    def norm_T(self, nblk, widx):
        P = self.P
        P.add("dve", lambda e: e.memset(self.ss[:], 0.0), w=["ss"])
        for b in range(nblk):
            P.add("act", lambda e, b=b: e.activation(out=self.xn[:], in_=self.xres[:, b, :], func=AF.Square,
                                                     accum_out=self.ss[:, b:b + 1]), r=[f"xres{b}"], w=["xn", "ss"])
        P.add("act", lambda e: e.activation(out=self.ss[:, 4:4 + nblk], in_=self.ss[:, 0:nblk], func=AF.Sqrt, scale=1.0 / DM, bias=EPS), r=["ss"], w=["ss"])
        P.add("dve", lambda e: e.reciprocal(out=self.ss[:, 4:4 + nblk], in_=self.ss[:, 4:4 + nblk]), r=["ss"], w=["ss"])
        for b in range(nblk):
            P.add("act", lambda e, b=b: e.activation(out=self.xn[:], in_=self.xres[:, b, :], func=AF.Copy,
                                                     scale=self.ss[:, 4 + b:5 + b]), r=[f"xres{b}", "ss"], w=["xn"])
            for c in range(8):
                P.add("pe", lambda e, c=c: e.transpose(out=self.pT[:, c * 128:(c + 1) * 128], in_=self.xn[:, c * 128:(c + 1) * 128],
                                                       identity=self.identb[:]), r=["xn", "identb"], w=["pT"])
            P.add("dve", lambda e, b=b: e.tensor_tensor(
                out=self.hT[:, :, b * 128:(b + 1) * 128], in0=self.pT[:].rearrange("p (c t) -> p c t", t=128),
                in1=self.nw[:, widx, :].unsqueeze(2).to_broadcast([128, 8, 128]), op=ALU.mult), r=["pT", "nw"], w=["hT"])

    def linear_fm(self, W2, col0, ncols, rhs_fn, rkeys, ntok, evac, piece=512, banks=(0, 4)):
        P = self.P
        K = W2.shape[0]
        kc = K // 128
        W3 = W2.rearrange("(c p) n -> p c n", p=128)
        for p0 in range(col0, col0 + ncols, piece):
            pc = min(piece, col0 + ncols - p0)
            view, key = self.wload(W3[:, :, p0:p0 + pc])
            for c0 in range(0, pc, 128):
                m = min(128, pc - c0)
                b = self.bank(*banks)
                for k in range(kc):
                    P.add("pe", lambda e, b=b, k=k, c0=c0, m=m, view=view: e.matmul(
                        self.ps[b][0:m, 0:ntok], lhsT=view[:, k, c0:c0 + m], rhs=rhs_fn(k), start=(k == 0), stop=(k == kc - 1)),
                        r=[key] + rkeys, w=[f"ps{b}"])
                evac((p0 + c0) // 128, m, b)

    def linear_tm_res(self, W2, act_fn, akeys, nblk):
        P = self.P
        K = W2.shape[0]
        kc = K // 128
        W3 = W2.rearrange("(c p) n -> p c n", p=128)
        sets = [[0, 1, 2, 3], [4, 5, 6, 3]]
        for ch in range(2):
            banks = sets[ch]
            for k0 in range(0, kc, 8):
                k1 = min(kc, k0 + 8)
                view, key = self.wload(W3[:, k0:k1, ch * 512:(ch + 1) * 512])
                for b in range(nblk):
                    for k in range(k0, k1):
                        P.add("pe", lambda e, b=b, k=k, k0=k0, view=view, banks=banks: e.matmul(
                            self.ps[banks[b]][:, :], lhsT=act_fn(k, b), rhs=view[:, k - k0, :], start=(k == 0), stop=(k == kc - 1)),
                            r=[key] + akeys, w=[f"ps{banks[b]}"])
            for b in range(nblk):
                P.add("dve", lambda e, b=b, ch=ch, banks=banks: e.tensor_tensor(
                    out=self.xres[:, b, ch * 512:(ch + 1) * 512], in0=self.xres[:, b, ch * 512:(ch + 1) * 512],
                    in1=self.ps[banks[b]][:, :], op=ALU.add), r=[f"xres{b}", f"ps{banks[b]}"], w=[f"xres{b}"])

    def ffn(self, layer, nblk, ntok):
        P, D = self.P, self.D
        self.norm_T(nblk, 1 + 2 * layer)
        Wg, Wu, Wd = D["ffn_w_gate"][layer], D["ffn_w_up"][layer], D["ffn_w_down"][layer]
        Wg3 = Wg.rearrange("(c p) n -> p c n", p=128)
        Wu3 = Wu.rearrange("(c p) n -> p c n", p=128)
        for p0 in range(0, DFF, 512):
            pc = min(512, DFF - p0)
            vg, kg = self.wload(Wg3[:, :, p0:p0 + pc])
            vu, ku = self.wload(Wu3[:, :, p0:p0 + pc])
            for c0 in range(0, pc, 128):
                f = (p0 + c0) // 128
                bg = self.bank(0, 6)
                bu = self.bank(0, 6)
                for k in range(8):
                    P.add("pe", lambda e, k=k, c0=c0, bg=bg, vg=vg: e.matmul(
                        self.ps[bg][:, 0:ntok], lhsT=vg[:, k, c0:c0 + 128], rhs=self.hT[:, k, 0:ntok], start=(k == 0), stop=(k == 7)),
                        r=[kg, "hT"], w=[f"ps{bg}"])
                for k in range(8):
                    P.add("pe", lambda e, k=k, c0=c0, bu=bu, vu=vu: e.matmul(
                        self.ps[bu][:, 0:ntok], lhsT=vu[:, k, c0:c0 + 128], rhs=self.hT[:, k, 0:ntok], start=(k == 0), stop=(k == 7)),
                        r=[ku, "hT"], w=[f"ps{bu}"])
                P.add("act", lambda e, bg=bg: e.activation(out=self.sg[:, 0:ntok], in_=self.ps[bg][:, 0:ntok], func=AF.Silu),
                      r=[f"ps{bg}"], w=["sg"])
                P.add("dve", lambda e, bu=bu, f=f: e.tensor_tensor(out=self.hid[:, f, 0:ntok], in0=self.sg[:, 0:ntok],
                                                                    in1=self.ps[bu][:, 0:ntok], op=ALU.mult),
                      r=["sg", f"ps{bu}"], w=[f"hid{f}"])
        hk = [f"hid{f}" for f in range(22)]
        self.linear_tm_res(Wd, lambda k, b: self.hid[:, k, b * 128:(b + 1) * 128], hk, nblk)

    def alloc_l0(self):
        sb = self.sb
        self.s5p = sb("s5p", [128, 16, 24], F32)
        self.Bt = sb("Bt", [128, 2, 16, 128], BF16)
        self.Ct = sb("Ct", [128, 2, 16, 32], BF16)
        self.rot = sb("rot", [128, 2, 16, 64], F32)
        self.Bt1 = sb("Bt1", [128, 2, 16, 128], BF16)
        self.Ct1 = sb("Ct1", [128, 2, 16, 32], BF16)
        self.K0T = sb("K0T", [128, 4, 128], BF16)
        self.xst = sb("xst", [128, 2, 16], F32)
        self.s5tt = sb("s5tt", [128, 6, 512], F32)
        self.s5t = [self.s5tt[:, i, :] for i in range(6)]
        self.g5tt = sb("g5tt", [128, 6, 512], F32)
        self.g5t = [self.g5tt[:, i, :] for i in range(6)]
        self.nfin = self.s5tt[:, 0:2, :].rearrange("p a t -> p (a t)")
        self.xrb = sb("xrb", [128, 2, 2, 512], BF16)
        self.ys = sb("ys", [128, 4, TT], F32)
        self.Bz = self.xres[:].rearrange("p b d -> p (b d)").rearrange("p (i g c) -> p i g c", i=2, g=16)
        self.Cz = self.ys[:].rearrange("p a t -> p (a t)")[:, 0:1024].rearrange("p (i g c) -> p i g c", i=2, g=16)
        self.dcol = sb("dcol", [128, 4], F32)
        self.bglu = sb("bglu", [128, 4], F32)
        self.Braw = self.sg[:].rearrange("p (i g c) -> p i g c", i=2, g=16)
        self.zb = self.ub

    def s5_setup(self):
        P, D = self.P, self.D
        p = self.s5p
        PI = float(np.pi)
        for nm, col in (("s5_lam_re", 0), ("s5_lam_im", 1)):
            P.add("sp", lambda e, nm=nm, col=col: e.dma_start(
                out=p[:, :, col], in_=D[nm][0].rearrange("(gp gl) p -> (gl p) gp", gl=2), allow_slow_non_contiguous=True),
                w=["s5p"], dma="init")
        ld = D["s5_log_dt"][0].rearrange("(gp gl) -> gl gp", gl=2)
        for gl in range(2):
            P.add("sp", lambda e, gl=gl: e.dma_start(out=p[gl * 64:(gl + 1) * 64, :, 2], in_=ld[gl].partition_broadcast(64),
                                                    allow_slow_non_contiguous=True), w=["s5p"], dma="init")

        def c(i):
            return p[:, :, i]
        k = ["s5p"]
        P.add("act", lambda e: e.activation(out=c(3), in_=c(2), func=AF.Exp), r=k, w=k)
        P.add("dve", lambda e: e.tensor_tensor(out=c(4), in0=c(0), in1=c(3), op=ALU.mult), r=k, w=k)
        P.add("dve", lambda e: e.tensor_tensor(out=c(5), in0=c(1), in1=c(3), op=ALU.mult), r=k, w=k)
        P.add("act", lambda e: e.activation(out=c(6), in_=c(4), func=AF.Exp), r=k, w=k)
        P.add("act", lambda e: e.activation(out=c(7), in_=c(5), func=AF.Sin, scale=1.0 / 16), r=k, w=k)
        P.add("act", lambda e: e.activation(out=c(9), in_=c(5), func=AF.Sin, scale=1.0 / 8), r=k, w=k)
        P.add("dve", lambda e: e.tensor_tensor(out=c(8), in0=c(7), in1=c(7), op=ALU.mult), r=k, w=k)
        P.add("dve", lambda e: e.tensor_scalar(out=c(10), in0=c(8), scalar1=-2.0, scalar2=1.0, op0=ALU.mult, op1=ALU.add), r=k, w=k)
        for _ in range(3):
            P.add("dve", lambda e: e.tensor_tensor(out=c(15), in0=c(10), in1=c(10), op=ALU.mult), r=k, w=k)
            P.add("dve", lambda e: e.tensor_tensor(out=c(16), in0=c(9), in1=c(9), op=ALU.mult), r=k, w=k)
            P.add("dve", lambda e: e.tensor_tensor(out=c(8), in0=c(9), in1=c(10), op=ALU.mult), r=k, w=k)
            P.add("dve", lambda e: e.tensor_scalar(out=c(9), in0=c(8), scalar1=2.0, scalar2=None, op0=ALU.mult), r=k, w=k)
            P.add("dve", lambda e: e.tensor_tensor(out=c(10), in0=c(15), in1=c(16), op=ALU.subtract), r=k, w=k)
        P.add("dve", lambda e: e.tensor_tensor(out=c(11), in0=c(6), in1=c(10), op=ALU.mult), r=k, w=k)
        P.add("dve", lambda e: e.tensor_tensor(out=c(12), in0=c(6), in1=c(9), op=ALU.mult), r=k, w=k)
        P.add("dve", lambda e: e.tensor_scalar(out=c(13), in0=c(11), scalar1=-1.0, scalar2=None, op0=ALU.add), r=k, w=k)
        P.add("dve", lambda e: e.tensor_tensor(out=c(14), in0=c(0), in1=c(0), op=ALU.mult), r=k, w=k)
        P.add("dve", lambda e: e.tensor_tensor(out=c(15), in0=c(1), in1=c(1), op=ALU.mult), r=k, w=k)
        P.add("dve", lambda e: e.tensor_tensor(out=c(14), in0=c(14), in1=c(15), op=ALU.add), r=k, w=k)
        P.add("dve", lambda e: e.reciprocal(out=c(14), in_=c(14)), r=k, w=k)
        P.add("dve", lambda e: e.tensor_tensor(out=c(15), in0=c(13), in1=c(0), op=ALU.mult), r=k, w=k)
        P.add("dve", lambda e: e.tensor_tensor(out=c(16), in0=c(12), in1=c(1), op=ALU.mult), r=k, w=k)
        P.add("dve", lambda e: e.tensor_tensor(out=c(15), in0=c(15), in1=c(16), op=ALU.add), r=k, w=k)
        P.add("dve", lambda e: e.tensor_tensor(out=c(17), in0=c(15), in1=c(14), op=ALU.mult), r=k, w=k)
        P.add("dve", lambda e: e.tensor_tensor(out=c(15), in0=c(12), in1=c(0), op=ALU.mult), r=k, w=k)
        P.add("dve", lambda e: e.tensor_tensor(out=c(16), in0=c(13), in1=c(1), op=ALU.mult), r=k, w=k)
        P.add("dve", lambda e: e.tensor_tensor(out=c(15), in0=c(15), in1=c(16), op=ALU.subtract), r=k, w=k)
        P.add("dve", lambda e: e.tensor_tensor(out=c(18), in0=c(15), in1=c(14), op=ALU.mult), r=k, w=k)
        if _S5STOP == 1:
            return
        for i, nm in enumerate(("s5_b_re", "s5_b_im")):
            P.add("sp", lambda e, i=i, nm=nm: e.dma_start(out=self.Braw[:, i, :, :],
                                                          in_=D[nm][0].rearrange("(gp gl) p c -> (gl p) gp c", gl=2)),
                  w=["sg"], dma="init")
        P.add("dve", lambda e: e.memset(self.Bz[:], 0.0), w=["xres"])
        P.add("dve", lambda e: e.memset(self.Cz[:], 0.0), w=["ys"])
        fre = p[:, :, 17:18].to_broadcast([128, 16, 16])
        fim = p[:, :, 18:19].to_broadcast([128, 16, 16])
        t0 = self.s5t[0][:, 0:256].rearrange("p (g c) -> p g c", c=16)
        t1 = self.s5t[1][:, 0:256].rearrange("p (g c) -> p g c", c=16)
        t2 = self.s5t[2][:, 0:256].rearrange("p (g c) -> p g c", c=16)
        bb = [self.s5t[3][:, 0:256].rearrange("p (g c) -> p g c", c=16), self.s5t[4][:, 0:256].rearrange("p (g c) -> p g c", c=16)]
        kk = ["s5p", "sg"]
        P.add("dve", lambda e: e.tensor_tensor(out=t0, in0=self.Braw[:, 0], in1=fre, op=ALU.mult), r=kk, w=["s5t0"])
        P.add("dve", lambda e: e.tensor_tensor(out=t1, in0=self.Braw[:, 1], in1=fim, op=ALU.mult), r=kk, w=["s5t1"])
        P.add("dve", lambda e: e.tensor_tensor(out=bb[0], in0=t0, in1=t1, op=ALU.subtract), r=["s5t0", "s5t1"], w=["s5t3"])
        for gl in range(2):
            for r in range(4):
                P.add("dve", lambda e, gl=gl, r=r: e.tensor_copy(
                    out=self.Bz[gl * 64:(gl + 1) * 64, 0].rearrange("p (cb r) c -> p cb r c", r=4)[:, :, r, 32 * r + gl * 16:32 * r + gl * 16 + 16],
                    in_=bb[0][gl * 64:(gl + 1) * 64].rearrange("p (cb r) c -> p cb r c", r=4)[:, :, r, :]), r=["s5t3"], w=["xres"])
        P.add("dve", lambda e: e.tensor_tensor(out=t0, in0=self.Braw[:, 1], in1=fre, op=ALU.mult), r=kk, w=["s5t0"])
        P.add("dve", lambda e: e.tensor_tensor(out=t1, in0=self.Braw[:, 0], in1=fim, op=ALU.mult), r=kk, w=["s5t1"])
        P.add("dve", lambda e: e.tensor_tensor(out=bb[1], in0=t0, in1=t1, op=ALU.add), r=["s5t0", "s5t1"], w=["s5t4"])
        for gl in range(2):
            for r in range(4):
                P.add("dve", lambda e, gl=gl, r=r: e.tensor_copy(
                    out=self.Bz[gl * 64:(gl + 1) * 64, 1].rearrange("p (cb r) c -> p cb r c", r=4)[:, :, r, 32 * r + gl * 16:32 * r + gl * 16 + 16],
                    in_=bb[1][gl * 64:(gl + 1) * 64].rearrange("p (cb r) c -> p cb r c", r=4)[:, :, r, :]), r=["s5t4"], w=["xres"])
        if _S5STOP == 2:
            return
        idf = self.cview("ident")
        for i in range(2):
            for gp in range(16):
                P.add("pe", lambda e, i=i, gp=gp: e.matmul(self.ps[0][:, 0:128], lhsT=self.Bz[:, i, gp, :], rhs=idf, start=True, stop=True),
                      r=["xres", "cst"], w=["ps0"])
                P.add("dve", lambda e, i=i, gp=gp: e.tensor_copy(out=self.Bt[:, i, gp, :], in_=self.ps[0][:, 0:128]), r=["ps0"], w=["Bt"])
        if _S5STOP == 3:
            return
        for i, nm in enumerate(("s5_c_re", "s5_c_im")):
            for g in range(32):
                gp, gl = g // 2, g % 2
                P.add("sp", lambda e, i=i, nm=nm, g=g, gp=gp, gl=gl: e.dma_start(
                    out=self.Cz[gl * 64:(gl + 1) * 64, i, gp, gl * 16:(gl + 1) * 16], in_=D[nm][0][g].rearrange("c p -> p c"),
                    allow_slow_non_contiguous=True), w=["ys"], dma="init")
        P.add("dve", lambda e: e.tensor_copy(out=self.Ct[:, 0], in_=self.Cz[:, 0]), r=["ys"], w=["Ct"])
        P.add("dve", lambda e: e.tensor_scalar(out=self.Ct[:, 1], in0=self.Cz[:, 1], scalar1=-1.0, scalar2=None, op0=ALU.mult), r=["ys"], w=["Ct"])
        if _S5STOP == 4:
            return
        scrF = self.hid[:].rearrange("p a b -> p (a b)").bitcast(F32)
        rotfull = scrF[:, 0:4096].rearrange("p (i g t) -> p i g t", i=2, g=16)
        cs, sn = rotfull[:, 0], rotfull[:, 1]
        P.add("dve", lambda e: e.tensor_copy(out=cs[:, :, 0:1], in_=p[:, :, 10:11]), r=k, w=["hidA"])
        P.add("dve", lambda e: e.tensor_copy(out=sn[:, :, 0:1], in_=p[:, :, 9:10]), r=k, w=["hidA"])
        L = 1
        while L < 128:
            c1 = cs[:, :, L - 1:L].to_broadcast([128, 16, L])
            s1 = sn[:, :, L - 1:L].to_broadcast([128, 16, L])
            sc0 = self.g5tt[:, 0:4, :].rearrange("p a (g t) -> p (a g) t", g=4)[:, :, 0:L]
            sc1 = self.g5tt[:, 4:6, :].rearrange("p a (g t) -> p (a g) t", g=8)[:, :, 0:L]
            P.add("dve", lambda e, L=L, c1=c1, sc0=sc0: e.tensor_tensor(out=sc0, in0=cs[:, :, 0:L], in1=c1, op=ALU.mult), r=["hidA"], w=["g5t0"])
            P.add("dve", lambda e, L=L, s1=s1, sc1=sc1: e.tensor_tensor(out=sc1, in0=sn[:, :, 0:L], in1=s1, op=ALU.mult), r=["hidA"], w=["g5t4", "g5t5"])
            P.add("dve", lambda e, L=L, sc0=sc0, sc1=sc1: e.tensor_tensor(out=cs[:, :, L:2 * L], in0=sc0, in1=sc1, op=ALU.subtract),
                  r=["g5t0", "g5t4", "g5t5", "hidA"], w=["hidA"])
            P.add("dve", lambda e, L=L, s1=s1, sc0=sc0: e.tensor_tensor(out=sc0, in0=cs[:, :, 0:L], in1=s1, op=ALU.mult), r=["hidA"], w=["g5t0"])
            P.add("dve", lambda e, L=L, c1=c1, sc1=sc1: e.tensor_tensor(out=sc1, in0=sn[:, :, 0:L], in1=c1, op=ALU.mult), r=["hidA"], w=["g5t4", "g5t5"])
            P.add("dve", lambda e, L=L, sc0=sc0, sc1=sc1: e.tensor_tensor(out=sn[:, :, L:2 * L], in0=sc0, in1=sc1, op=ALU.add),
                  r=["g5t0", "g5t4", "g5t5", "hidA"], w=["hidA"])
            L *= 2
        for i in range(2):
            P.add("dve", lambda e, i=i: e.tensor_copy(out=self.rot[:, i], in_=rotfull[:, i].rearrange("p g (n two) -> p g n two", two=2)[:, :, :, 1]),
                  r=["hidA"], w=["rot"])
        P.add("dve", lambda e: e.tensor_tensor(out=c(22), in0=c(6), in1=c(6), op=ALU.mult), r=k, w=k)
        Czp = scrF[:, 0:4096].rearrange("p (i g c) -> p i g c", i=2, g=16)
        P.add("dve", lambda e: e.memset(Czp, 0.0), w=["hidA"])
        for i in range(2):
            for gl in range(2):
                for r in range(4):
                    P.add("dve", lambda e, i=i, gl=gl, r=r: e.tensor_scalar(
                        out=Czp[gl * 64:(gl + 1) * 64, i].rearrange("p (cb r) c -> p cb r c", r=4)[:, :, r, 32 * r + gl * 16:32 * r + gl * 16 + 16],
                        in0=self.Cz[gl * 64:(gl + 1) * 64, i].rearrange("p (cb r) c -> p cb r c", r=4)[:, :, r, gl * 16:gl * 16 + 16],
                        scalar1=(1.0 if i == 0 else -1.0), scalar2=None, op0=ALU.mult), r=["ys"], w=["hidA"])
        for cb in range(4):
            n = 0
            for r in range(4):
                for i in range(2):
                    P.add("pe", lambda e, cb=cb, r=r, i=i, n=n: e.matmul(self.ps[1][:, 0:128], lhsT=self.Bz[:, i, 4 * cb + r, :], rhs=Czp[:, i, 4 * cb + r, :],
                                                                       start=(n == 0), stop=(n == 7)), r=["xres", "hidA"], w=["ps1"])
                    n += 1
            P.add("dve", lambda e, cb=cb: e.tensor_copy(out=self.K0T[:, cb, :], in_=self.ps[1][:, 0:128]), r=["ps1"], w=["K0T"])
        lre = p[:, :, 11:12].to_broadcast([128, 16, 32])
        lim = p[:, :, 12:13].to_broadcast([128, 16, 32])
        u0 = self.s5t[0][:, 0:512].rearrange("p (g c) -> p g c", c=32)
        u1 = self.s5t[1][:, 0:512].rearrange("p (g c) -> p g c", c=32)
        P.add("dve", lambda e: e.tensor_tensor(out=u0, in0=self.Cz[:, 0], in1=lre, op=ALU.mult), r=["ys", "s5p"], w=["s5t0"])
        P.add("dve", lambda e: e.tensor_tensor(out=u1, in0=self.Cz[:, 1], in1=lim, op=ALU.mult), r=["ys", "s5p"], w=["s5t1"])
        P.add("dve", lambda e: e.tensor_tensor(out=self.Ct1[:, 0], in0=u0, in1=u1, op=ALU.subtract), r=["s5t0", "s5t1"], w=["Ct1"])
        P.add("dve", lambda e: e.tensor_tensor(out=u0, in0=self.Cz[:, 0], in1=lim, op=ALU.mult), r=["ys", "s5p"], w=["s5t0"])
        P.add("dve", lambda e: e.tensor_tensor(out=u1, in0=self.Cz[:, 1], in1=lre, op=ALU.mult), r=["ys", "s5p"], w=["s5t1"])
        P.add("dve", lambda e: e.tensor_tensor(out=u0, in0=u0, in1=u1, op=ALU.add), r=["s5t0", "s5t1"], w=["s5t0"])
        P.add("dve", lambda e: e.tensor_scalar(out=self.Ct1[:, 1], in0=u0, scalar1=-1.0, scalar2=None, op0=ALU.mult), r=["s5t0"], w=["Ct1"])
        lre16 = p[:, :, 11:12].to_broadcast([128, 16, 16])
        lim16 = p[:, :, 12:13].to_broadcast([128, 16, 16])
        t0 = self.s5t[0][:, 0:256].rearrange("p (g c) -> p g c", c=16)
        t1 = self.s5t[1][:, 0:256].rearrange("p (g c) -> p g c", c=16)
        t2 = self.s5t[2][:, 0:256].rearrange("p (g c) -> p g c", c=16)
        P.add("dve", lambda e: e.memset(self.Bz[:], 0.0), w=["xres"])
        for i in range(2):
            a_, b_ = (bb[0], bb[1]) if i == 0 else (bb[1], bb[0])
            P.add("dve", lambda e, a_=a_: e.tensor_tensor(out=t0, in0=a_, in1=lre16, op=ALU.mult), r=["s5t3", "s5t4", "s5p"], w=["s5t0"])
            P.add("dve", lambda e, b_=b_: e.tensor_tensor(out=t1, in0=b_, in1=lim16, op=ALU.mult), r=["s5t3", "s5t4", "s5p"], w=["s5t1"])
            P.add("dve", lambda e, i=i: e.tensor_tensor(out=t2, in0=t0, in1=t1, op=(ALU.subtract if i == 0 else ALU.add)), r=["s5t0", "s5t1"], w=["s5t2"])
            for gl in range(2):
                for r in range(4):
                    P.add("dve", lambda e, i=i, gl=gl, r=r: e.tensor_copy(
                        out=self.Bz[gl * 64:(gl + 1) * 64, i].rearrange("p (cb r) c -> p cb r c", r=4)[:, :, r, 32 * r + gl * 16:32 * r + gl * 16 + 16],
                        in_=t2[gl * 64:(gl + 1) * 64].rearrange("p (cb r) c -> p cb r c", r=4)[:, :, r, :]), r=["s5t2"], w=["xres"])
        for i in range(2):
            for gp in range(16):
                P.add("pe", lambda e, i=i, gp=gp: e.matmul(self.ps[0][:, 0:128], lhsT=self.Bz[:, i, gp, :], rhs=idf, start=True, stop=True),
                      r=["xres", "cst"], w=["ps0"])
                P.add("dve", lambda e, i=i, gp=gp: e.tensor_copy(out=self.Bt1[:, i, gp, :], in_=self.ps[0][:, 0:128]), r=["ps0"], w=["Bt1"])
        if _S5STOP == 5:
            return
        P.add("sp", lambda e: e.dma_start(out=self.dcol[:], in_=D["s5_d"][0].rearrange("(c p) -> p c", p=128), allow_slow_non_contiguous=True),
              w=["dcol"], dma="init")
        P.add("sp", lambda e: e.dma_start(out=self.bglu[:], in_=D["s5_b_glu"][0].rearrange("(c p) -> p c", p=128), allow_slow_non_contiguous=True),
              w=["bglu"], dma="init")

    def s5_iter(self, sub, cb, st, ntok, nval):
        P = self.P
        TS, NS = 128, 64
        t0 = sub * TS
        T = self.s5t if st == 0 else self.g5t
        tk = [("s5t" if st == 0 else "g5t") + str(i) for i in range(6)]
        bR, bI, bY = (0, 1, 4) if st == 0 else (2, 3, 5)
        kx = f"xrb{st}"
        xv = [self.xrb[:, st, i, 0:4 * (NS + 1)].rearrange("p (g n) -> p g n", n=NS + 1) for i in range(2)]
        cs = self.rot[:, 0, 4 * cb:4 * cb + 4, :].rearrange("p g t -> p (g t)")
        sn = self.rot[:, 1, 4 * cb:4 * cb + 4, :].rearrange("p g t -> p (g t)")
        u2 = self.ub[:, cb, t0:t0 + TS].rearrange("p (n two) -> p n two", two=2)
        ue, uo = u2[:, :, 0], u2[:, :, 1]
        Wd = 4 * NS
        for i in range(2):
            P.add("act", lambda e, i=i: e.activation(out=xv[i][:, :, 0:1], in_=self.xst[:, i, 4 * cb:4 * cb + 4].unsqueeze(2), func=AF.Copy),
                  r=["xst"], w=[kx])
        for i, bk in ((0, bR), (1, bI)):
            for r in range(4):
                gp = 4 * cb + r
                P.add("pe", lambda e, i=i, bk=bk, r=r, gp=gp: e.matmul(self.ps[bk][:, r * NS:(r + 1) * NS], lhsT=self.Bt1[:, i, gp, :], rhs=ue,
                                                                       start=True, stop=False), r=["Bt1", "ub"], w=[f"ps{bk}"])
                P.add("pe", lambda e, i=i, bk=bk, r=r, gp=gp: e.matmul(self.ps[bk][:, r * NS:(r + 1) * NS], lhsT=self.Bt[:, i, gp, :], rhs=uo,
                                                                       start=False, stop=True), r=["Bt", "ub"], w=[f"ps{bk}"])
        yield
        pR, pI = self.ps[bR][:, 0:Wd], self.ps[bI][:, 0:Wd]
        kR, kI = f"ps{bR}", f"ps{bI}"
        A = [t[:, 0:Wd] for t in T]
        P.add("dve", lambda e: e.tensor_tensor(out=A[0], in0=pR, in1=cs, op=ALU.mult), r=[kR, "rot"], w=[tk[0]])
        yield
        P.add("dve", lambda e: e.tensor_tensor(out=A[1], in0=pI, in1=sn, op=ALU.mult), r=[kI, "rot"], w=[tk[1]])
        yield
        P.add("dve", lambda e: e.tensor_tensor(out=A[2], in0=pI, in1=cs, op=ALU.mult), r=[kI, "rot"], w=[tk[2]])
        yield
        P.add("dve", lambda e: e.tensor_tensor(out=A[3], in0=pR, in1=sn, op=ALU.mult), r=[kR, "rot"], w=[tk[3]])
        yield
        P.add("dve", lambda e: e.tensor_tensor(out=A[0], in0=A[0], in1=A[1], op=ALU.add), r=[tk[0], tk[1]], w=[tk[0]])
        yield
        P.add("dve", lambda e: e.tensor_tensor(out=A[2], in0=A[2], in1=A[3], op=ALU.subtract), r=[tk[2], tk[3]], w=[tk[2]])
        yield
        for r in range(4):
            gp = 4 * cb + r
            rb = self.s5p[:, gp, 22:23].to_broadcast([128, NS])
            sl = slice(r * NS, (r + 1) * NS)
            P.add("dve", lambda e, rb=rb, sl=sl, gp=gp: e.tensor_tensor_scan(out=T[4][:, sl], data0=rb, data1=T[0][:, sl], initial=self.xst[:, 0, gp:gp + 1],
                                                                             op0=ALU.mult, op1=ALU.add), r=["s5p", tk[0], "xst"], w=[tk[4]])
            yield
            P.add("dve", lambda e, rb=rb, sl=sl, gp=gp: e.tensor_tensor_scan(out=T[5][:, sl], data0=rb, data1=T[2][:, sl], initial=self.xst[:, 1, gp:gp + 1],
                                                                             op0=ALU.mult, op1=ALU.add), r=["s5p", tk[2], "xst"], w=[tk[5]])
            yield
        P.add("dve", lambda e: e.tensor_tensor(out=A[0], in0=A[4], in1=cs, op=ALU.mult), r=[tk[4], "rot"], w=[tk[0]])
        yield
        P.add("dve", lambda e: e.tensor_tensor(out=A[1], in0=A[5], in1=sn, op=ALU.mult), r=[tk[5], "rot"], w=[tk[1]])
        yield
        P.add("dve", lambda e: e.tensor_tensor(out=A[2], in0=A[4], in1=sn, op=ALU.mult), r=[tk[4], "rot"], w=[tk[2]])
        yield
        P.add("dve", lambda e: e.tensor_tensor(out=A[3], in0=A[5], in1=cs, op=ALU.mult), r=[tk[5], "rot"], w=[tk[3]])
        yield
        P.add("dve", lambda e: e.tensor_tensor(out=A[0], in0=A[0], in1=A[1], op=ALU.subtract), r=[tk[0], tk[1]], w=[tk[0]])
        yield
        P.add("dve", lambda e: e.tensor_tensor(out=A[2], in0=A[2], in1=A[3], op=ALU.add), r=[tk[2], tk[3]], w=[tk[2]])
        yield
        for i, tt in ((0, 0), (1, 2)):
            P.add("act", lambda e, i=i, tt=tt: e.activation(out=xv[i][:, :, 1:NS + 1], in_=A[tt].rearrange("p (g n) -> p g n", n=NS), func=AF.Copy),
                  r=[tk[tt]], w=[kx])
        lastn = NS - 1
        if nval < ntok:
            lastn = (nval - 1 - t0 - 1) // 2
        if 0 <= lastn < NS:
            for i, tt in ((0, 0), (1, 2)):
                P.add("dve", lambda e, i=i, tt=tt, lastn=lastn: e.tensor_copy(
                    out=self.xst[:, i, 4 * cb:4 * cb + 4], in_=A[tt].rearrange("p (g n) -> p g n", n=NS)[:, :, lastn]), r=[tk[tt], kx], w=["xst"])
                yield
        pY = self.ps[bY]
        for r in range(4):
            gp = 4 * cb + r
            for i in range(2):
                P.add("pe", lambda e, r=r, gp=gp, i=i: e.matmul(pY[32 * r:32 * r + 32, 0:NS], lhsT=self.Ct[:, i, gp, :], rhs=xv[i][:, r, 1:NS + 1],
                                                                start=(i == 0), stop=(i == 1), tile_position=(0, 32 * r)), r=["Ct", kx], w=[f"ps{bY}"])
        P.add("pe", lambda e: e.matmul(pY[:, NS:2 * NS], lhsT=self.K0T[:, cb, :], rhs=ue, start=True, stop=False), r=["K0T", "ub"], w=[f"ps{bY}"])
        for r in range(4):
            gp = 4 * cb + r
            for i in range(2):
                P.add("pe", lambda e, r=r, gp=gp, i=i: e.matmul(pY[32 * r:32 * r + 32, NS:2 * NS], lhsT=self.Ct1[:, i, gp, :], rhs=xv[i][:, r, 0:NS],
                                                                start=False, stop=(i == 1), tile_position=(0, 32 * r)), r=["Ct1", kx], w=[f"ps{bY}"])
        yield
        y2 = self.ys[:, cb, t0:t0 + TS].rearrange("p (n two) -> p n two", two=2)
        P.add("dve", lambda e: e.scalar_tensor_tensor(out=y2[:, :, 1], in0=uo, scalar=self.dcol[:, cb:cb + 1], in1=pY[:, 0:NS],
                                                      op0=ALU.mult, op1=ALU.add), r=["ub", "dcol", f"ps{bY}"], w=[f"ys_{cb}"])
        yield
        P.add("dve", lambda e: e.scalar_tensor_tensor(out=y2[:, :, 0], in0=ue, scalar=self.dcol[:, cb:cb + 1], in1=pY[:, NS:2 * NS],
                                                      op0=ALU.mult, op1=ALU.add), r=["ub", "dcol", f"ps{bY}"], w=[f"ys_{cb}"])
        yield

    def s5_tile(self, ntok, nval, state_out=None):
        P, D = self.P, self.D
        nsub = ntok // 128
        for sub in range(nsub):
            for cb0 in (0, 2):
                gens = [self.s5_iter(sub, cb0, 0, ntok, nval), self.s5_iter(sub, cb0 + 1, 1, ntok, nval)]
                alive = [True, True]
                while any(alive):
                    for gi in range(2):
                        if alive[gi]:
                            try:
                                next(gens[gi])
                            except StopIteration:
                                alive[gi] = False
        if state_out is not None:
            for i in range(2):
                dst = state_out[i].rearrange("(gp gl) p -> (gl p) gp", gl=2)
                P.add("sp", lambda e, i=i, dst=dst: e.dma_start(out=dst, in_=self.xst[:, i, :], allow_slow_non_contiguous=True),
                      r=["xst"], w=[f"so{i}"], dma="out")
        ys_ = [self.ys[:, cb, 0:ntok] for cb in range(4)]
        ts_ = [self.s5t[cb][:, 0:ntok] for cb in range(4)]
        for cb in range(4):
            P.add("dve", lambda e, cb=cb: e.tensor_tensor(out=ts_[cb], in0=ys_[cb], in1=ys_[cb], op=ALU.mult), r=[f"ys_{cb}"], w=[f"s5t{cb}"])
        for cb in range(4):
            P.add("dve", lambda e, cb=cb: e.tensor_scalar(out=ts_[cb], in0=ts_[cb], scalar1=0.044715, scalar2=1.0, op0=ALU.mult, op1=ALU.add),
                  r=[f"s5t{cb}"], w=[f"s5t{cb}"])
        for cb in range(4):
            P.add("dve", lambda e, cb=cb: e.tensor_tensor(out=ts_[cb], in0=ts_[cb], in1=ys_[cb], op=ALU.mult), r=[f"s5t{cb}", f"ys_{cb}"], w=[f"s5t{cb}"])
        for cb in range(4):
            P.add("act", lambda e, cb=cb: e.activation(out=ts_[cb], in_=ts_[cb], func=AF.Sigmoid, scale=1.5957691216), r=[f"s5t{cb}"], w=[f"s5t{cb}"])
        for cb in range(4):
            P.add("dve", lambda e, cb=cb: e.tensor_tensor(out=ys_[cb], in0=ts_[cb], in1=ys_[cb], op=ALU.mult), r=[f"s5t{cb}", f"ys_{cb}"], w=[f"ys_{cb}"])
        for cb in range(4):
            P.add("act", lambda e, cb=cb: e.activation(out=self.zb[:, cb, 0:ntok], in_=ys_[cb], func=AF.Copy), r=[f"ys_{cb}"], w=["ub"])

        def evac(ci, m, b):
            P.add("act", lambda e: e.activation(out=self.sg[:, 0:ntok], in_=self.ps[b][:, 0:ntok], func=AF.Sigmoid, bias=self.bglu[:, ci:ci + 1]),
                  r=[f"ps{b}", "bglu"], w=["sg"])
            P.add("dve", lambda e: e.tensor_tensor(out=self.mixT[:, ci, 0:ntok], in0=self.sg[:, 0:ntok], in1=self.ys[:, ci, 0:ntok], op=ALU.mult),
                  r=["sg", f"ys_{ci}"], w=["mixT"])
        self.linear_fm(D["s5_w_glu"][0], 0, 512, lambda k: self.zb[:, k, 0:ntok], ["ub"], ntok, evac, banks=(4, 7))

    def proj_in(self, ntok, nval, conv_out=None):
        P, D = self.P, self.D

        def evac(ci, m, b):
            src = self.ps[b][0:m, 0:ntok]
            if ci < 4:
                P.add("act", lambda e: e.activation(out=self.ub[:, ci, 0:ntok], in_=src, func=AF.Copy), r=[f"ps{b}"], w=["ub"])
            elif ci < 16:
                P.add("act", lambda e: e.activation(out=self.qkvb[:, ci - 4, 3:3 + ntok], in_=src, func=AF.Copy), r=[f"ps{b}"], w=["qkvb"])
                if conv_out is not None:
                    P.add("dve", lambda e: e.tensor_copy(out=self.cv32[:, ci - 4, :], in_=self.ps[b][:, nval - 3:nval]), r=[f"ps{b}"], w=["cv32"])
                    P.add("sp", lambda e: e.dma_start(out=conv_out[:, (ci - 4) * 128:(ci - 3) * 128].rearrange("j p -> p j"), in_=self.cv32[:, ci - 4, :],
                                                      allow_slow_non_contiguous=True), r=["cv32"], w=[f"co{ci}"], dma="out")
            elif ci < 20:
                P.add("act", lambda e: e.activation(out=self.zs[:, ci - 16, 0:ntok], in_=src, func=AF.Silu), r=[f"ps{b}"], w=["zs"])
            else:
                P.add("act", lambda e: e.activation(out=self.ba[:, 0:ntok], in_=src, func=AF.Copy), r=[f"ps{b}"], w=["ba"])
        self.linear_fm(D["w_in"][0], 0, IN_COLS, lambda k: self.hT[:, k, 0:ntok], ["hT"], ntok, evac)

    def final_store(self, nblk, dst_fn):
        P = self.P
        P.add("sp", lambda e: e.dma_start(out=self.nfin, in_=self.D["norm_final"].partition_broadcast(128)), w=["s5t0", "s5t1"], dma="ldn")
        P.add("dve", lambda e: e.memset(self.ss[:], 0.0), w=["ss"])
        for b in range(nblk):
            P.add("act", lambda e, b=b: e.activation(out=self.xn[:], in_=self.xres[:, b, :], func=AF.Square,
                                                     accum_out=self.ss[:, b:b + 1]), r=[f"xres{b}"], w=["xn", "ss"])
        P.add("act", lambda e: e.activation(out=self.ss[:, 4:4 + nblk], in_=self.ss[:, 0:nblk], func=AF.Sqrt, scale=1.0 / DM, bias=EPS), r=["ss"], w=["ss"])
        P.add("dve", lambda e: e.reciprocal(out=self.ss[:, 4:4 + nblk], in_=self.ss[:, 4:4 + nblk]), r=["ss"], w=["ss"])
        for b in range(nblk):
            yo = self.yo2[b % 2]
            ky = f"ys_{2 * (b % 2)}"
            ky2 = f"ys_{2 * (b % 2) + 1}"
            P.add("dve", lambda e, b=b, yo=yo: e.scalar_tensor_tensor(out=yo, in0=self.xres[:, b, :], scalar=self.ss[:, 4 + b:5 + b],
                                                                      in1=self.nfin, op0=ALU.mult, op1=ALU.mult),
                  r=[f"xres{b}", "ss", "s5t0", "s5t1"], w=[ky, ky2])
            dst, rows = dst_fn(b)
            P.add("sp", lambda e, dst=dst, rows=rows, yo=yo: e.dma_start(out=dst, in_=yo[0:rows, :]), r=[ky, ky2], w=["ydram"], dma=f"outy{b % 2}")

    def build(self):
        P, D = self.P, self.D
        self.alloc()
        self.alloc_l0()
        self.yo2 = [self.ys[:, 0:2, :].rearrange("p a t -> p (a t)"), self.ys[:, 2:4, :].rearrange("p a t -> p (a t)")]
        self.alloc_gdn()
        self.alloc_swa()
        self.setup()
        if self.stages != -2:
            self.s5_setup()
        self.gdn_setup()
        self.swa_setup()
        seqs = [("p", 0)]
        if self.stages >= 2:
            seqs += [("s", 0), ("s", 1)]
        for kind, si in seqs:
            if kind == "p":
                ntiles, ntok, nval = SEQ // TT, TT, TT
                P.add("dve", lambda e: e.memset(self.xst[:], 0.0), w=["xst"])
                P.add("dve", lambda e: e.memset(self.qkvb[:, :, 0:3], 0.0), w=["qkvb"])
                P.add("dve", lambda e: e.memset(self.S32[:], 0.0), w=["S32"])
                P.add("dve", lambda e: e.memset(self.Sb[:], 0.0), w=["Sb"])
            else:
                ntiles, ntok, nval = 1, 128, DEC_SEQ
                for i, nm in enumerate(("st_s5_re", "st_s5_im")):
                    P.add("sp", lambda e, i=i, nm=nm, si=si: e.dma_start(
                        out=self.xst[:, i, :], in_=D[nm][si].rearrange("(gp gl) p -> (gl p) gp", gl=2), allow_slow_non_contiguous=True),
                        w=["xst"], dma="ldst")
                P.add("sp", lambda e, si=si: e.dma_start(out=self.S32[:], in_=D["st_gdn"][si].rearrange("h k v -> k h v")), w=["S32"], dma="ldst")
                P.add("act", lambda e: e.activation(out=self.Sb[:], in_=self.S32[:], func=AF.Copy), r=["S32"], w=["Sb"])
                if not (_RISK & 8):
                    self.swa_init_sample(si)
                for b in range(12):
                    P.add("sp", lambda e, si=si, b=b: e.dma_start(out=self.cv32[:, b, :], in_=D["st_conv"][si][:, b * 128:(b + 1) * 128].rearrange("j p -> p j"),
                                                                  allow_slow_non_contiguous=True), w=["cv32"], dma="ldst")
                P.add("dve", lambda e: e.tensor_copy(out=self.qkvb[:, :, 0:3], in_=self.cv32[:]), r=["cv32"], w=["qkvb"])
            if self.stages == 0:
                ntiles = 1
            if self.stages < 0:
                ntiles = 0
                continue
            nblk = ntok // 128
            for ti in range(ntiles):
                last = ti == ntiles - 1
                if kind == "p":
                    for b in range(4):
                        P.add("sp", lambda e, ti=ti, b=b: e.dma_start(out=self.xres[:, b, :], in_=D["xp"][ti * TT + b * 128:ti * TT + (b + 1) * 128, :]),
                              w=[f"xres{b}"], dma=f"ldx{b}")
                else:
                    P.add("dve", lambda e: e.memset(self.xres[:, 0, :], 0.0), w=["xres0"])
                    P.add("sp", lambda e, si=si: e.dma_start(out=self.xres[0:DEC_SEQ, 0, :], in_=D["xs"][si]), w=["xres0"], dma="ldx0")
                self.norm_T(nblk, 0)
                conv_out = None
                if last and not (_RISK & 2):
                    conv_out = D["p_conv"] if kind == "p" else D["s_conv"][si]
                self.proj_in(ntok, nval, conv_out)
                so = None
                if last:
                    so = (D["p_s5_re"], D["p_s5_im"]) if kind == "p" else (D["s_s5_re"][si], D["s_s5_im"][si])
                go = None
                if last:
                    go = D["p_gdn"] if kind == "p" else D["s_gdn"][si]
                self.s5_tile(ntok, nval, so)
                self.gdn_tile(ntok, nval, go)
                gens = []
                wts = []
                alive = []
                while any(alive):
                    for gi in range(0):
                        for _ in range(wts[gi]):
                            if not alive[gi]:
                                break
                            try:
                                next(gens[gi])
                            except StopIteration:
                                alive[gi] = False
                P.add("dve", lambda e, ntok=ntok: e.tensor_copy(out=self.qkvb[:, :, 0:3], in_=self.qkvb[:, :, ntok:ntok + 3]), r=["qkvb"], w=["qkvb"])
                self.linear_tm_res(D["w_out_ab"][0], lambda k, b: self.mixT[:, k, b * 128:(b + 1) * 128], ["mixT"], nblk)
                self.ffn(0, nblk, ntok)
                self.norm_T(nblk, 2)
                if not (_RISK & 8):
                    self.swa_tile(ntok, nval, ti == 0, kind, si, last)
                self.ffn(1, nblk, ntok)
                if kind == "p":
                    self.final_store(nblk, lambda b, ti=ti: (D["y_p"][ti * TT + b * 128:ti * TT + (b + 1) * 128, :], 128))
                else:
                    self.final_store(nblk, lambda b, si=si: (D["y_s"][si], DEC_SEQ))
        P.emit()
        self.es.close()
        return self.nc


_CACHE = {}


def _program(stages=99):
    if stages not in _CACHE:
        _CACHE[stages] = Builder(stages).build()
    return _CACHE[stages]


def kernel(x_prompt, x_sample, state_s5_re, state_s5_im, state_gdn, state_gdn_conv, cache_swa_k, cache_swa_v, **w):
    f = lambda a: np.ascontiguousarray(np.asarray(a, dtype=np.float32))
    nc = _program()
    wd = {n: f(w[n]) for n, _ in W_SPECS}
    in_maps = []
    for c in range(NCORES):
        m = dict(wd)
        m["xp"] = f(x_prompt[c])
        m["xs"] = f(x_sample[2 * c:2 * c + 2])
        m["st_s5_re"] = f(state_s5_re[0, 2 * c:2 * c + 2])
        m["st_s5_im"] = f(state_s5_im[0, 2 * c:2 * c + 2])
        m["st_gdn"] = f(state_gdn[0, 2 * c:2 * c + 2])
        m["st_conv"] = f(state_gdn_conv[0, 2 * c:2 * c + 2])
        m["st_k"] = f(np.asarray(cache_swa_k)[0, 2 * c:2 * c + 2].reshape(2, 128, 256))
        m["st_v"] = f(np.asarray(cache_swa_v)[0, 2 * c:2 * c + 2].reshape(2, 128, 256))
        m["consts"] = _CARR
        in_maps.append(m)
    res = run_bass_kernel_spmd(nc, in_maps, core_ids=list(range(NCORES)))
    R = res.results
    cat = lambda n: np.concatenate([np.asarray(r[n], dtype=np.float32) for r in R], axis=0)
    stk = lambda n: np.stack([np.asarray(r[n], dtype=np.float32) for r in R], axis=0)
    y_p = stk("y_p")
    y_s = cat("y_s")
    outs = [y_p, y_s,
            stk("p_s5_re")[None], stk("p_s5_im")[None], stk("p_gdn")[None], stk("p_conv")[None],
            stk("p_k").reshape(1, 8, 128, 4, 64), stk("p_v").reshape(1, 8, 128, 4, 64),
            cat("s_s5_re")[None], cat("s_s5_im")[None], cat("s_gdn")[None], cat("s_conv")[None],
            cat("s_k").reshape(1, 16, 128, 4, 64), cat("s_v").reshape(1, 16, 128, 4, 64)]
    return tuple(outs)


def _carve(hid, f0, n, dt):
    v = hid[:, f0:f0 + n, :].rearrange("p a b -> p (a b)")
    if dt == F32:
        v = v.bitcast(F32)
    return v, [f"hid{f}" for f in range(f0, f0 + n)]


def alloc_gdn(self):
    sb = self.sb
    self.cw = sb("cw", [128, 4, 12], F32)
    self.gp8 = sb("gp8", [8, 8], F32)
    self.gnw = sb("gnw", [128, 1], F32)
    self.S32 = sb("S32", [128, 4, 128], F32)
    self.Sb = sb("Sb", [128, 4, 128], BF16)
    self.g8t = [sb(f"g8t{i}", [8, TT], F32) for i in range(3)]
    self.bg = sb("bg", [8, TT], F32)
    self.tok = sb("tok", [128, 2, 8, 4], F32)
    self.egl = sb("egl", [128, 4, 8], F32)
    self.vnew = sb("vnew", [128, 2, 2, 128], BF16)
    self.rs8 = sb("rs8", [128, 2, 16], F32)
    self.eye64b = sb("eye64b", [128, 64], BF16)


def gdn_setup(self):
    P, D = self.P, self.D
    for j in range(4):
        P.add("sp", lambda e, j=j: e.dma_start(out=self.cw[:, j, :], in_=D["gdn_conv_w"][0][j].rearrange("(b p) -> p b", p=128),
                                               allow_slow_non_contiguous=True), w=["cw"], dma="init")
    P.add("dve", lambda e: e.memset(self.gp8[:], 0.0), w=["gp8"])
    P.add("sp", lambda e: e.dma_start(out=self.gp8[4:8, 0:1], in_=D["gdn_dt_bias"][0].rearrange("(h o) -> h o", o=1)), w=["gp8"], dma="init")
    P.add("sp", lambda e: e.dma_start(out=self.gp8[4:8, 1:2], in_=D["gdn_a_log"][0].rearrange("(h o) -> h o", o=1)), w=["gp8"], dma="init")
    P.add("sp", lambda e: e.dma_start(out=self.gnw[:], in_=D["gdn_norm_w"][0].rearrange("(p o) -> p o", o=1)), w=["gnw"], dma="init")
    gm = self.cview("gm", 8)
    P.add("act", lambda e: e.activation(out=self.gp8[:, 3:4], in_=self.gp8[:, 1:2], func=AF.Exp), r=["gp8"], w=["gp8"])
    P.add("dve", lambda e: e.tensor_tensor(out=self.gp8[:, 2:3], in0=self.gp8[:, 3:4], in1=gm[:, 2:3], op=ALU.mult), r=["gp8", "cst"], w=["gp8"])
    P.add("dve", lambda e: e.tensor_copy(out=self.eye64b[:], in_=self.cview("eye64")), r=["cst"], w=["eye64b"])


def gdn_pair(self, pr, R, ntok, nval):
    P, D = self.P, self.D
    nch = ntok // 64
    W = ntok
    heads = (2 * pr, 2 * pr + 1)
    T, tk = R["T"], R["tk"]
    ps = [self.ps[b] for b in R["banks"]]
    pk = [f"ps{b}" for b in R["banks"]]
    qkvc, k_qkvc, qd, k_qd, P2, k_P2, Q2, k_Q2 = R["qkvc"], R["k_qkvc"], R["qd"], R["k_qd"], R["P2"], R["k_P2"], R["Q2"], R["k_Q2"]
    attnT, k_at, vb, k_vb, kbg, k_kbg, kdec, k_kdec = R["attnT"], R["k_at"], R["vb"], R["k_vb"], R["kbg"], R["k_kbg"], R["kdec"], R["k_kdec"]
    Ttb, k_Ttb, wT, k_wT, on, k_on = R["Ttb"], R["k_Ttb"], R["wT"], R["k_wT"], R["on"], R["k_on"]
    vnew = self.vnew[:, pr]
    ktok, krs, kegl, kS32, kSb = f"tok{pr}", f"rs8{pr}", f"egl{pr}", f"S32p{pr}", f"Sbp{pr}"
    gm = self.cview("gm", 8)
    sel = self.cview("sel", 8).rearrange("k (r m) -> k r m", m=128)
    selp = self.cview("selp", 8).rearrange("k (h t) -> k h t", t=2)
    eye = self.cview("eye64")
    mUs = self.cview("mUs")
    mUi = self.cview("mUi")
    ones = self.cview("ones")
    idb = self.identb

    def v3(ap, inner=64):
        return ap[:, 0:nch * inner].rearrange("p (c i) -> p c i", i=inner)

    def blk_of(idx):
        return (idx // 2) * 4 + heads[idx % 2]

    def conv_pair(p):
        for j in range(4):
            for u_ in range(2):
                idx = 2 * p + u_
                blk = blk_of(idx)
                acc = T[3 * u_][:, 0:W]
                if j == 0:
                    P.add("dve", lambda e, blk=blk, acc=acc: e.tensor_scalar(out=acc, in0=self.qkvb[:, blk, c0:c0 + W], scalar1=self.cw[:, 0, blk:blk + 1],
                                                                              scalar2=None, op0=ALU.mult), r=["qkvb", "cw"], w=[tk[3 * u_]])
                else:
                    P.add("dve", lambda e, blk=blk, acc=acc, j=j: e.scalar_tensor_tensor(
                        out=acc, in0=self.qkvb[:, blk, c0 + j:c0 + j + W], scalar=self.cw[:, j, blk:blk + 1], in1=acc, op0=ALU.mult, op1=ALU.add),
                        r=["qkvb", "cw", tk[3 * u_]], w=[tk[3 * u_]])

    def mid_pair(p):
        for u_ in range(2):
            idx = 2 * p + u_
            acc, cq, sq = T[3 * u_][:, 0:W], T[3 * u_ + 1][:, 0:W], T[3 * u_ + 2][:, 0:W]
            if p == 2:
                P.add("act", lambda e, idx=idx, acc=acc: e.activation(out=qkvc[:, idx, 0:W], in_=acc, func=AF.Silu), r=[tk[3 * u_]], w=k_qkvc)
                continue
            P.add("act", lambda e, acc=acc, cq=cq: e.activation(out=cq, in_=acc, func=AF.Silu), r=[tk[3 * u_]], w=[tk[3 * u_ + 1]])
            P.add("act", lambda e, cq=cq, sq=sq: e.activation(out=sq, in_=cq, func=AF.Square), r=[tk[3 * u_ + 1]], w=[tk[3 * u_ + 2]])
            P.add("pe", lambda e, sq=sq, u_=u_: e.matmul(ps[u_][:, 0:W], lhsT=ones, rhs=sq, start=True, stop=True), r=["cst", tk[3 * u_ + 2]], w=[pk[u_]])
            sc = 128.0 if p == 0 else 1.0
            P.add("act", lambda e, sq=sq, u_=u_, sc=sc: e.activation(out=sq, in_=ps[u_][:, 0:W], func=AF.Sqrt, scale=sc, bias=sc * EPS),
                  r=[pk[u_]], w=[tk[3 * u_ + 2]])

    def fin_pair(p):
        for u_ in range(2):
            idx = 2 * p + u_
            cq, sq = T[3 * u_ + 1][:, 0:W], T[3 * u_ + 2][:, 0:W]
            P.add("dve", lambda e, sq=sq: e.reciprocal(out=sq, in_=sq), r=[tk[3 * u_ + 2]], w=[tk[3 * u_ + 2]])
        for u_ in range(2):
            idx = 2 * p + u_
            cq, sq = T[3 * u_ + 1][:, 0:W], T[3 * u_ + 2][:, 0:W]
            P.add("dve", lambda e, idx=idx, cq=cq, sq=sq: e.tensor_tensor(out=qkvc[:, idx, 0:W], in0=cq, in1=sq, op=ALU.mult),
                  r=[tk[3 * u_ + 1], tk[3 * u_ + 2]], w=k_qkvc)

    c0 = 0
    conv_pair(0)
    mid_pair(0)
    conv_pair(1)
    fin_pair(0)
    mid_pair(1)
    yield
    conv_pair(2)
    fin_pair(1)
    mid_pair(2)
    yield
    qT = [qkvc[:, 0, :], qkvc[:, 1, :]]
    kT = [qkvc[:, 2, :], qkvc[:, 3, :]]
    vT = [qkvc[:, 4, :], qkvc[:, 5, :]]
    for hh in range(2):
        h = heads[hh]
        rows = slice(64 * hh, 64 * hh + 64)
        P.add("pe", lambda e, h=h, hh=hh, rows=rows: e.matmul(ps[0][rows, 0:W], lhsT=sel[:, 4 + h, 0:64], rhs=self.bg[:, 0:W], start=True, stop=True,
                                                              tile_position=(0, 64 * hh)), r=["cst", "bg"], w=[pk[0]])
        P.add("pe", lambda e, h=h, hh=hh, rows=rows: e.matmul(ps[1][rows, 0:W], lhsT=sel[:, h, 0:64], rhs=self.bg[:, 0:W], start=True, stop=True,
                                                              tile_position=(0, 64 * hh)), r=["cst", "bg"], w=[pk[1]])
        for c in range(nch):
            P.add("pe", lambda e, h=h, hh=hh, rows=rows, c=c: e.matmul(ps[2][rows, 2 * c:2 * c + 2], lhsT=self.bg[:, c * 64:(c + 1) * 64],
                                                                       rhs=selp[:, h, :], start=True, stop=True, tile_position=(0, 64 * hh)),
                  r=["cst", "bg"], w=[pk[2]])
    tok = self.tok[:, pr]
    P.add("dve", lambda e: e.tensor_copy(out=tok[:, 0:nch, 0:2], in_=ps[2][:, 0:2 * nch].rearrange("p (c t) -> p c t", t=2)), r=[pk[2]], w=[ktok])
    E = T[0]
    P.add("dve", lambda e: e.tensor_tensor(out=v3(E), in0=v3(ps[0]), in1=tok[:, 0:nch, 1:2].to_broadcast([128, nch, 64]), op=ALU.subtract),
          r=[pk[0], ktok], w=[tk[0]])
    P.add("dve", lambda e: e.tensor_scalar(out=E[:, 0:W], in0=E[:, 0:W], scalar1=0.0, scalar2=None, op0=ALU.min), r=[tk[0]], w=[tk[0]])
    P.add("act", lambda e: e.activation(out=E[:, 0:W], in_=E[:, 0:W], func=AF.Exp), r=[tk[0]], w=[tk[0]])
    P.add("dve", lambda e: e.tensor_tensor(out=tok[:, 0:nch, 3:4], in0=v3(ps[0])[:, :, 63:64], in1=tok[:, 0:nch, 1:2], op=ALU.subtract),
          r=[pk[0], ktok], w=[ktok])
    P.add("act", lambda e: e.activation(out=tok[:, 0:nch, 3:4], in_=tok[:, 0:nch, 3:4], func=AF.Exp), r=[ktok], w=[ktok])
    P.add("act", lambda e: e.activation(out=tok[:, 0:nch, 2:3], in_=tok[:, 0:nch, 1:2], func=AF.Exp), r=[ktok], w=[ktok])
    P.add("dve", lambda e: e.tensor_tensor(out=tok[:, 0:nch, 2:3], in0=tok[:, 0:nch, 2:3], in1=tok[:, 0:nch, 0:1], op=ALU.mult), r=[ktok], w=[ktok])
    yield
    Bm, Am, Tt = T[1], T[2], T[5]
    mUs_b = mUs.unsqueeze(1).to_broadcast([128, nch, 64])
    mUi_b = mUi.unsqueeze(1).to_broadcast([128, nch, 64])
    eye_b = eye.unsqueeze(1).to_broadcast([128, nch, 64])
    for hh in range(2):
        rows = slice(64 * hh, 64 * hh + 64)
        for c in range(nch):
            cs_ = slice(c * 64, (c + 1) * 64)
            P.add("pe", lambda e, hh=hh, rows=rows, cs_=cs_: e.matmul(ps[2][rows, cs_], lhsT=kT[hh][:, cs_], rhs=kT[hh][:, cs_], start=True, stop=True,
                                                                      tile_position=(0, 64 * hh)), r=k_qkvc, w=[pk[2]])
    P.add("dve", lambda e: e.tensor_tensor(out=Bm[:, 0:W], in0=ps[2][:, 0:W], in1=E[:, 0:W], op=ALU.mult), r=[pk[2], tk[0]], w=[tk[1]])
    for hh in range(2):
        rows = slice(64 * hh, 64 * hh + 64)
        for c in range(nch):
            cs_ = slice(c * 64, (c + 1) * 64)
            P.add("pe", lambda e, hh=hh, rows=rows, cs_=cs_: e.matmul(ps[2][rows, cs_], lhsT=kT[hh][:, cs_], rhs=qT[hh][:, cs_], start=True, stop=True,
                                                                      tile_position=(0, 64 * hh)), r=k_qkvc, w=[pk[2]])
    P.add("dve", lambda e: e.tensor_tensor(out=v3(Bm), in0=v3(Bm), in1=mUs_b, op=ALU.mult), r=[tk[1], "cst"], w=[tk[1]])
    P.add("dve", lambda e: e.tensor_tensor(out=Bm[:, 0:W], in0=Bm[:, 0:W], in1=ps[1][:, 0:W], op=ALU.mult), r=[tk[1], pk[1]], w=[tk[1]])
    P.add("dve", lambda e: e.tensor_tensor(out=Am[:, 0:W], in0=ps[2][:, 0:W], in1=E[:, 0:W], op=ALU.mult), r=[pk[2], tk[0]], w=[tk[2]])
    P.add("dve", lambda e: e.tensor_tensor(out=v3(attnT), in0=v3(Am), in1=mUi_b, op=ALU.mult), r=[tk[2], "cst"], w=k_at)
    yield
    t0b = T[0].bitcast(BF16)
    Bm_b, Am_b = t0b[:, 0:512], t0b[:, 512:1024]
    P.add("act", lambda e: e.activation(out=Bm_b[:, 0:W], in_=Bm[:, 0:W], func=AF.Copy), r=[tk[1], *k_at], w=[tk[0]])
    for hh in range(2):
        rows = slice(64 * hh, 64 * hh + 64)
        for c in range(nch):
            cs_ = slice(c * 64, (c + 1) * 64)
            P.add("pe", lambda e, hh=hh, rows=rows, cs_=cs_: e.matmul(ps[2][rows, cs_], lhsT=Bm_b[rows, cs_], rhs=self.eye64b[rows, :], start=True, stop=True,
                                                                      tile_position=(64 * hh, 64 * hh)), r=[tk[0], "eye64b"], w=[pk[2]])
    P.add("dve", lambda e: e.tensor_copy(out=Am_b[:, 0:W], in_=ps[2][:, 0:W]), r=[pk[2]], w=[tk[0]])
    P.add("dve", lambda e: e.tensor_tensor(out=v3(Tt), in0=eye_b, in1=v3(Bm), op=ALU.subtract), r=[tk[1], "cst"], w=[tk[5]])
    P.add("act", lambda e: e.activation(out=Ttb[:, 0:W], in_=Tt[:, 0:W], func=AF.Copy), r=[tk[5]], w=k_Ttb)
    yield
    Pm, Qm, kP, kQ = Bm_b, Am_b, [tk[0]], [tk[0]]
    sets = [(T[3].bitcast(BF16)[:, 0:512], T[4].bitcast(BF16)[:, 0:512], tk[3:4], tk[4:5]),
            (P2.bitcast(BF16)[:, 0:512], Q2.bitcast(BF16)[:, 0:512], k_P2, k_Q2)]
    for lvl in range(5):
        Pn, Qn, kPn, kQn = sets[lvl % 2]
        for hh in range(2):
            rows = slice(64 * hh, 64 * hh + 64)
            for c in range(nch):
                cs_ = slice(c * 64, (c + 1) * 64)
                tp = (64 * hh, 64 * hh)
                P.add("pe", lambda e, rows=rows, cs_=cs_, tp=tp, Pm=Pm, Qm=Qm: e.matmul(ps[1][rows, cs_], lhsT=Pm[rows, cs_], rhs=Qm[rows, cs_],
                                                                                        start=True, stop=True, tile_position=tp),
                      r=[*kP, *kQ], w=[pk[1]])
                if lvl < 4:
                    P.add("pe", lambda e, rows=rows, cs_=cs_, tp=tp, Pm=Pm, Qm=Qm: e.matmul(ps[0][rows, cs_], lhsT=Qm[rows, cs_], rhs=Pm[rows, cs_],
                                                                                            start=True, stop=True, tile_position=tp),
                          r=[*kP, *kQ], w=[pk[0]])
        yield
        P.add("dve", lambda e, Qn=Qn: e.tensor_copy(out=Qn[:, 0:W], in_=ps[1][:, 0:W]), r=[pk[1]], w=kQn)
        if lvl < 4:
            P.add("act", lambda e, Pn=Pn: e.activation(out=Pn[:, 0:W], in_=ps[0][:, 0:W], func=AF.Copy), r=[pk[0]], w=kPn)
        yield
        for hh in range(2):
            rows = slice(64 * hh, 64 * hh + 64)
            for c in range(nch):
                cs_ = slice(c * 64, (c + 1) * 64)
                P.add("pe", lambda e, rows=rows, cs_=cs_, hh=hh, Qn=Qn: e.matmul(ps[2][rows, cs_], lhsT=Qn[rows, cs_], rhs=Ttb[rows, cs_],
                                                                                 start=True, stop=True, tile_position=(64 * hh, 64 * hh)),
                      r=[*kQn, *k_Ttb], w=[pk[2]])
        yield
        P.add("dve", lambda e: e.tensor_tensor(out=Tt[:, 0:W], in0=Tt[:, 0:W], in1=ps[2][:, 0:W], op=ALU.add), r=[tk[5], pk[2]], w=[tk[5]])
        P.add("act", lambda e: e.activation(out=Ttb[:, 0:W], in_=Tt[:, 0:W], func=AF.Copy), r=[tk[5]], w=k_Ttb)
        Pm, Qm, kP, kQ = Pn, Qn, kPn, kQn
        yield
    pT3 = self.pT[:, 0:nch * 128].rearrange("p (c d) -> p c d", d=128)
    for src, dsts in ((vT, "v"), (kT, "k")):
        for hh in range(2):
            rows = slice(64 * hh, 64 * hh + 64)
            for c in range(nch):
                P.add("pe", lambda e, src=src, hh=hh, rows=rows, c=c: e.transpose(out=self.pT[rows, c * 128:(c + 1) * 128], in_=src[hh][:, c * 64:(c + 1) * 64],
                                                                                   identity=idb[:], tile_position=(0, 64 * hh)),
                      r=[*k_qkvc, "identb"], w=["pT"])
        if dsts == "v":
            P.add("dve", lambda e: e.tensor_tensor(out=v3(vb, 128), in0=pT3, in1=tok[:, 0:nch, 0:1].to_broadcast([128, nch, 128]), op=ALU.mult),
                  r=["pT", ktok], w=k_vb)
        else:
            P.add("dve", lambda e: e.tensor_tensor(out=v3(kbg, 128), in0=pT3, in1=tok[:, 0:nch, 2:3].to_broadcast([128, nch, 128]), op=ALU.mult),
                  r=["pT", ktok], w=k_kbg)
            P.add("dve", lambda e: e.tensor_tensor(out=v3(kdec, 128), in0=pT3, in1=tok[:, 0:nch, 3:4].to_broadcast([128, nch, 128]), op=ALU.mult),
                  r=["pT", ktok], w=k_kdec)
    yield
    u = [T[1], T[2]]
    for hh in range(2):
        rows = slice(64 * hh, 64 * hh + 64)
        for c in range(nch):
            ub_, uc = (0, c) if c < 4 else (1, c - 4)
            P.add("pe", lambda e, hh=hh, rows=rows, c=c, ub_=ub_, uc=uc: e.matmul(
                ps[ub_][rows, uc * 128:(uc + 1) * 128], lhsT=Ttb[rows, c * 64:(c + 1) * 64], rhs=vb[rows, c * 128:(c + 1) * 128],
                start=True, stop=True, tile_position=(64 * hh, 64 * hh)), r=[*k_Ttb, *k_vb], w=[pk[ub_]])
    P.add("act", lambda e: e.activation(out=u[0][:, 0:min(4, nch) * 128], in_=ps[0][:, 0:min(4, nch) * 128], func=AF.Copy), r=[pk[0]], w=[tk[1]])
    if nch > 4:
        P.add("act", lambda e: e.activation(out=u[1][:, :], in_=ps[1][:, :], func=AF.Copy), r=[pk[1]], w=[tk[2]])
    wb = (2, 0)
    for hh in range(2):
        rows = slice(64 * hh, 64 * hh + 64)
        for c in range(nch):
            P.add("pe", lambda e, hh=hh, rows=rows, c=c: e.matmul(
                ps[wb[hh]][:, c * 64:(c + 1) * 64], lhsT=kbg[rows, c * 128:(c + 1) * 128], rhs=Ttb[rows, c * 64:(c + 1) * 64],
                start=True, stop=True, tile_position=(64 * hh, 0)), r=[*k_Ttb, *k_kbg], w=[pk[wb[hh]]])
    for hh in range(2):
        P.add("act", lambda e, hh=hh: e.activation(out=wT[:, hh, 0:W], in_=ps[wb[hh]][:, 0:W], func=AF.Copy), r=[pk[wb[hh]]], w=k_wT)
    yield
    eg = T[0]
    for hh in range(2):
        h = heads[hh]
        P.add("pe", lambda e, h=h: e.matmul(ps[1][:, 0:W], lhsT=sel[:, 4 + h, :], rhs=self.bg[:, 0:W], start=True, stop=True), r=["cst", "bg"], w=[pk[1]])
        P.add("act", lambda e: e.activation(out=eg[:, 0:W], in_=ps[1][:, 0:W], func=AF.Exp), r=[pk[1]], w=[tk[0]])
        P.add("dve", lambda e, hh=hh: e.tensor_tensor(out=qd[:, hh, 0:W], in0=qT[hh][:, 0:W], in1=eg[:, 0:W], op=ALU.mult), r=[*k_qkvc, tk[0]], w=k_qd)
        P.add("dve", lambda e, h=h: e.tensor_copy(out=self.egl[:, h, 0:nch], in_=v3(eg)[:, :, 63]), r=[tk[0]], w=[kegl])
    yield
    o_tm = [T[3], T[4]]
    for c in range(nch):
        slot = c % 2
        for hh in range(2):
            h = heads[hh]
            rows = slice(64 * hh, 64 * hh + 64)
            P.add("pe", lambda e, hh=hh, h=h, rows=rows, c=c: e.matmul(ps[0][rows, 0:128], lhsT=wT[:, hh, c * 64:(c + 1) * 64], rhs=self.Sb[:, h, :],
                                                                       start=True, stop=True, tile_position=(0, 64 * hh)), r=[*k_wT, kSb], w=[pk[0]])
        yield
        ub_, uc = (0, c) if c < 4 else (1, c - 4)
        P.add("dve", lambda e, slot=slot, ub_=ub_, uc=uc: e.tensor_tensor(out=vnew[:, slot, :], in0=u[ub_][:, uc * 128:(uc + 1) * 128],
                                                                           in1=ps[0][:, 0:128], op=ALU.subtract),
              r=[tk[1 + ub_], pk[0]], w=[f"vnew{pr}_{slot}"])
        for hh in range(2):
            h = heads[hh]
            rows = slice(64 * hh, 64 * hh + 64)
            P.add("pe", lambda e, hh=hh, h=h, rows=rows, c=c: e.matmul(ps[0][rows, 128:256], lhsT=qd[:, hh, c * 64:(c + 1) * 64], rhs=self.Sb[:, h, :],
                                                                       start=True, stop=False, tile_position=(0, 64 * hh)), r=[*k_qd, kSb], w=[pk[0]])
            P.add("pe", lambda e, hh=hh, rows=rows, c=c, slot=slot: e.matmul(ps[0][rows, 128:256], lhsT=attnT[rows, c * 64:(c + 1) * 64],
                                                                             rhs=vnew[rows, slot, :], start=False, stop=True,
                                                                             tile_position=(64 * hh, 64 * hh)), r=[*k_at, f"vnew{pr}_{slot}"], w=[pk[0]])
            P.add("pe", lambda e, hh=hh, rows=rows, c=c, slot=slot: e.matmul(ps[1 + hh][:, 0:128], lhsT=kdec[rows, c * 128:(c + 1) * 128],
                                                                             rhs=vnew[rows, slot, :], start=True, stop=True,
                                                                             tile_position=(64 * hh, 0)), r=[*k_kdec, f"vnew{pr}_{slot}"], w=[pk[1 + hh]])
        yield
        P.add("act", lambda e, ub_=ub_, uc=uc: e.activation(out=o_tm[ub_][:, uc * 128:(uc + 1) * 128], in_=ps[0][:, 128:256], func=AF.Copy),
              r=[pk[0]], w=[tk[3 + ub_]])
        for hh in range(2):
            h = heads[hh]
            P.add("dve", lambda e, hh=hh, h=h, c=c: e.scalar_tensor_tensor(out=self.S32[:, h, :], in0=self.S32[:, h, :], scalar=self.egl[:, h, c:c + 1],
                                                                           in1=ps[1 + hh][:, 0:128], op0=ALU.mult, op1=ALU.add),
                  r=[kS32, kegl, pk[1 + hh]], w=[kS32])
        P.add("act", lambda e, pr=pr: e.activation(out=self.Sb[:, 2 * pr:2 * pr + 2, :], in_=self.S32[:, 2 * pr:2 * pr + 2, :], func=AF.Copy), r=[kS32], w=[kSb])
        yield
    rs = self.rs8[:, pr]
    for ub_ in range(2 if nch > 4 else 1):
        ncc = min(4, nch)
        o3 = o_tm[ub_][:, 0:ncc * 128].rearrange("p (c d) -> p c d", d=128)
        sq3 = T[0][:, 0:ncc * 128].rearrange("p (c d) -> p c d", d=128)
        P.add("dve", lambda e, o3=o3, sq3=sq3: e.tensor_tensor(out=sq3, in0=o3, in1=o3, op=ALU.mult), r=[tk[3 + ub_]], w=[tk[0]])
        P.add("dve", lambda e, sq3=sq3, ub_=ub_, ncc=ncc: e.reduce_sum(out=rs[:, 4 * ub_:4 * ub_ + ncc], in_=sq3, axis=AX.X), r=[tk[0]], w=[krs])
        P.add("act", lambda e, ub_=ub_, ncc=ncc: e.activation(out=rs[:, 8 + 4 * ub_:8 + 4 * ub_ + ncc], in_=rs[:, 4 * ub_:4 * ub_ + ncc], func=AF.Sqrt,
                                                              scale=1.0 / 128, bias=EPS), r=[krs], w=[krs])
        P.add("dve", lambda e, ub_=ub_, ncc=ncc: e.reciprocal(out=rs[:, 8 + 4 * ub_:8 + 4 * ub_ + ncc], in_=rs[:, 8 + 4 * ub_:8 + 4 * ub_ + ncc]), r=[krs], w=[krs])
        P.add("dve", lambda e, o3=o3, ub_=ub_, ncc=ncc: e.tensor_tensor(
            out=on[:, ub_ * 512:ub_ * 512 + ncc * 128].rearrange("p (c d) -> p c d", d=128), in0=o3,
            in1=rs[:, 8 + 4 * ub_:8 + 4 * ub_ + ncc].unsqueeze(2).to_broadcast([128, ncc, 128]), op=ALU.mult), r=[tk[3 + ub_], krs], w=k_on)
    yield
    for hh in range(2):
        rows = slice(64 * hh, 64 * hh + 64)
        for c in range(nch):
            P.add("pe", lambda e, hh=hh, rows=rows, c=c: e.matmul(ps[hh][:, c * 64:(c + 1) * 64], lhsT=on[rows, c * 128:(c + 1) * 128], rhs=self.eye64b[rows, :],
                                                                  start=True, stop=True, tile_position=(64 * hh, 0)), r=[*k_on, "eye64b"], w=[pk[hh]])
        h = heads[hh]
        P.add("dve", lambda e, hh=hh, h=h: e.scalar_tensor_tensor(out=self.mixT[:, 4 + h, 0:W], in0=ps[hh][:, 0:W], scalar=self.gnw[:, 0:1],
                                                                  in1=self.zs[:, h, 0:W], op0=ALU.mult, op1=ALU.mult), r=[pk[hh], "gnw", "zs"], w=["mixT"])


def gdn_tile(self, ntok, nval, state_out=None):
    P, D = self.P, self.D
    nch = ntok // 64
    T = self.g5t
    tk = [f"g5t{i}" for i in range(6)]
    gm = self.cview("gm", 8)
    sel = self.cview("sel", 8).rearrange("k (r m) -> k r m", m=128)
    selp = self.cview("selp", 8).rearrange("k (h t) -> k h t", t=2)
    eye = self.cview("eye64")
    mUs = self.cview("mUs")
    mUi = self.cview("mUi")
    ones = self.cview("ones")
    idb = self.identb
    W = ntok

    def v3(ap, inner=64):
        return ap[:, 0:nch * inner].rearrange("p (c i) -> p c i", i=inner)

    g0, g1, g2 = self.g8t
    ba = self.ba
    P.add("act", lambda e: e.activation(out=g0[:, 0:W], in_=ba[:, 0:W], func=AF.Exp, bias=self.gp8[:, 0:1]), r=["ba", "gp8"], w=["g8t0"])
    P.add("act", lambda e: e.activation(out=g0[:, 0:W], in_=g0[:, 0:W], func=AF.Ln, bias=1.0), r=["g8t0"], w=["g8t0"])
    P.add("dve", lambda e: e.tensor_scalar(out=g1[:, 0:W], in0=g0[:, 0:W], scalar1=self.gp8[:, 2:3], scalar2=None, op0=ALU.mult), r=["g8t0", "gp8"], w=["g8t1"])
    P.add("act", lambda e: e.activation(out=g0[:, 0:W], in_=ba[:, 0:W], func=AF.Sigmoid), r=["ba", "g8t1"], w=["g8t0"])
    P.add("dve", lambda e: e.scalar_tensor_tensor(out=g2[:, 0:W], in0=g0[:, 0:W], scalar=gm[:, 0:1], in1=g1[:, 0:W], op0=ALU.mult, op1=ALU.add),
          r=["g8t0", "g8t1", "cst"], w=["g8t2"])
    if nval < ntok:
        P.add("dve", lambda e: e.memset(g2[:, nval:W], 0.0), w=["g8t2"])
    P.add("dve", lambda e: e.tensor_tensor_scan(out=g0[:, 0:W], data0=self.cview("cmask", 8)[:, 0:W], data1=g2[:, 0:W], initial=0.0,
                                                op0=ALU.mult, op1=ALU.add), r=["g8t2", "cst"], w=["g8t0"])
    P.add("dve", lambda e: e.tensor_scalar(out=g1[:, 0:W], in0=g0[:, 0:W], scalar1=gm[:, 1:2], scalar2=None, op0=ALU.mult), r=["g8t0", "cst"], w=["g8t1"])
    P.add("dve", lambda e: e.scalar_tensor_tensor(out=self.bg[:, 0:W], in0=g2[:, 0:W], scalar=gm[:, 0:1], in1=g1[:, 0:W], op0=ALU.mult, op1=ALU.add),
          r=["g8t2", "g8t1", "cst"], w=["bg"])


    hid = self.hid
    R0 = {"T": self.g5t, "tk": [f"g5t{i}" for i in range(6)], "banks": (0, 1, 2)}
    q_, R0["k_qkvc"] = _carve(hid, 0, 6, BF16)
    R0["qkvc"] = q_.rearrange("p (a t) -> p a t", t=TT)
    q_, R0["k_qd"] = _carve(hid, 6, 2, BF16)
    R0["qd"] = q_.rearrange("p (a t) -> p a t", t=TT)
    R0["P2"], R0["k_P2"] = _carve(hid, 8, 2, F32)
    R0["Q2"], R0["k_Q2"] = _carve(hid, 10, 2, F32)
    R0["attnT"], R0["k_at"] = _carve(hid, 12, 1, BF16)
    R0["vb"], R0["k_vb"] = _carve(hid, 13, 2, BF16)
    R0["kbg"], R0["k_kbg"] = _carve(hid, 17, 2, BF16)
    R0["kdec"], R0["k_kdec"] = _carve(hid, 19, 2, BF16)
    R0["Ttb"], R0["k_Ttb"] = _carve(hid, 21, 1, BF16)
    q_, R0["k_wT"] = _carve(hid, 8, 2, BF16)
    R0["wT"] = q_.rearrange("p (a t) -> p a t", t=TT)
    R0["on"], R0["k_on"] = _carve(hid, 10, 2, BF16)
    R1 = {"T": self.s5t, "tk": [f"s5t{i}" for i in range(6)], "banks": (4, 5, 6)}
    R1["qkvc"], R1["k_qkvc"] = self.hT[:, 0:6, :], ["hT_q"]
    R1["qd"], R1["k_qd"] = self.hT[:, 6:8, :], ["hT_d"]
    R1["P2"], R1["k_P2"] = self.ys[:, 0, :], ["ys_0"]
    R1["Q2"], R1["k_Q2"] = self.ys[:, 1, :], ["ys_1"]
    R1["wT"], R1["k_wT"] = self.ys[:, 2, :].bitcast(BF16).rearrange("p (a t) -> p a t", t=TT), ["ys_2"]
    R1["on"], R1["k_on"] = self.ys[:, 3, :].bitcast(BF16), ["ys_3"]
    R1["attnT"], R1["k_at"] = self.ub[:, 0, :], ["ub_0"]
    R1["Ttb"], R1["k_Ttb"] = self.ub[:, 1, :], ["ub_1"]
    R1["vb"], R1["k_vb"] = self.ub[:, 2:4, :].rearrange("p a t -> p (a t)"), ["ub_23"]
    R1["kbg"], R1["k_kbg"] = self.xrb[:, 0].rearrange("p a t -> p (a t)"), ["xrb0"]
    R1["kdec"], R1["k_kdec"] = self.xrb[:, 1].rearrange("p a t -> p (a t)"), ["xrb1"]
    gens = [gdn_pair(self, 0, R0, ntok, nval), gdn_pair(self, 1, R1, ntok, nval)]
    alive = [True, True]
    while any(alive):
        for gi in range(2):
            if alive[gi]:
                try:
                    next(gens[gi])
                except StopIteration:
                    alive[gi] = False
    if state_out is not None:
        P.add("sp", lambda e: e.dma_start(out=state_out.rearrange("h k v -> k h v"), in_=self.S32[:]), r=["S32"], w=["gdn_out"], dma="out")


Builder.alloc_gdn = alloc_gdn
Builder.gdn_setup = gdn_setup
Builder.gdn_tile = gdn_tile


def alloc_swa(self):
    sb = self.sb
    self.kTd = sb("kTd", [128, 4, 128 + TT], BF16)
    self.vtm = sb("vtm", [128, 5, 256], BF16)
    self.esink = sb("esink", [128, 8], F32)
    self.onesb = sb("onesb", [128, 64], BF16)
    self.rcp = self.sg[:, 0:256]


def swa_setup(self):
    P, D = self.P, self.D
    sk = D["swa_sinks"][0].rearrange("(m two) -> two m", two=2)
    for half in range(2):
        P.add("sp", lambda e, half=half: e.dma_start(out=self.esink[64 * half:64 * half + 64, :], in_=sk[half].partition_broadcast(64),
                                                     allow_slow_non_contiguous=True), w=["esink"], dma="init")
    P.add("act", lambda e: e.activation(out=self.esink[:], in_=self.esink[:], func=AF.Exp), r=["esink"], w=["esink"])
    P.add("dve", lambda e: e.memset(self.onesb[:], 1.0), w=["onesb"])


def swa_init_sample(self, si):
    P, D = self.P, self.D
    ck, kck = _carve(self.hid, 14, 1, BF16)
    ck = ck[:, 0:256]
    P.add("pool", lambda e: e.dma_start(out=self.vtm[:, 0, :], in_=D["st_v"][si]), w=["vtm"], dma="ldkv")
    P.add("pool", lambda e: e.dma_start(out=ck, in_=D["st_k"][si]), w=kck, dma="ldkv")
    for kv in range(4):
        for half in range(2):
            P.add("pe", lambda e, kv=kv, half=half: e.matmul(self.ps[0][64 * half:64 * half + 64, kv * 128:(kv + 1) * 128], lhsT=ck[:, kv * 64:(kv + 1) * 64],
                                                             rhs=self.identb[:], start=True, stop=True, tile_position=(0, 64 * half)),
                  r=[*kck, "identb"], w=["ps0"])
    P.add("act", lambda e: e.activation(out=self.kTd[:, :, 0:128], in_=self.ps[0][:, :].rearrange("p (k t) -> p k t", t=128), func=AF.Copy), r=["ps0"], w=["kTd"])


def swa_tile(self, ntok, nval, first, kind, si, last):
    P, D = self.P, self.D
    nblk = ntok // 128
    nch = ntok // 64 if kind == "p" else 1
    hid = self.hid
    qT, k_qT = _carve(hid, 0, 8, BF16)
    qT = qT.rearrange("p (a t) -> p a t", t=TT)
    ETs, k_ETs = [], []
    for j in range(2):
        et, ke = _carve(hid, 8 + 2 * j, 2, BF16)
        ETs.append(et.rearrange("p (a t) -> p a t", t=TT))
        k_ETs.append(ke)
    st32, k_st = self.sg[:].rearrange("p (a t) -> p a t", t=256), ["sg"]
    Wq, Wk, Wv = D["swa_wq"][0], D["swa_wk"][0], D["swa_wv"][0]

    def evq(ci, m, b):
        P.add("act", lambda e: e.activation(out=qT[:, ci, 0:ntok], in_=self.ps[b][:, 0:ntok], func=AF.Copy, scale=0.125), r=[f"ps{b}"], w=k_qT)
    self.linear_fm(Wq, 0, 1024, lambda k: self.hT[:, k, 0:ntok], ["hT"], ntok, evq)
    if _SWASTOP == 1:
        P.add('dve', lambda e: e.memset(self.mixT[:], 0.0), w=['mixT'])
        return
    vk, kk_ = self.wload(Wk.rearrange("(c p) n -> p c n", p=128))
    for kv in range(4):
        b = self.bank()
        for half in range(2):
            for k in range(8):
                P.add("pe", lambda e, kv=kv, b=b, half=half, k=k: e.matmul(self.ps[b][64 * half:64 * half + 64, 0:ntok], lhsT=vk[:, k, kv * 64:(kv + 1) * 64],
                                                                           rhs=self.hT[:, k, 0:ntok], start=(k == 0), stop=(k == 7), tile_position=(0, 64 * half)),
                      r=[kk_, "hT"], w=[f"ps{b}"])
        P.add("act", lambda e, kv=kv, b=b: e.activation(out=self.kTd[:, kv, 128:128 + ntok], in_=self.ps[b][:, 0:ntok], func=AF.Copy), r=[f"ps{b}"], w=["kTd"])
    if _SWASTOP == 2:
        P.add('dve', lambda e: e.memset(self.mixT[:], 0.0), w=['mixT'])
        return
    if last:
        ob = nblk - 1
        b = self.bank()
        for k in range(8):
            P.add("pe", lambda e, b=b, k=k: e.matmul(self.ps[b][:, 0:256], lhsT=self.hT[:, k, ob * 128:(ob + 1) * 128], rhs=vk[:, k, :], start=(k == 0), stop=(k == 7)),
                  r=[kk_, "hT"], w=[f"ps{b}"])
        P.add("dve", lambda e, b=b: e.tensor_copy(out=st32[:, 0, :], in_=self.ps[b][:, 0:256]), r=[f"ps{b}"], w=k_st)
    if _SWASTOP == 3:
        P.add('dve', lambda e: e.memset(self.mixT[:], 0.0), w=['mixT'])
        return
    vv, kv_ = self.wload(Wv.rearrange("(c p) n -> p c n", p=128))
    for blk in range(min(nblk, int(os.environ.get('VBLK', '9')))):
        b = self.bank()
        for k in range(8):
            P.add("pe", lambda e, b=b, k=k, blk=blk: e.matmul(self.ps[b][:, 0:256], lhsT=self.hT[:, k, blk * 128:(blk + 1) * 128], rhs=vv[:, k, :],
                                                              start=(k == 0), stop=(k == 7)), r=[kv_, "hT"], w=[f"ps{b}"])
        P.add("act", lambda e, b=b, blk=blk: e.activation(out=self.vtm[:, 1 + blk, :], in_=self.ps[b][:, 0:256], func=AF.Copy), r=[f"ps{b}"], w=["vtm"])
        if last and blk == nblk - 1 and not (_RISK & 64):
            P.add("act", lambda e, b=b: e.activation(out=st32[:, 1, :], in_=self.ps[b][:, 0:256], func=AF.Copy), r=[f"ps{b}"], w=k_st)
    if last and not (_RISK & 32):
        if kind == "p":
            P.add("sp", lambda e: e.dma_start(out=D["p_k"][:, :], in_=st32[:, 0, :]), r=k_st, w=["ok"], dma="out")
            P.add("sp", lambda e: e.dma_start(out=D["p_v"][:, :], in_=st32[:, 1, :]), r=k_st, w=["ov"], dma="out")
        else:
            n0 = 128 - DEC_SEQ
            for j, (nm, src) in enumerate((("s_k", "st_k"), ("s_v", "st_v"))):
                P.add("sp", lambda e, nm=nm, src=src: e.dma_start(out=D[nm][si][0:n0, :], in_=D[src][si][DEC_SEQ:128, :]), w=[f"o{nm}a"], dma="out")
                P.add("sp", lambda e, nm=nm, j=j: e.dma_start(out=D[nm][si][n0:128, :], in_=st32[0:DEC_SEQ, j, :]), r=k_st, w=[f"o{nm}b"], dma="out")
    if _SWASTOP == 4:
        P.add('dve', lambda e: e.memset(self.mixT[:], 0.0), w=['mixT'])
        return
    mbc = self.cview("mb")
    steps = []
    for c in range(nch):
        if kind == "p":
            lo, hi = 64 * c, 64 * c + 192
            if first:
                lo = max(lo, 128)
        else:
            lo, hi = 0, 128 + nval
        pieces = []
        for blk in range(lo // 128, (hi - 1) // 128 + 1):
            a = max(lo, blk * 128) - blk * 128
            b_ = min(hi, (blk + 1) * 128) - blk * 128
            pieces.append((blk, a, b_))
        for gq in range(2):
            steps.append((c, gq, pieces))
    pS = [[self.ps[0], self.ps[1]], [self.ps[2], self.ps[3]]]
    kS = [["ps0", "ps1"], ["ps2", "ps3"]]

    def scores_exp(it):
        c, gq, pieces = steps[it]
        ET, k_ET = ETs[it % 2], k_ETs[it % 2]
        for pi, (blk, a, b_) in enumerate(pieces):
            mcol = {(0, 128): 0, (0, 64): 1, (64, 128): 2, (0, 16): 3}[(a, b_)]
            for mi in range(4):
                m = 4 * gq + mi
                kv = m // 2
                for half in range(2):
                    P.add("pe", lambda e, pi=pi, blk=blk, m=m, kv=kv, half=half, mi=mi, c=c: e.matmul(
                        pS[pi][half][:, mi * 64:(mi + 1) * 64], lhsT=self.kTd[64 * half:64 * half + 64, kv, blk * 128:(blk + 1) * 128],
                        rhs=qT[64 * half:64 * half + 64, m, c * 64:(c + 1) * 64], start=True, stop=True, tile_position=(64 * half, 0)),
                        r=["kTd", *k_qT], w=[kS[pi][half]])
            for half in range(2):
                P.add("act", lambda e, pi=pi, mcol=mcol, half=half, ET=ET: e.activation(
                    out=ET[:, pi, half * 256:(half + 1) * 256], in_=pS[pi][half][:, 0:256], func=AF.Exp, bias=mbc[:, mcol:mcol + 1]),
                    r=[kS[pi][half], "cst"], w=k_ET)

    def pv_out(it):
        c, gq, pieces = steps[it]
        ET, k_ET = ETs[it % 2], k_ETs[it % 2]
        pOD = self.ps[4 + it % 2]
        kOD = f"ps{4 + it % 2}"
        np_ = len(pieces)
        for mi in range(4):
            m = 4 * gq + mi
            kv = m // 2
            for half in range(2):
                col = (half * 4 + mi) * 64
                for pi, (blk, a, b_) in enumerate(pieces):
                    P.add("pe", lambda e, pi=pi, blk=blk, kv=kv, half=half, col=col, mi=mi: e.matmul(
                        pOD[64 * half:64 * half + 64, mi * 64:(mi + 1) * 64], lhsT=self.vtm[:, blk, kv * 64:(kv + 1) * 64],
                        rhs=ET[:, pi, col:col + 64], start=(pi == 0), stop=(pi == np_ - 1), tile_position=(0, 64 * half)),
                        r=["vtm", *k_ET], w=[kOD])
                for pi, (blk, a, b_) in enumerate(pieces):
                    P.add("pe", lambda e, pi=pi, half=half, col=col, mi=mi: e.matmul(
                        pOD[64 * half:64 * half + 64, 256 + mi * 64:256 + (mi + 1) * 64], lhsT=self.onesb[:, :],
                        rhs=ET[:, pi, col:col + 64], start=(pi == 0), stop=(pi == np_ - 1), tile_position=(0, 64 * half)),
                        r=["onesb", *k_ET], w=[kOD])
        rc3 = self.rcp.rearrange("p (m i) -> p m i", i=64)
        P.add("dve", lambda e: e.tensor_tensor(out=rc3, in0=pOD[:, 256:512].rearrange("p (m i) -> p m i", i=64),
                                               in1=self.esink[:, 4 * gq:4 * gq + 4].unsqueeze(2).to_broadcast([128, 4, 64]), op=ALU.add),
              r=[kOD, "esink"], w=["sg"])
        P.add("dve", lambda e: e.reciprocal(out=self.rcp, in_=self.rcp), r=["sg"], w=["sg"])
        P.add("dve", lambda e: e.tensor_tensor(out=self.mixT[:, 4 * gq:4 * gq + 4, c * 64:(c + 1) * 64],
                                               in0=pOD[:, 0:256].rearrange("p (m i) -> p m i", i=64), in1=rc3, op=ALU.mult),
              r=[kOD, "sg"], w=["mixT"])

    scores_exp(0)
    for it in range(len(steps)):
        if it + 1 < len(steps):
            scores_exp(it + 1)
        pv_out(it)
    if kind != "p" and ntok > 64:
        P.add("dve", lambda e: e.memset(self.mixT[:, :, 64:ntok], 0.0), w=["mixT"])
    if _SWASTOP == 5:
        P.add('dve', lambda e: e.memset(self.mixT[:], 0.0), w=['mixT'])
        return
    P.add("dve", lambda e: e.tensor_copy(out=self.kTd[:, :, 0:128], in_=self.kTd[:, :, ntok:ntok + 128]), r=["kTd"], w=["kTd"])
    P.add("dve", lambda e: e.tensor_copy(out=self.vtm[:, 0, :], in_=self.vtm[:, nblk, :]), r=["vtm"], w=["vtm"])
    self.linear_tm_res(D["swa_wo"][0], lambda k, b: self.mixT[:, k, b * 128:(b + 1) * 128], ["mixT"], nblk)


Builder.alloc_swa = alloc_swa
Builder.swa_setup = swa_setup
Builder.swa_init_sample = swa_init_sample
Builder.swa_tile = swa_tile
```

```python
import contextlib
import numpy as np
import concourse.bass as bass
import concourse.mybir as mybir
from concourse.bass_utils import run_bass_kernel_spmd

F32 = mybir.dt.float32
BF16 = mybir.dt.bfloat16
ALU = mybir.AluOpType
AF = mybir.ActivationFunctionType
AX = mybir.AxisListType

import os
_S5STOP = int(os.environ.get('S5STOP', '0'))
_RISK = int(os.environ.get('RISK', '0'))
_ILV = int(os.environ.get('ILV', '3'))
_PRENG = os.environ.get('PRENG', 'dve')
_PREF = int(os.environ.get('PREF', '1'))
_SWASTOP = int(os.environ.get('SWASTOP', '0'))
_ATT = int(os.environ.get('ATT', '9'))
NCORES = 8
DM = 1024
SEQ = 8192
TT = 512
DEC_SEQ = 16
IN_COLS = 2568
DFF = 2816
EPS = 1e-6


class _Op:
    __slots__ = ("eng", "fn", "deps", "sig", "seq", "dsem", "dval", "n")

    def __init__(self, eng, fn):
        self.eng = eng
        self.fn = fn
        self.deps = []
        self.sig = False
        self.seq = 0
        self.dsem = None
        self.dval = 0
        self.n = 0


class Prog:
    ENGS = ("pe", "act", "dve", "pool", "sp")

    def __init__(self, nc):
        self.nc = nc
        self.q = {e: [] for e in self.ENGS}
        self.st = {}
        self.dcount = {}
        self.groups = {}

    def _expand(self, keys):
        out = []
        for k in keys:
            out.extend(self.groups.get(k, (k,)))
        return out

    def add(self, eng, fn, r=(), w=(), dma=None):
        op = _Op(eng, fn)
        r0, w0 = r, w
        r, w = self._expand(r), self._expand(w)
        deps = []
        for k in r:
            s = self.st.get(k)
            if s is not None and s[0] is not None:
                deps.append(s[0])
        for k in w:
            s = self.st.get(k)
            if s is not None:
                if s[0] is not None:
                    deps.append(s[0])
                deps.extend(s[1])
        is_dma = dma is not None
        op.n = self.nops = getattr(self, "nops", 0) + 1
        rawset = set()
        for k in r:
            s_ = self.st.get(k)
            if s_ is not None and s_[0] is not None:
                rawset.add(id(s_[0]))
        latest = {}
        for d in deps:
            if d.dsem is not None:
                continue
            if d.eng == eng and not is_dma and id(d) not in rawset:
                continue
            if d.eng not in latest or d.n > latest[d.eng].n:
                latest[d.eng] = d
        deps = [d for d in deps if d.dsem is not None or latest.get(d.eng) is d]
        if dma in ("init", "ldst", "ldkv"):
            dma = dma[0] + "_" + w0[0]
        elif dma == "out":
            dma = "o_" + (r0[0] if len(r0) else "dram")
        seen = set()
        for d in deps:
            if id(d) in seen or d is op:
                continue
            seen.add(id(d))
            if d.dsem is None and d.eng == eng and not is_dma:
                if eng == "pe":
                    continue
                israw = False
                for k in r:
                    s = self.st.get(k)
                    if s is not None and s[0] is d:
                        israw = True
                if not israw:
                    continue
            op.deps.append((d, self.dcount[d.dsem] if d.dsem is not None else 0))
            if d.dsem is None:
                d.sig = True
        for k in r:
            s = self.st.setdefault(k, [None, []])
            s[1].append(op)
        for k in w:
            self.st[k] = [op, []]
        if is_dma:
            op.dsem = dma
            self.dcount[dma] = self.dcount.get(dma, 0) + 16
            op.dval = self.dcount[dma]
        self.q[eng].append(op)
        return op

    def emit(self, final_eng="sp"):
        nc = self.nc
        with contextlib.ExitStack() as es:
            esem = {e: es.enter_context(nc.semaphore("S_" + e)) for e in self.ENGS}
            dsem = {n: es.enter_context(nc.semaphore("D_" + n)) for n in self.dcount}
            for e in self.ENGS:
                c = 0
                for op in self.q[e]:
                    if op.sig:
                        c += 1
                        op.seq = c
            block = es.enter_context(nc.Block())

            def run(e, eng):
                waited = {}
                for op in self.q[e]:
                    for d, dv in op.deps:
                        if d.dsem is not None:
                            key, sem, val = ("d", d.dsem), dsem[d.dsem], dv
                        else:
                            key, sem, val = ("e", d.eng), esem[d.eng], d.seq
                        if waited.get(key, 0) >= val:
                            continue
                        waited[key] = val
                        eng.wait_ge(sem, val)
                    ins = op.fn(eng)
                    if op.dsem is not None:
                        ins.then_inc(dsem[op.dsem], 16)
                    elif op.sig:
                        ins.then_inc(esem[e], 1)
                if e == final_eng:
                    for n, cnt in self.dcount.items():
                        eng.wait_ge(dsem[n], cnt)

            @block.tensor
            def _(eng):
                run("pe", eng)

            @block.scalar
            def _(eng):
                run("act", eng)

            @block.vector
            def _(eng):
                run("dve", eng)

            @block.gpsimd
            def _(eng):
                run("pool", eng)

            @block.sync
            def _(eng):
                run("sp", eng)


def _consts():
    c = {}
    c["ident"] = np.eye(128, dtype=np.float32)
    j = np.arange(128)[:, None] % 64
    i = np.arange(64)[None, :]
    c["mUs"] = (i > j).astype(np.float32)
    c["mUi"] = (i >= j).astype(np.float32)
    c["eye64"] = (i == j).astype(np.float32)
    sel = np.zeros((128, 8, 128), np.float32)
    for r in range(8):
        sel[r, r, :] = 1.0
    c["sel"] = sel.reshape(128, 8 * 128)
    c["ones"] = np.ones((128, 128), np.float32)
    m = np.ones((128, 512), np.float32)
    m[:, ::64] = 0.0
    c["cmask"] = m
    selp = np.zeros((128, 4, 2), np.float32)
    for h in range(4):
        selp[h, h, 0] = 1.0
        selp[4 + h, h, 1] = 1.0
    c["selp"] = selp.reshape(128, 8)
    gm = np.zeros((128, 4), np.float32)
    gm[0:4, 0] = 1.0
    gm[4:8, 1] = 1.0
    gm[4:8, 2] = -1.0
    c["gm"] = gm
    mb = np.zeros((128, 4), np.float32)
    mb[64:, 1] = -30000.0
    mb[:64, 2] = -30000.0
    mb[16:, 3] = -30000.0
    c["mb"] = mb
    off = {}
    cols = 0
    for k, v in c.items():
        off[k] = (cols, v.shape[1])
        cols += v.shape[1]
    arr = np.concatenate([c[k] for k in c], axis=1)
    return arr, off


_CARR, _COFF = _consts()

W_SPECS = [
    ("norm_mix", (2, 1024)), ("norm_ffn", (2, 1024)), ("norm_final", (1024,)), ("w_in", (1, 1024, IN_COLS)),
    ("s5_lam_re", (1, 32, 64)), ("s5_lam_im", (1, 32, 64)), ("s5_log_dt", (1, 32)),
    ("s5_b_re", (1, 32, 64, 16)), ("s5_b_im", (1, 32, 64, 16)), ("s5_c_re", (1, 32, 16, 64)), ("s5_c_im", (1, 32, 16, 64)),
    ("s5_d", (1, 512)), ("s5_w_glu", (1, 512, 512)), ("s5_b_glu", (1, 512)),
    ("gdn_conv_w", (1, 4, 1536)), ("gdn_a_log", (1, 4)), ("gdn_dt_bias", (1, 4)), ("gdn_norm_w", (1, 128)),
    ("w_out_ab", (1, 1024, 1024)), ("swa_wq", (1, 1024, 1024)), ("swa_wk", (1, 1024, 256)), ("swa_wv", (1, 1024, 256)),
    ("swa_sinks", (1, 16)), ("swa_wo", (1, 1024, 1024)),
    ("ffn_w_gate", (2, 1024, DFF)), ("ffn_w_up", (2, 1024, DFF)), ("ffn_w_down", (2, DFF, 1024)),
]
IN_SPECS = [
    ("xp", (SEQ, DM)), ("xs", (2, DEC_SEQ, DM)),
    ("st_s5_re", (2, 32, 64)), ("st_s5_im", (2, 32, 64)), ("st_gdn", (2, 4, 128, 128)), ("st_conv", (2, 3, 1536)),
    ("st_k", (2, 128, 256)), ("st_v", (2, 128, 256)),
    ("consts", _CARR.shape),
]
OUT_SPECS = [
    ("y_p", (SEQ, DM)), ("y_s", (2, DEC_SEQ, DM)),
    ("p_s5_re", (32, 64)), ("p_s5_im", (32, 64)), ("p_gdn", (4, 128, 128)), ("p_conv", (3, 1536)),
    ("p_k", (128, 256)), ("p_v", (128, 256)),
    ("s_s5_re", (2, 32, 64)), ("s_s5_im", (2, 32, 64)), ("s_gdn", (2, 4, 128, 128)), ("s_conv", (2, 3, 1536)),
    ("s_k", (2, 128, 256)), ("s_v", (2, 128, 256)),
]


class Builder:
    def __init__(self, stages=99):
        self.stages = stages
        self.nc = bass.Bass("TRN2", target_bir_lowering=False)
        nc = self.nc
        self.D = {}
        for n, s in IN_SPECS + W_SPECS:
            self.D[n] = nc.dram_tensor(n, list(s), F32, kind="ExternalInput").ap()
        for n, s in OUT_SPECS:
            self.D[n] = nc.dram_tensor(n, list(s), F32, kind="ExternalOutput").ap()
        self.es = contextlib.ExitStack()
        self.P = Prog(nc)
        self.P.groups.update({"hT": ["hT_q", "hT_d"], "ys": ["ys_0", "ys_1", "ys_2", "ys_3"], "ub": ["ub_0", "ub_1", "ub_23"],
                              "xres": ["xres0", "xres1", "xres2", "xres3"], "hidA": [f"hid{i}" for i in range(22)], "S32": ["S32p0", "S32p1"], "Sb": ["Sbp0", "Sbp1"], "tok": ["tok0", "tok1"], "egl": ["egl0", "egl1"]})
        self.nslot = 4
        self.slot_i = 0
        self.bank_i = 0

    def sb(self, name, shape, dt):
        return self.es.enter_context(self.nc.sbuf_tensor(name, list(shape), dt))

    def psum(self, name, shape, dt):
        return self.es.enter_context(self.nc.psum_tensor(name, list(shape), dt))

    def alloc(self):
        sb = self.sb
        self.cst = sb("cst", [128, _CARR.shape[1]], F32)
        self.identb = sb("identb", [128, 128], BF16)
        self.xres = sb("xres", [128, 4, DM], F32)
        self.xn = sb("xn", [128, DM], BF16)
        self.ss = sb("ss", [128, 8], F32)
        self.ss2 = sb("ss2", [128, 8], F32)
        self.hT = sb("hT", [128, 8, TT], BF16)
        self.nw = sb("nw", [128, 5, 8], F32)
        self.ub = sb("ub", [128, 4, TT], BF16)
        self.qkvb = sb("qkvb", [128, 12, 3 + TT], BF16)
        self.cv32 = sb("cv32", [128, 12, 3], F32)
        self.zs = sb("zs", [128, 4, TT], BF16)
        self.ba = sb("ba", [8, TT], F32)
        self.mixT = sb("mixT", [128, 8, TT], BF16)
        self.hid = sb("hid", [128, 22, TT], BF16)
        self.sg = sb("sg", [128, TT], F32)
        self.slots = [sb(f"wslot{i}", [128, 4096], BF16) for i in range(self.nslot)]
        self.ps = [self.psum(f"ps{i}", [128, 512], F32) for i in range(7)]
        self.pT = self.psum("pT", [128, 1024], BF16)

    def cview(self, name, rows=128):
        o, n = _COFF[name]
        return self.cst[0:rows, o:o + n]

    def wload(self, src3):
        i = self.slot_i
        self.slot_i = (i + 1) % self.nslot
        kc, n = src3.shape[1], src3.shape[2]
        assert kc * n <= 4096
        view = self.slots[i][:, 0:kc * n].rearrange("p (k n) -> p k n", n=n)
        key = f"wslot{i}"
        self.P.add("pool", lambda e: e.dma_start(out=view, in_=src3), w=[key], dma=key)
        return view, key

    def bank(self, lo=0, hi=4):
        b = lo + self.bank_i % (hi - lo)
        self.bank_i += 1
        return b

    def setup(self):
        P, D = self.P, self.D
        P.add("sp", lambda e: e.dma_start(out=self.cst[:], in_=D["consts"][:, :]), w=["cst"], dma="init")
        idv = self.cview("ident")
        P.add("dve", lambda e: e.tensor_copy(out=self.identb[:], in_=idv), r=["cst"], w=["identb"])
        srcs = [D["norm_mix"][0], D["norm_ffn"][0], D["norm_mix"][1], D["norm_ffn"][1]]
        for i, s in enumerate(srcs):
            P.add("sp", lambda e, i=i, s=s: e.dma_start(out=self.nw[:, i, :], in_=s.rearrange("(c p) -> p c", p=128),
                                                        allow_slow_non_contiguous=True), w=["nw"], dma="init")

    def norm_T(self, nblk, widx, src=None):
        P = self.P
        if src is None:
            src = [(self.xres[:, b, :], [f"xres{b}"]) for b in range(nblk)]
        P.add("dve", lambda e: e.memset(self.ss[:], 0.0), w=["ss"])
        for b in range(nblk):
            xa, xk = src[b]
            P.add("act", lambda e, b=b, xa=xa: e.activation(out=self.xn[:], in_=xa, func=AF.Square,
                                                            accum_out=self.ss[:, b:b + 1]), r=xk, w=["xn", "ss"])
        P.add("act", lambda e: e.activation(out=self.ss[:, 4:4 + nblk], in_=self.ss[:, 0:nblk], func=AF.Sqrt, scale=1.0 / DM, bias=EPS), r=["ss"], w=["ss"])
        P.add("dve", lambda e: e.reciprocal(out=self.ss[:, 4:4 + nblk], in_=self.ss[:, 4:4 + nblk]), r=["ss"], w=["ss"])
        for b in range(nblk):
            xa, xk = src[b]
            P.add("act", lambda e, b=b, xa=xa: e.activation(out=self.xn[:], in_=xa, func=AF.Copy,
                                                            scale=self.ss[:, 4 + b:5 + b]), r=[*xk, "ss"], w=["xn"])
            for c in range(8):
                P.add("pe", lambda e, c=c: e.transpose(out=self.pT[:, c * 128:(c + 1) * 128], in_=self.xn[:, c * 128:(c + 1) * 128],
                                                       identity=self.identb[:]), r=["xn", "identb"], w=["pT"])
            P.add("dve", lambda e, b=b: e.tensor_tensor(
                out=self.hT[:, :, b * 128:(b + 1) * 128], in0=self.pT[:].rearrange("p (c t) -> p c t", t=128),
                in1=self.nw[:, widx, :].unsqueeze(2).to_broadcast([128, 8, 128]), op=ALU.mult), r=["pT", "nw"], w=["hT"])

    def linear_fm(self, W2, col0, ncols, rhs_fn, rkeys, ntok, evac, piece=512, banks=(0, 4)):
        P = self.P
        K = W2.shape[0]
        kc = K // 128
        W3 = W2.rearrange("(c p) n -> p c n", p=128)
        for p0 in range(col0, col0 + ncols, piece):
            pc = min(piece, col0 + ncols - p0)
            view, key = self.wload(W3[:, :, p0:p0 + pc])
            for c0 in range(0, pc, 128):
                m = min(128, pc - c0)
                b = self.bank(*banks)
                for k in range(kc):
                    P.add("pe", lambda e, b=b, k=k, c0=c0, m=m, view=view: e.matmul(
                        self.ps[b][0:m, 0:ntok], lhsT=view[:, k, c0:c0 + m], rhs=rhs_fn(k), start=(k == 0), stop=(k == kc - 1)),
                        r=[key] + rkeys, w=[f"ps{b}"])
                evac((p0 + c0) // 128, m, b)

    def linear_tm_res(self, W2, act_fn, akeys, nblk):
        P = self.P
        K = W2.shape[0]
        kc = K // 128
        W3 = W2.rearrange("(c p) n -> p c n", p=128)
        sets = [[0, 1, 2, 3], [4, 5, 6, 3]]
        for ch in range(2):
            banks = sets[ch]
            for k0 in range(0, kc, 8):
                k1 = min(kc, k0 + 8)
                view, key = self.wload(W3[:, k0:k1, ch * 512:(ch + 1) * 512])
                for b in range(nblk):
                    for k in range(k0, k1):
                        P.add("pe", lambda e, b=b, k=k, k0=k0, view=view, banks=banks: e.matmul(
                            self.ps[banks[b]][:, :], lhsT=act_fn(k, b), rhs=view[:, k - k0, :], start=(k == 0), stop=(k == kc - 1)),
                            r=[key] + akeys, w=[f"ps{banks[b]}"])
            for b in range(nblk):
                P.add("dve", lambda e, b=b, ch=ch, banks=banks: e.tensor_tensor(
                    out=self.xres[:, b, ch * 512:(ch + 1) * 512], in0=self.xres[:, b, ch * 512:(ch + 1) * 512],
                    in1=self.ps[banks[b]][:, :], op=ALU.add), r=[f"xres{b}", f"ps{banks[b]}"], w=[f"xres{b}"])

    def ffn(self, layer, nblk, ntok):
        P, D = self.P, self.D
        self.norm_T(nblk, 1 + 2 * layer)
        Wg, Wu, Wd = D["ffn_w_gate"][layer], D["ffn_w_up"][layer], D["ffn_w_down"][layer]
        Wg3 = Wg.rearrange("(c p) n -> p c n", p=128)
        Wu3 = Wu.rearrange("(c p) n -> p c n", p=128)
        for p0 in range(0, DFF, 512):
            pc = min(512, DFF - p0)
            vg, kg = self.wload(Wg3[:, :, p0:p0 + pc])
            vu, ku = self.wload(Wu3[:, :, p0:p0 + pc])
            for c0 in range(0, pc, 128):
                f = (p0 + c0) // 128
                bg = self.bank(0, 6)
                bu = self.bank(0, 6)
                for k in range(8):
                    P.add("pe", lambda e, k=k, c0=c0, bg=bg, vg=vg: e.matmul(
                        self.ps[bg][:, 0:ntok], lhsT=vg[:, k, c0:c0 + 128], rhs=self.hT[:, k, 0:ntok], start=(k == 0), stop=(k == 7)),
                        r=[kg, "hT"], w=[f"ps{bg}"])
                for k in range(8):
                    P.add("pe", lambda e, k=k, c0=c0, bu=bu, vu=vu: e.matmul(
                        self.ps[bu][:, 0:ntok], lhsT=vu[:, k, c0:c0 + 128], rhs=self.hT[:, k, 0:ntok], start=(k == 0), stop=(k == 7)),
                        r=[ku, "hT"], w=[f"ps{bu}"])
                P.add("act", lambda e, bg=bg: e.activation(out=self.sg[:, 0:ntok], in_=self.ps[bg][:, 0:ntok], func=AF.Silu),
                      r=[f"ps{bg}"], w=["sg"])
                P.add("dve", lambda e, bu=bu, f=f: e.tensor_tensor(out=self.hid[:, f, 0:ntok], in0=self.sg[:, 0:ntok],
                                                                    in1=self.ps[bu][:, 0:ntok], op=ALU.mult),
                      r=["sg", f"ps{bu}"], w=[f"hid{f}"])
        hk = [f"hid{f}" for f in range(22)]
        self.linear_tm_res(Wd, lambda k, b: self.hid[:, k, b * 128:(b + 1) * 128], hk, nblk)

    def alloc_l0(self):
        sb = self.sb
        self.s5p = sb("s5p", [128, 16, 24], F32)
        self.Bt = sb("Bt", [128, 2, 16, 128], BF16)
        self.Ct = sb("Ct", [128, 2, 16, 32], BF16)
        self.rot = sb("rot", [128, 2, 16, 64], F32)
        self.Bt1 = sb("Bt1", [128, 2, 16, 128], BF16)
        self.Ct1 = sb("Ct1", [128, 2, 16, 32], BF16)
        self.K0T = sb("K0T", [128, 4, 128], BF16)
        self.xst = sb("xst", [128, 2, 16], F32)
        self.s5tt = sb("s5tt", [128, 6, 512], F32)
        self.s5t = [self.s5tt[:, i, :] for i in range(6)]
        self.g5tt = sb("g5tt", [128, 6, 512], F32)
        self.g5t = [self.g5tt[:, i, :] for i in range(6)]
        self.nfin = self.s5tt[:, 0:2, :].rearrange("p a t -> p (a t)")
        self.xrb = sb("xrb", [128, 2, 2, 512], BF16)
        self.ys = sb("ys", [128, 4, TT], F32)
        self.Bz = self.xres[:].rearrange("p b d -> p (b d)").rearrange("p (i g c) -> p i g c", i=2, g=16)
        self.Cz = self.ys[:].rearrange("p a t -> p (a t)")[:, 0:1024].rearrange("p (i g c) -> p i g c", i=2, g=16)
        self.dcol = sb("dcol", [128, 4], F32)
        self.bglu = sb("bglu", [128, 4], F32)
        self.Braw = self.sg[:].rearrange("p (i g c) -> p i g c", i=2, g=16)
        self.zb = self.ub

    def s5_setup(self):
        P, D = self.P, self.D
        p = self.s5p
        PI = float(np.pi)
        for nm, col in (("s5_lam_re", 0), ("s5_lam_im", 1)):
            P.add("sp", lambda e, nm=nm, col=col: e.dma_start(
                out=p[:, :, col], in_=D[nm][0].rearrange("(gp gl) p -> (gl p) gp", gl=2), allow_slow_non_contiguous=True),
                w=["s5p"], dma="init")
        ld = D["s5_log_dt"][0].rearrange("(gp gl) -> gl gp", gl=2)
        for gl in range(2):
            P.add("sp", lambda e, gl=gl: e.dma_start(out=p[gl * 64:(gl + 1) * 64, :, 2], in_=ld[gl].partition_broadcast(64),
                                                    allow_slow_non_contiguous=True), w=["s5p"], dma="init")

        def c(i):
            return p[:, :, i]
        k = ["s5p"]
        P.add("act", lambda e: e.activation(out=c(3), in_=c(2), func=AF.Exp), r=k, w=k)
        P.add("dve", lambda e: e.tensor_tensor(out=c(4), in0=c(0), in1=c(3), op=ALU.mult), r=k, w=k)
        P.add("dve", lambda e: e.tensor_tensor(out=c(5), in0=c(1), in1=c(3), op=ALU.mult), r=k, w=k)
        P.add("act", lambda e: e.activation(out=c(6), in_=c(4), func=AF.Exp), r=k, w=k)
        P.add("act", lambda e: e.activation(out=c(7), in_=c(5), func=AF.Sin, scale=1.0 / 16), r=k, w=k)
        P.add("act", lambda e: e.activation(out=c(9), in_=c(5), func=AF.Sin, scale=1.0 / 8), r=k, w=k)
        P.add("dve", lambda e: e.tensor_tensor(out=c(8), in0=c(7), in1=c(7), op=ALU.mult), r=k, w=k)
        P.add("dve", lambda e: e.tensor_scalar(out=c(10), in0=c(8), scalar1=-2.0, scalar2=1.0, op0=ALU.mult, op1=ALU.add), r=k, w=k)
        for _ in range(3):
            P.add("dve", lambda e: e.tensor_tensor(out=c(15), in0=c(10), in1=c(10), op=ALU.mult), r=k, w=k)
            P.add("dve", lambda e: e.tensor_tensor(out=c(16), in0=c(9), in1=c(9), op=ALU.mult), r=k, w=k)
            P.add("dve", lambda e: e.tensor_tensor(out=c(8), in0=c(9), in1=c(10), op=ALU.mult), r=k, w=k)
            P.add("dve", lambda e: e.tensor_scalar(out=c(9), in0=c(8), scalar1=2.0, scalar2=None, op0=ALU.mult), r=k, w=k)
            P.add("dve", lambda e: e.tensor_tensor(out=c(10), in0=c(15), in1=c(16), op=ALU.subtract), r=k, w=k)
        P.add("dve", lambda e: e.tensor_tensor(out=c(11), in0=c(6), in1=c(10), op=ALU.mult), r=k, w=k)
        P.add("dve", lambda e: e.tensor_tensor(out=c(12), in0=c(6), in1=c(9), op=ALU.mult), r=k, w=k)
        P.add("dve", lambda e: e.tensor_scalar(out=c(13), in0=c(11), scalar1=-1.0, scalar2=None, op0=ALU.add), r=k, w=k)
        P.add("dve", lambda e: e.tensor_tensor(out=c(14), in0=c(0), in1=c(0), op=ALU.mult), r=k, w=k)
        P.add("dve", lambda e: e.tensor_tensor(out=c(15), in0=c(1), in1=c(1), op=ALU.mult), r=k, w=k)
        P.add("dve", lambda e: e.tensor_tensor(out=c(14), in0=c(14), in1=c(15), op=ALU.add), r=k, w=k)
        P.add("dve", lambda e: e.reciprocal(out=c(14), in_=c(14)), r=k, w=k)
        P.add("dve", lambda e: e.tensor_tensor(out=c(15), in0=c(13), in1=c(0), op=ALU.mult), r=k, w=k)
        P.add("dve", lambda e: e.tensor_tensor(out=c(16), in0=c(12), in1=c(1), op=ALU.mult), r=k, w=k)
        P.add("dve", lambda e: e.tensor_tensor(out=c(15), in0=c(15), in1=c(16), op=ALU.add), r=k, w=k)
        P.add("dve", lambda e: e.tensor_tensor(out=c(17), in0=c(15), in1=c(14), op=ALU.mult), r=k, w=k)
        P.add("dve", lambda e: e.tensor_tensor(out=c(15), in0=c(12), in1=c(0), op=ALU.mult), r=k, w=k)
        P.add("dve", lambda e: e.tensor_tensor(out=c(16), in0=c(13), in1=c(1), op=ALU.mult), r=k, w=k)
        P.add("dve", lambda e: e.tensor_tensor(out=c(15), in0=c(15), in1=c(16), op=ALU.subtract), r=k, w=k)
        P.add("dve", lambda e: e.tensor_tensor(out=c(18), in0=c(15), in1=c(14), op=ALU.mult), r=k, w=k)
        if _S5STOP == 1:
            return
        for i, nm in enumerate(("s5_b_re", "s5_b_im")):
            P.add("sp", lambda e, i=i, nm=nm: e.dma_start(out=self.Braw[:, i, :, :],
                                                          in_=D[nm][0].rearrange("(gp gl) p c -> (gl p) gp c", gl=2)),
                  w=["sg"], dma="init")
        P.add("dve", lambda e: e.memset(self.Bz[:], 0.0), w=["xres"])
        P.add("dve", lambda e: e.memset(self.Cz[:], 0.0), w=["ys"])
        fre = p[:, :, 17:18].to_broadcast([128, 16, 16])
        fim = p[:, :, 18:19].to_broadcast([128, 16, 16])
        t0 = self.s5t[0][:, 0:256].rearrange("p (g c) -> p g c", c=16)
        t1 = self.s5t[1][:, 0:256].rearrange("p (g c) -> p g c", c=16)
        t2 = self.s5t[2][:, 0:256].rearrange("p (g c) -> p g c", c=16)
        bb = [self.s5t[3][:, 0:256].rearrange("p (g c) -> p g c", c=16), self.s5t[4][:, 0:256].rearrange("p (g c) -> p g c", c=16)]
        kk = ["s5p", "sg"]
        P.add("dve", lambda e: e.tensor_tensor(out=t0, in0=self.Braw[:, 0], in1=fre, op=ALU.mult), r=kk, w=["s5t0"])
        P.add("dve", lambda e: e.tensor_tensor(out=t1, in0=self.Braw[:, 1], in1=fim, op=ALU.mult), r=kk, w=["s5t1"])
        P.add("dve", lambda e: e.tensor_tensor(out=bb[0], in0=t0, in1=t1, op=ALU.subtract), r=["s5t0", "s5t1"], w=["s5t3"])
        for gl in range(2):
            for r in range(4):
                P.add("dve", lambda e, gl=gl, r=r: e.tensor_copy(
                    out=self.Bz[gl * 64:(gl + 1) * 64, 0].rearrange("p (cb r) c -> p cb r c", r=4)[:, :, r, 32 * r + gl * 16:32 * r + gl * 16 + 16],
                    in_=bb[0][gl * 64:(gl + 1) * 64].rearrange("p (cb r) c -> p cb r c", r=4)[:, :, r, :]), r=["s5t3"], w=["xres"])
        P.add("dve", lambda e: e.tensor_tensor(out=t0, in0=self.Braw[:, 1], in1=fre, op=ALU.mult), r=kk, w=["s5t0"])
        P.add("dve", lambda e: e.tensor_tensor(out=t1, in0=self.Braw[:, 0], in1=fim, op=ALU.mult), r=kk, w=["s5t1"])
        P.add("dve", lambda e: e.tensor_tensor(out=bb[1], in0=t0, in1=t1, op=ALU.add), r=["s5t0", "s5t1"], w=["s5t4"])
        for gl in range(2):
            for r in range(4):
                P.add("dve", lambda e, gl=gl, r=r: e.tensor_copy(
                    out=self.Bz[gl * 64:(gl + 1) * 64, 1].rearrange("p (cb r) c -> p cb r c", r=4)[:, :, r, 32 * r + gl * 16:32 * r + gl * 16 + 16],
                    in_=bb[1][gl * 64:(gl + 1) * 64].rearrange("p (cb r) c -> p cb r c", r=4)[:, :, r, :]), r=["s5t4"], w=["xres"])
        if _S5STOP == 2:
            return
        idf = self.cview("ident")
        for i in range(2):
            for gp in range(16):
                P.add("pe", lambda e, i=i, gp=gp: e.matmul(self.ps[0][:, 0:128], lhsT=self.Bz[:, i, gp, :], rhs=idf, start=True, stop=True),
                      r=["xres", "cst"], w=["ps0"])
                P.add("dve", lambda e, i=i, gp=gp: e.tensor_copy(out=self.Bt[:, i, gp, :], in_=self.ps[0][:, 0:128]), r=["ps0"], w=["Bt"])
        if _S5STOP == 3:
            return
        for i, nm in enumerate(("s5_c_re", "s5_c_im")):
            for g in range(32):
                gp, gl = g // 2, g % 2
                P.add("sp", lambda e, i=i, nm=nm, g=g, gp=gp, gl=gl: e.dma_start(
                    out=self.Cz[gl * 64:(gl + 1) * 64, i, gp, gl * 16:(gl + 1) * 16], in_=D[nm][0][g].rearrange("c p -> p c"),
                    allow_slow_non_contiguous=True), w=["ys"], dma="init")
        P.add("dve", lambda e: e.tensor_copy(out=self.Ct[:, 0], in_=self.Cz[:, 0]), r=["ys"], w=["Ct"])
        P.add("dve", lambda e: e.tensor_scalar(out=self.Ct[:, 1], in0=self.Cz[:, 1], scalar1=-1.0, scalar2=None, op0=ALU.mult), r=["ys"], w=["Ct"])
        if _S5STOP == 4:
            return
        scrF = self.hid[:].rearrange("p a b -> p (a b)").bitcast(F32)
        rotfull = scrF[:, 0:4096].rearrange("p (i g t) -> p i g t", i=2, g=16)
        cs, sn = rotfull[:, 0], rotfull[:, 1]
        P.add("dve", lambda e: e.tensor_copy(out=cs[:, :, 0:1], in_=p[:, :, 10:11]), r=k, w=["hidA"])
        P.add("dve", lambda e: e.tensor_copy(out=sn[:, :, 0:1], in_=p[:, :, 9:10]), r=k, w=["hidA"])
        L = 1
        while L < 128:
            c1 = cs[:, :, L - 1:L].to_broadcast([128, 16, L])
            s1 = sn[:, :, L - 1:L].to_broadcast([128, 16, L])
            sc0 = self.g5tt[:, 0:4, :].rearrange("p a (g t) -> p (a g) t", g=4)[:, :, 0:L]
            sc1 = self.g5tt[:, 4:6, :].rearrange("p a (g t) -> p (a g) t", g=8)[:, :, 0:L]
            P.add("dve", lambda e, L=L, c1=c1, sc0=sc0: e.tensor_tensor(out=sc0, in0=cs[:, :, 0:L], in1=c1, op=ALU.mult), r=["hidA"], w=["g5t0"])
            P.add("dve", lambda e, L=L, s1=s1, sc1=sc1: e.tensor_tensor(out=sc1, in0=sn[:, :, 0:L], in1=s1, op=ALU.mult), r=["hidA"], w=["g5t4", "g5t5"])
            P.add("dve", lambda e, L=L, sc0=sc0, sc1=sc1: e.tensor_tensor(out=cs[:, :, L:2 * L], in0=sc0, in1=sc1, op=ALU.subtract),
                  r=["g5t0", "g5t4", "g5t5", "hidA"], w=["hidA"])
            P.add("dve", lambda e, L=L, s1=s1, sc0=sc0: e.tensor_tensor(out=sc0, in0=cs[:, :, 0:L], in1=s1, op=ALU.mult), r=["hidA"], w=["g5t0"])
            P.add("dve", lambda e, L=L, c1=c1, sc1=sc1: e.tensor_tensor(out=sc1, in0=sn[:, :, 0:L], in1=c1, op=ALU.mult), r=["hidA"], w=["g5t4", "g5t5"])
            P.add("dve", lambda e, L=L, sc0=sc0, sc1=sc1: e.tensor_tensor(out=sn[:, :, L:2 * L], in0=sc0, in1=sc1, op=ALU.add),
                  r=["g5t0", "g5t4", "g5t5", "hidA"], w=["hidA"])
            L *= 2
        for i in range(2):
            P.add("dve", lambda e, i=i: e.tensor_copy(out=self.rot[:, i], in_=rotfull[:, i].rearrange("p g (n two) -> p g n two", two=2)[:, :, :, 1]),
                  r=["hidA"], w=["rot"])
        P.add("dve", lambda e: e.tensor_tensor(out=c(22), in0=c(6), in1=c(6), op=ALU.mult), r=k, w=k)
        Czp = scrF[:, 0:4096].rearrange("p (i g c) -> p i g c", i=2, g=16)
        P.add("dve", lambda e: e.memset(Czp, 0.0), w=["hidA"])
        for i in range(2):
            for gl in range(2):
                for r in range(4):
                    P.add("dve", lambda e, i=i, gl=gl, r=r: e.tensor_scalar(
                        out=Czp[gl * 64:(gl + 1) * 64, i].rearrange("p (cb r) c -> p cb r c", r=4)[:, :, r, 32 * r + gl * 16:32 * r + gl * 16 + 16],
                        in0=self.Cz[gl * 64:(gl + 1) * 64, i].rearrange("p (cb r) c -> p cb r c", r=4)[:, :, r, gl * 16:gl * 16 + 16],
                        scalar1=(1.0 if i == 0 else -1.0), scalar2=None, op0=ALU.mult), r=["ys"], w=["hidA"])
        for cb in range(4):
            n = 0
            for r in range(4):
                for i in range(2):
                    P.add("pe", lambda e, cb=cb, r=r, i=i, n=n: e.matmul(self.ps[1][:, 0:128], lhsT=self.Bz[:, i, 4 * cb + r, :], rhs=Czp[:, i, 4 * cb + r, :],
                                                                       start=(n == 0), stop=(n == 7)), r=["xres", "hidA"], w=["ps1"])
                    n += 1
            P.add("dve", lambda e, cb=cb: e.tensor_copy(out=self.K0T[:, cb, :], in_=self.ps[1][:, 0:128]), r=["ps1"], w=["K0T"])
        lre = p[:, :, 11:12].to_broadcast([128, 16, 32])
        lim = p[:, :, 12:13].to_broadcast([128, 16, 32])
        u0 = self.s5t[0][:, 0:512].rearrange("p (g c) -> p g c", c=32)
        u1 = self.s5t[1][:, 0:512].rearrange("p (g c) -> p g c", c=32)
        P.add("dve", lambda e: e.tensor_tensor(out=u0, in0=self.Cz[:, 0], in1=lre, op=ALU.mult), r=["ys", "s5p"], w=["s5t0"])
        P.add("dve", lambda e: e.tensor_tensor(out=u1, in0=self.Cz[:, 1], in1=lim, op=ALU.mult), r=["ys", "s5p"], w=["s5t1"])
        P.add("dve", lambda e: e.tensor_tensor(out=self.Ct1[:, 0], in0=u0, in1=u1, op=ALU.subtract), r=["s5t0", "s5t1"], w=["Ct1"])
        P.add("dve", lambda e: e.tensor_tensor(out=u0, in0=self.Cz[:, 0], in1=lim, op=ALU.mult), r=["ys", "s5p"], w=["s5t0"])
        P.add("dve", lambda e: e.tensor_tensor(out=u1, in0=self.Cz[:, 1], in1=lre, op=ALU.mult), r=["ys", "s5p"], w=["s5t1"])
        P.add("dve", lambda e: e.tensor_tensor(out=u0, in0=u0, in1=u1, op=ALU.add), r=["s5t0", "s5t1"], w=["s5t0"])
        P.add("dve", lambda e: e.tensor_scalar(out=self.Ct1[:, 1], in0=u0, scalar1=-1.0, scalar2=None, op0=ALU.mult), r=["s5t0"], w=["Ct1"])
        lre16 = p[:, :, 11:12].to_broadcast([128, 16, 16])
        lim16 = p[:, :, 12:13].to_broadcast([128, 16, 16])
        t0 = self.s5t[0][:, 0:256].rearrange("p (g c) -> p g c", c=16)
        t1 = self.s5t[1][:, 0:256].rearrange("p (g c) -> p g c", c=16)
        t2 = self.s5t[2][:, 0:256].rearrange("p (g c) -> p g c", c=16)
        P.add("dve", lambda e: e.memset(self.Bz[:], 0.0), w=["xres"])
        for i in range(2):
            a_, b_ = (bb[0], bb[1]) if i == 0 else (bb[1], bb[0])
            P.add("dve", lambda e, a_=a_: e.tensor_tensor(out=t0, in0=a_, in1=lre16, op=ALU.mult), r=["s5t3", "s5t4", "s5p"], w=["s5t0"])
            P.add("dve", lambda e, b_=b_: e.tensor_tensor(out=t1, in0=b_, in1=lim16, op=ALU.mult), r=["s5t3", "s5t4", "s5p"], w=["s5t1"])
            P.add("dve", lambda e, i=i: e.tensor_tensor(out=t2, in0=t0, in1=t1, op=(ALU.subtract if i == 0 else ALU.add)), r=["s5t0", "s5t1"], w=["s5t2"])
            for gl in range(2):
                for r in range(4):
                    P.add("dve", lambda e, i=i, gl=gl, r=r: e.tensor_copy(
                        out=self.Bz[gl * 64:(gl + 1) * 64, i].rearrange("p (cb r) c -> p cb r c", r=4)[:, :, r, 32 * r + gl * 16:32 * r + gl * 16 + 16],
                        in_=t2[gl * 64:(gl + 1) * 64].rearrange("p (cb r) c -> p cb r c", r=4)[:, :, r, :]), r=["s5t2"], w=["xres"])
        for i in range(2):
            for gp in range(16):
                P.add("pe", lambda e, i=i, gp=gp: e.matmul(self.ps[0][:, 0:128], lhsT=self.Bz[:, i, gp, :], rhs=idf, start=True, stop=True),
                      r=["xres", "cst"], w=["ps0"])
                P.add("dve", lambda e, i=i, gp=gp: e.tensor_copy(out=self.Bt1[:, i, gp, :], in_=self.ps[0][:, 0:128]), r=["ps0"], w=["Bt1"])
        if _S5STOP == 5:
            return
        P.add("sp", lambda e: e.dma_start(out=self.dcol[:], in_=D["s5_d"][0].rearrange("(c p) -> p c", p=128), allow_slow_non_contiguous=True),
              w=["dcol"], dma="init")
        P.add("sp", lambda e: e.dma_start(out=self.bglu[:], in_=D["s5_b_glu"][0].rearrange("(c p) -> p c", p=128), allow_slow_non_contiguous=True),
              w=["bglu"], dma="init")

    def s5_iter(self, sub, cb, st, ntok, nval):
        P = self.P
        TS, NS = 128, 64
        t0 = sub * TS
        T = self.s5t if st == 0 else self.g5t
        tk = [("s5t" if st == 0 else "g5t") + str(i) for i in range(6)]
        bR, bI, bY = (0, 1, 4) if st == 0 else (2, 3, 5)
        kx = f"xrb{st}"
        xv = [self.xrb[:, st, i, 0:4 * (NS + 1)].rearrange("p (g n) -> p g n", n=NS + 1) for i in range(2)]
        cs = self.rot[:, 0, 4 * cb:4 * cb + 4, :].rearrange("p g t -> p (g t)")
        sn = self.rot[:, 1, 4 * cb:4 * cb + 4, :].rearrange("p g t -> p (g t)")
        u2 = self.ub[:, cb, t0:t0 + TS].rearrange("p (n two) -> p n two", two=2)
        ue, uo = u2[:, :, 0], u2[:, :, 1]
        Wd = 4 * NS
        for i in range(2):
            P.add("act", lambda e, i=i: e.activation(out=xv[i][:, :, 0:1], in_=self.xst[:, i, 4 * cb:4 * cb + 4].unsqueeze(2), func=AF.Copy),
                  r=["xst"], w=[kx])
        for i, bk in ((0, bR), (1, bI)):
            for r in range(4):
                gp = 4 * cb + r
                P.add("pe", lambda e, i=i, bk=bk, r=r, gp=gp: e.matmul(self.ps[bk][:, r * NS:(r + 1) * NS], lhsT=self.Bt1[:, i, gp, :], rhs=ue,
                                                                       start=True, stop=False), r=["Bt1", "ub"], w=[f"ps{bk}"])
                P.add("pe", lambda e, i=i, bk=bk, r=r, gp=gp: e.matmul(self.ps[bk][:, r * NS:(r + 1) * NS], lhsT=self.Bt[:, i, gp, :], rhs=uo,
                                                                       start=False, stop=True), r=["Bt", "ub"], w=[f"ps{bk}"])
        yield
        pR, pI = self.ps[bR][:, 0:Wd], self.ps[bI][:, 0:Wd]
        kR, kI = f"ps{bR}", f"ps{bI}"
        A = [t[:, 0:Wd] for t in T]
        P.add("dve", lambda e: e.tensor_tensor(out=A[0], in0=pR, in1=cs, op=ALU.mult), r=[kR, "rot"], w=[tk[0]])
        yield
        P.add("dve", lambda e: e.tensor_tensor(out=A[1], in0=pI, in1=sn, op=ALU.mult), r=[kI, "rot"], w=[tk[1]])
        yield
        P.add("dve", lambda e: e.tensor_tensor(out=A[2], in0=pI, in1=cs, op=ALU.mult), r=[kI, "rot"], w=[tk[2]])
        yield
        P.add("dve", lambda e: e.tensor_tensor(out=A[3], in0=pR, in1=sn, op=ALU.mult), r=[kR, "rot"], w=[tk[3]])
        yield
        P.add("dve", lambda e: e.tensor_tensor(out=A[0], in0=A[0], in1=A[1], op=ALU.add), r=[tk[0], tk[1]], w=[tk[0]])
        yield
        P.add("dve", lambda e: e.tensor_tensor(out=A[2], in0=A[2], in1=A[3], op=ALU.subtract), r=[tk[2], tk[3]], w=[tk[2]])
        yield
        for r in range(4):
            gp = 4 * cb + r
            rb = self.s5p[:, gp, 22:23].to_broadcast([128, NS])
            sl = slice(r * NS, (r + 1) * NS)
            P.add("dve", lambda e, rb=rb, sl=sl, gp=gp: e.tensor_tensor_scan(out=T[4][:, sl], data0=rb, data1=T[0][:, sl], initial=self.xst[:, 0, gp:gp + 1],
                                                                             op0=ALU.mult, op1=ALU.add), r=["s5p", tk[0], "xst"], w=[tk[4]])
            yield
            P.add("dve", lambda e, rb=rb, sl=sl, gp=gp: e.tensor_tensor_scan(out=T[5][:, sl], data0=rb, data1=T[2][:, sl], initial=self.xst[:, 1, gp:gp + 1],
                                                                             op0=ALU.mult, op1=ALU.add), r=["s5p", tk[2], "xst"], w=[tk[5]])
            yield
        P.add("dve", lambda e: e.tensor_tensor(out=A[0], in0=A[4], in1=cs, op=ALU.mult), r=[tk[4], "rot"], w=[tk[0]])
        yield
        P.add("dve", lambda e: e.tensor_tensor(out=A[1], in0=A[5], in1=sn, op=ALU.mult), r=[tk[5], "rot"], w=[tk[1]])
        yield
        P.add("dve", lambda e: e.tensor_tensor(out=A[2], in0=A[4], in1=sn, op=ALU.mult), r=[tk[4], "rot"], w=[tk[2]])
        yield
        P.add("dve", lambda e: e.tensor_tensor(out=A[3], in0=A[5], in1=cs, op=ALU.mult), r=[tk[5], "rot"], w=[tk[3]])
        yield
        P.add("dve", lambda e: e.tensor_tensor(out=A[0], in0=A[0], in1=A[1], op=ALU.subtract), r=[tk[0], tk[1]], w=[tk[0]])
        yield
        P.add("dve", lambda e: e.tensor_tensor(out=A[2], in0=A[2], in1=A[3], op=ALU.add), r=[tk[2], tk[3]], w=[tk[2]])
        yield
        for i, tt in ((0, 0), (1, 2)):
            P.add("act", lambda e, i=i, tt=tt: e.activation(out=xv[i][:, :, 1:NS + 1], in_=A[tt].rearrange("p (g n) -> p g n", n=NS), func=AF.Copy),
                  r=[tk[tt]], w=[kx])
        lastn = NS - 1
        if nval < ntok:
            lastn = (nval - 1 - t0 - 1) // 2
        if 0 <= lastn < NS:
            for i, tt in ((0, 0), (1, 2)):
                P.add("dve", lambda e, i=i, tt=tt, lastn=lastn: e.tensor_copy(
                    out=self.xst[:, i, 4 * cb:4 * cb + 4], in_=A[tt].rearrange("p (g n) -> p g n", n=NS)[:, :, lastn]), r=[tk[tt], kx], w=["xst"])
                yield
        pY = self.ps[bY]
        for r in range(4):
            gp = 4 * cb + r
            for i in range(2):
                P.add("pe", lambda e, r=r, gp=gp, i=i: e.matmul(pY[32 * r:32 * r + 32, 0:NS], lhsT=self.Ct[:, i, gp, :], rhs=xv[i][:, r, 1:NS + 1],
                                                                start=(i == 0), stop=(i == 1), tile_position=(0, 32 * r)), r=["Ct", kx], w=[f"ps{bY}"])
        P.add("pe", lambda e: e.matmul(pY[:, NS:2 * NS], lhsT=self.K0T[:, cb, :], rhs=ue, start=True, stop=False), r=["K0T", "ub"], w=[f"ps{bY}"])
        for r in range(4):
            gp = 4 * cb + r
            for i in range(2):
                P.add("pe", lambda e, r=r, gp=gp, i=i: e.matmul(pY[32 * r:32 * r + 32, NS:2 * NS], lhsT=self.Ct1[:, i, gp, :], rhs=xv[i][:, r, 0:NS],
                                                                start=False, stop=(i == 1), tile_position=(0, 32 * r)), r=["Ct1", kx], w=[f"ps{bY}"])
        yield
        y2 = self.ys[:, cb, t0:t0 + TS].rearrange("p (n two) -> p n two", two=2)
        P.add("dve", lambda e: e.scalar_tensor_tensor(out=y2[:, :, 1], in0=uo, scalar=self.dcol[:, cb:cb + 1], in1=pY[:, 0:NS],
                                                      op0=ALU.mult, op1=ALU.add), r=["ub", "dcol", f"ps{bY}"], w=[f"ys_{cb}"])
        yield
        P.add("dve", lambda e: e.scalar_tensor_tensor(out=y2[:, :, 0], in0=ue, scalar=self.dcol[:, cb:cb + 1], in1=pY[:, NS:2 * NS],
                                                      op0=ALU.mult, op1=ALU.add), r=["ub", "dcol", f"ps{bY}"], w=[f"ys_{cb}"])
        yield

    def s5_tile(self, ntok, nval, state_out=None):
        P, D = self.P, self.D
        nsub = ntok // 128
        for sub in range(nsub):
            for cb0 in (0, 2):
                gens = [self.s5_iter(sub, cb0, 0, ntok, nval), self.s5_iter(sub, cb0 + 1, 1, ntok, nval)]
                alive = [True, True]
                while any(alive):
                    for gi in range(2):
                        if alive[gi]:
                            try:
                                next(gens[gi])
                            except StopIteration:
                                alive[gi] = False
        if state_out is not None:
            for i in range(2):
                dst = state_out[i].rearrange("(gp gl) p -> (gl p) gp", gl=2)
                P.add("sp", lambda e, i=i, dst=dst: e.dma_start(out=dst, in_=self.xst[:, i, :], allow_slow_non_contiguous=True),
                      r=["xst"], w=[f"so{i}"], dma="out")
        ys_ = [self.ys[:, cb, 0:ntok] for cb in range(4)]
        ts_ = [self.s5t[cb][:, 0:ntok] for cb in range(4)]
        for cb in range(4):
            P.add("dve", lambda e, cb=cb: e.tensor_tensor(out=ts_[cb], in0=ys_[cb], in1=ys_[cb], op=ALU.mult), r=[f"ys_{cb}"], w=[f"s5t{cb}"])
        for cb in range(4):
            P.add("dve", lambda e, cb=cb: e.tensor_scalar(out=ts_[cb], in0=ts_[cb], scalar1=0.044715, scalar2=1.0, op0=ALU.mult, op1=ALU.add),
                  r=[f"s5t{cb}"], w=[f"s5t{cb}"])
        for cb in range(4):
            P.add("dve", lambda e, cb=cb: e.tensor_tensor(out=ts_[cb], in0=ts_[cb], in1=ys_[cb], op=ALU.mult), r=[f"s5t{cb}", f"ys_{cb}"], w=[f"s5t{cb}"])
        for cb in range(4):
            P.add("act", lambda e, cb=cb: e.activation(out=ts_[cb], in_=ts_[cb], func=AF.Sigmoid, scale=1.5957691216), r=[f"s5t{cb}"], w=[f"s5t{cb}"])
        for cb in range(4):
            P.add("dve", lambda e, cb=cb: e.tensor_tensor(out=ys_[cb], in0=ts_[cb], in1=ys_[cb], op=ALU.mult), r=[f"s5t{cb}", f"ys_{cb}"], w=[f"ys_{cb}"])
        for cb in range(4):
            P.add("act", lambda e, cb=cb: e.activation(out=self.zb[:, cb, 0:ntok], in_=ys_[cb], func=AF.Copy), r=[f"ys_{cb}"], w=["ub"])

        def evac(ci, m, b):
            P.add("act", lambda e: e.activation(out=self.sg[:, 0:ntok], in_=self.ps[b][:, 0:ntok], func=AF.Sigmoid, bias=self.bglu[:, ci:ci + 1]),
                  r=[f"ps{b}", "bglu"], w=["sg"])
            P.add("dve", lambda e: e.tensor_tensor(out=self.mixT[:, ci, 0:ntok], in0=self.sg[:, 0:ntok], in1=self.ys[:, ci, 0:ntok], op=ALU.mult),
                  r=["sg", f"ys_{ci}"], w=["mixT"])
        self.linear_fm(D["s5_w_glu"][0], 0, 512, lambda k: self.zb[:, k, 0:ntok], ["ub"], ntok, evac, banks=(4, 7))

    def proj_in(self, ntok, nval, conv_out=None):
        P, D = self.P, self.D

        def evac(ci, m, b):
            src = self.ps[b][0:m, 0:ntok]
            if ci < 4:
                P.add("act", lambda e: e.activation(out=self.ub[:, ci, 0:ntok], in_=src, func=AF.Copy), r=[f"ps{b}"], w=["ub"])
            elif ci < 16:
                P.add("act", lambda e: e.activation(out=self.qkvb[:, ci - 4, 3:3 + ntok], in_=src, func=AF.Copy), r=[f"ps{b}"], w=["qkvb"])
                if conv_out is not None:
                    P.add("dve", lambda e: e.tensor_copy(out=self.cv32[:, ci - 4, :], in_=self.ps[b][:, nval - 3:nval]), r=[f"ps{b}"], w=["cv32"])
                    P.add("sp", lambda e: e.dma_start(out=conv_out[:, (ci - 4) * 128:(ci - 3) * 128].rearrange("j p -> p j"), in_=self.cv32[:, ci - 4, :],
                                                      allow_slow_non_contiguous=True), r=["cv32"], w=[f"co{ci}"], dma="out")
            elif ci < 20:
                P.add("act", lambda e: e.activation(out=self.zs[:, ci - 16, 0:ntok], in_=src, func=AF.Silu), r=[f"ps{b}"], w=["zs"])
            else:
                P.add("act", lambda e: e.activation(out=self.ba[:, 0:ntok], in_=src, func=AF.Copy), r=[f"ps{b}"], w=["ba"])
        self.linear_fm(D["w_in"][0], 0, IN_COLS, lambda k: self.hT[:, k, 0:ntok], ["hT"], ntok, evac)

    def final_store(self, nblk, dst_fn):
        P = self.P
        P.add("sp", lambda e: e.dma_start(out=self.nfin, in_=self.D["norm_final"].partition_broadcast(128)), w=["s5t0", "s5t1"], dma="ldn")
        P.add("dve", lambda e: e.memset(self.ss2[:], 0.0), w=["ss2"])
        for b in range(nblk):
            P.add("act", lambda e, b=b: e.activation(out=self.xn[:], in_=self.xres[:, b, :], func=AF.Square,
                                                     accum_out=self.ss2[:, b:b + 1]), r=[f"xres{b}"], w=["xn", "ss2"])
        P.add("act", lambda e: e.activation(out=self.ss2[:, 4:4 + nblk], in_=self.ss2[:, 0:nblk], func=AF.Sqrt, scale=1.0 / DM, bias=EPS), r=["ss2"], w=["ss2"])
        P.add("dve", lambda e: e.reciprocal(out=self.ss2[:, 4:4 + nblk], in_=self.ss2[:, 4:4 + nblk]), r=["ss2"], w=["ss2"])
        for b in range(nblk):
            yo = self.yo2[b % 2]
            ky = f"ys_{2 * (b % 2)}"
            ky2 = f"ys_{2 * (b % 2) + 1}"
            P.add("dve", lambda e, b=b, yo=yo: e.scalar_tensor_tensor(out=yo, in0=self.xres[:, b, :], scalar=self.ss2[:, 4 + b:5 + b],
                                                                      in1=self.nfin, op0=ALU.mult, op1=ALU.mult),
                  r=[f"xres{b}", "ss2", "s5t0", "s5t1"], w=[ky, ky2])
            dst, rows = dst_fn(b)
            P.add("sp", lambda e, dst=dst, rows=rows, yo=yo: e.dma_start(out=dst, in_=yo[0:rows, :]), r=[ky, ky2], w=["ydram"], dma=f"outy{b % 2}")

    def build(self):
        P, D = self.P, self.D
        self.alloc()
        self.alloc_l0()
        self.yo2 = [self.ys[:, 0:2, :].rearrange("p a t -> p (a t)"), self.ys[:, 2:4, :].rearrange("p a t -> p (a t)")]
        self.alloc_gdn()
        self.alloc_swa()
        self.setup()
        if self.stages != -2:
            self.s5_setup()
        self.gdn_setup()
        self.swa_setup()
        seqs = [("p", 0)]
        if self.stages >= 2:
            seqs += [("s", 0), ("s", 1)]
        for kind, si in seqs:
            if kind == "p":
                ntiles, ntok, nval = SEQ // TT, TT, TT
                P.add("dve", lambda e: e.memset(self.xst[:], 0.0), w=["xst"])
                P.add("dve", lambda e: e.memset(self.qkvb[:, :, 0:3], 0.0), w=["qkvb"])
                P.add("dve", lambda e: e.memset(self.S32[:], 0.0), w=["S32"])
                P.add("dve", lambda e: e.memset(self.Sb[:], 0.0), w=["Sb"])
            else:
                ntiles, ntok, nval = 1, 128, DEC_SEQ
                for i, nm in enumerate(("st_s5_re", "st_s5_im")):
                    P.add("sp", lambda e, i=i, nm=nm, si=si: e.dma_start(
                        out=self.xst[:, i, :], in_=D[nm][si].rearrange("(gp gl) p -> (gl p) gp", gl=2), allow_slow_non_contiguous=True),
                        w=["xst"], dma="ldst")
                P.add("sp", lambda e, si=si: e.dma_start(out=self.S32[:], in_=D["st_gdn"][si].rearrange("h k v -> k h v")), w=["S32"], dma="ldst")
                P.add("act", lambda e: e.activation(out=self.Sb[:], in_=self.S32[:], func=AF.Copy), r=["S32"], w=["Sb"])
                if not (_RISK & 8):
                    self.swa_init_sample(si)
                for b in range(12):
                    P.add("sp", lambda e, si=si, b=b: e.dma_start(out=self.cv32[:, b, :], in_=D["st_conv"][si][:, b * 128:(b + 1) * 128].rearrange("j p -> p j"),
                                                                  allow_slow_non_contiguous=True), w=["cv32"], dma="ldst")
                P.add("dve", lambda e: e.tensor_copy(out=self.qkvb[:, :, 0:3], in_=self.cv32[:]), r=["cv32"], w=["qkvb"])
            if self.stages == 0:
                ntiles = 1
            if self.stages < 0:
                ntiles = 0
                continue
            nblk = ntok // 128
            for ti in range(ntiles):
                last = ti == ntiles - 1
                if kind == "p":
                    for b in range(4):
                        P.add("sp", lambda e, ti=ti, b=b: e.dma_start(out=self.xres[:, b, :], in_=D["xp"][ti * TT + b * 128:ti * TT + (b + 1) * 128, :]),
                              w=[f"xres{b}"], dma=f"ldx{b}")
                else:
                    P.add("dve", lambda e: e.memset(self.xres[:, 0, :], 0.0), w=["xres0"])
                    P.add("sp", lambda e, si=si: e.dma_start(out=self.xres[0:DEC_SEQ, 0, :], in_=D["xs"][si]), w=["xres0"], dma="ldx0")
                if not (kind == "p" and ti > 0 and _PREF):
                    self.norm_T(nblk, 0)
                conv_out = None
                if last and not (_RISK & 2):
                    conv_out = D["p_conv"] if kind == "p" else D["s_conv"][si]
                self.proj_in(ntok, nval, conv_out)
                so = None
                if last:
                    so = (D["p_s5_re"], D["p_s5_im"]) if kind == "p" else (D["s_s5_re"][si], D["s_s5_im"][si])
                go = None
                if last:
                    go = D["p_gdn"] if kind == "p" else D["s_gdn"][si]
                self.s5_tile(ntok, nval, so)
                self.gdn_tile(ntok, nval, go)
                gens = []
                wts = []
                alive = []
                while any(alive):
                    for gi in range(0):
                        for _ in range(wts[gi]):
                            if not alive[gi]:
                                break
                            try:
                                next(gens[gi])
                            except StopIteration:
                                alive[gi] = False
                P.add("dve", lambda e, ntok=ntok: e.tensor_copy(out=self.qkvb[:, :, 0:3], in_=self.qkvb[:, :, ntok:ntok + 3]), r=["qkvb"], w=["qkvb"])
                self.linear_tm_res(D["w_out_ab"][0], lambda k, b: self.mixT[:, k, b * 128:(b + 1) * 128], ["mixT"], nblk)
                self.ffn(0, nblk, ntok)
                self.norm_T(nblk, 2)
                if not (_RISK & 8):
                    self.swa_tile(ntok, nval, ti == 0, kind, si, last)
                self.ffn(1, nblk, ntok)
                if kind == "p" and not last and _PREF:
                    mf = self.mixT[:].rearrange("p a t -> p (a t)").bitcast(F32)
                    stg = [(mf[:, 0:DM], ["mixT"]), (mf[:, DM:2 * DM], ["mixT"]),
                           (self.ub[:].rearrange("p a t -> p (a t)").bitcast(F32), ["ub"]),
                           (self.zs[:].rearrange("p a t -> p (a t)").bitcast(F32), ["zs"])]
                    for b in range(4):
                        P.add("sp", lambda e, ti=ti, b=b, stg=stg: e.dma_start(
                            out=stg[b][0], in_=D["xp"][(ti + 1) * TT + b * 128:(ti + 1) * TT + (b + 1) * 128, :]), w=stg[b][1], dma=f"lds{b}")
                    self.norm_T(nblk, 0, src=stg)
                if kind == "p":
                    self.final_store(nblk, lambda b, ti=ti: (D["y_p"][ti * TT + b * 128:ti * TT + (b + 1) * 128, :], 128))
                else:
                    self.final_store(nblk, lambda b, si=si: (D["y_s"][si], DEC_SEQ))
        P.emit()
        self.es.close()
        return self.nc


_CACHE = {}


def _program(stages=99):
    if stages not in _CACHE:
        _CACHE[stages] = Builder(stages).build()
    return _CACHE[stages]


def kernel(x_prompt, x_sample, state_s5_re, state_s5_im, state_gdn, state_gdn_conv, cache_swa_k, cache_swa_v, **w):
    f = lambda a: np.ascontiguousarray(np.asarray(a, dtype=np.float32))
    nc = _program()
    wd = {n: f(w[n]) for n, _ in W_SPECS}
    in_maps = []
    for c in range(NCORES):
        m = dict(wd)
        m["xp"] = f(x_prompt[c])
        m["xs"] = f(x_sample[2 * c:2 * c + 2])
        m["st_s5_re"] = f(state_s5_re[0, 2 * c:2 * c + 2])
        m["st_s5_im"] = f(state_s5_im[0, 2 * c:2 * c + 2])
        m["st_gdn"] = f(state_gdn[0, 2 * c:2 * c + 2])
        m["st_conv"] = f(state_gdn_conv[0, 2 * c:2 * c + 2])
        m["st_k"] = f(np.asarray(cache_swa_k)[0, 2 * c:2 * c + 2].reshape(2, 128, 256))
        m["st_v"] = f(np.asarray(cache_swa_v)[0, 2 * c:2 * c + 2].reshape(2, 128, 256))
        m["consts"] = _CARR
        in_maps.append(m)
    res = run_bass_kernel_spmd(nc, in_maps, core_ids=list(range(NCORES)))
    R = res.results
    cat = lambda n: np.concatenate([np.asarray(r[n], dtype=np.float32) for r in R], axis=0)
    stk = lambda n: np.stack([np.asarray(r[n], dtype=np.float32) for r in R], axis=0)
    y_p = stk("y_p")
    y_s = cat("y_s")
    outs = [y_p, y_s,
            stk("p_s5_re")[None], stk("p_s5_im")[None], stk("p_gdn")[None], stk("p_conv")[None],
            stk("p_k").reshape(1, 8, 128, 4, 64), stk("p_v").reshape(1, 8, 128, 4, 64),
            cat("s_s5_re")[None], cat("s_s5_im")[None], cat("s_gdn")[None], cat("s_conv")[None],
            cat("s_k").reshape(1, 16, 128, 4, 64), cat("s_v").reshape(1, 16, 128, 4, 64)]
    return tuple(outs)


def _carve(hid, f0, n, dt):
    v = hid[:, f0:f0 + n, :].rearrange("p a b -> p (a b)")
    if dt == F32:
        v = v.bitcast(F32)
    return v, [f"hid{f}" for f in range(f0, f0 + n)]


def alloc_gdn(self):
    sb = self.sb
    self.cw = sb("cw", [128, 4, 12], F32)
    self.gp8 = sb("gp8", [8, 8], F32)
    self.gnw = sb("gnw", [128, 1], F32)
    self.S32 = sb("S32", [128, 4, 128], F32)
    self.Sb = sb("Sb", [128, 4, 128], BF16)
    self.g8t = [sb(f"g8t{i}", [8, TT], F32) for i in range(3)]
    self.bg = sb("bg", [8, TT], F32)
    self.tok = sb("tok", [128, 2, 8, 4], F32)
    self.egl = sb("egl", [128, 4, 8], F32)
    self.vnew = sb("vnew", [128, 2, 2, 128], BF16)
    self.rs8 = sb("rs8", [128, 2, 16], F32)
    self.eye64b = sb("eye64b", [128, 64], BF16)


def gdn_setup(self):
    P, D = self.P, self.D
    for j in range(4):
        P.add("sp", lambda e, j=j: e.dma_start(out=self.cw[:, j, :], in_=D["gdn_conv_w"][0][j].rearrange("(b p) -> p b", p=128),
                                               allow_slow_non_contiguous=True), w=["cw"], dma="init")
    P.add("dve", lambda e: e.memset(self.gp8[:], 0.0), w=["gp8"])
    P.add("sp", lambda e: e.dma_start(out=self.gp8[4:8, 0:1], in_=D["gdn_dt_bias"][0].rearrange("(h o) -> h o", o=1)), w=["gp8"], dma="init")
    P.add("sp", lambda e: e.dma_start(out=self.gp8[4:8, 1:2], in_=D["gdn_a_log"][0].rearrange("(h o) -> h o", o=1)), w=["gp8"], dma="init")
    P.add("sp", lambda e: e.dma_start(out=self.gnw[:], in_=D["gdn_norm_w"][0].rearrange("(p o) -> p o", o=1)), w=["gnw"], dma="init")
    gm = self.cview("gm", 8)
    P.add("act", lambda e: e.activation(out=self.gp8[:, 3:4], in_=self.gp8[:, 1:2], func=AF.Exp), r=["gp8"], w=["gp8"])
    P.add("dve", lambda e: e.tensor_tensor(out=self.gp8[:, 2:3], in0=self.gp8[:, 3:4], in1=gm[:, 2:3], op=ALU.mult), r=["gp8", "cst"], w=["gp8"])
    P.add("dve", lambda e: e.tensor_copy(out=self.eye64b[:], in_=self.cview("eye64")), r=["cst"], w=["eye64b"])


def gdn_pair(self, pr, R, ntok, nval):
    P, D = self.P, self.D
    nch = ntok // 64
    W = ntok
    heads = (2 * pr, 2 * pr + 1)
    T, tk = R["T"], R["tk"]
    ps = [self.ps[b] for b in R["banks"]]
    pk = [f"ps{b}" for b in R["banks"]]
    qkvc, k_qkvc, qd, k_qd, P2, k_P2, Q2, k_Q2 = R["qkvc"], R["k_qkvc"], R["qd"], R["k_qd"], R["P2"], R["k_P2"], R["Q2"], R["k_Q2"]
    attnT, k_at, vb, k_vb, kbg, k_kbg, kdec, k_kdec = R["attnT"], R["k_at"], R["vb"], R["k_vb"], R["kbg"], R["k_kbg"], R["kdec"], R["k_kdec"]
    Ttb, k_Ttb, wT, k_wT, on, k_on = R["Ttb"], R["k_Ttb"], R["wT"], R["k_wT"], R["on"], R["k_on"]
    vnew = self.vnew[:, pr]
    ktok, krs, kegl, kS32, kSb = f"tok{pr}", f"rs8{pr}", f"egl{pr}", f"S32p{pr}", f"Sbp{pr}"
    gm = self.cview("gm", 8)
    sel = self.cview("sel", 8).rearrange("k (r m) -> k r m", m=128)
    selp = self.cview("selp", 8).rearrange("k (h t) -> k h t", t=2)
    eye = self.cview("eye64")
    mUs = self.cview("mUs")
    mUi = self.cview("mUi")
    ones = self.cview("ones")
    idb = self.identb

    def v3(ap, inner=64):
        return ap[:, 0:nch * inner].rearrange("p (c i) -> p c i", i=inner)

    def blk_of(idx):
        return (idx // 2) * 4 + heads[idx % 2]

    def conv_pair(p):
        for j in range(4):
            for u_ in range(2):
                idx = 2 * p + u_
                blk = blk_of(idx)
                acc = T[3 * u_][:, 0:W]
                if j == 0:
                    P.add("dve", lambda e, blk=blk, acc=acc: e.tensor_scalar(out=acc, in0=self.qkvb[:, blk, c0:c0 + W], scalar1=self.cw[:, 0, blk:blk + 1],
                                                                              scalar2=None, op0=ALU.mult), r=["qkvb", "cw"], w=[tk[3 * u_]])
                else:
                    P.add("dve", lambda e, blk=blk, acc=acc, j=j: e.scalar_tensor_tensor(
                        out=acc, in0=self.qkvb[:, blk, c0 + j:c0 + j + W], scalar=self.cw[:, j, blk:blk + 1], in1=acc, op0=ALU.mult, op1=ALU.add),
                        r=["qkvb", "cw", tk[3 * u_]], w=[tk[3 * u_]])

    def mid_pair(p):
        for u_ in range(2):
            idx = 2 * p + u_
            acc, cq, sq = T[3 * u_][:, 0:W], T[3 * u_ + 1][:, 0:W], T[3 * u_ + 2][:, 0:W]
            if p == 2:
                P.add("act", lambda e, idx=idx, acc=acc: e.activation(out=qkvc[:, idx, 0:W], in_=acc, func=AF.Silu), r=[tk[3 * u_]], w=k_qkvc)
                continue
            P.add("act", lambda e, acc=acc, cq=cq: e.activation(out=cq, in_=acc, func=AF.Silu), r=[tk[3 * u_]], w=[tk[3 * u_ + 1]])
            P.add("act", lambda e, cq=cq, sq=sq: e.activation(out=sq, in_=cq, func=AF.Square), r=[tk[3 * u_ + 1]], w=[tk[3 * u_ + 2]])
            P.add("pe", lambda e, sq=sq, u_=u_: e.matmul(ps[u_][:, 0:W], lhsT=ones, rhs=sq, start=True, stop=True), r=["cst", tk[3 * u_ + 2]], w=[pk[u_]])
            sc = 128.0 if p == 0 else 1.0
            P.add("act", lambda e, sq=sq, u_=u_, sc=sc: e.activation(out=sq, in_=ps[u_][:, 0:W], func=AF.Sqrt, scale=sc, bias=sc * EPS),
                  r=[pk[u_]], w=[tk[3 * u_ + 2]])

    def fin_pair(p):
        for u_ in range(2):
            idx = 2 * p + u_
            cq, sq = T[3 * u_ + 1][:, 0:W], T[3 * u_ + 2][:, 0:W]
            P.add("dve", lambda e, sq=sq: e.reciprocal(out=sq, in_=sq), r=[tk[3 * u_ + 2]], w=[tk[3 * u_ + 2]])
        for u_ in range(2):
            idx = 2 * p + u_
            cq, sq = T[3 * u_ + 1][:, 0:W], T[3 * u_ + 2][:, 0:W]
            P.add("dve", lambda e, idx=idx, cq=cq, sq=sq: e.tensor_tensor(out=qkvc[:, idx, 0:W], in0=cq, in1=sq, op=ALU.mult),
                  r=[tk[3 * u_ + 1], tk[3 * u_ + 2]], w=k_qkvc)

    c0 = 0
    conv_pair(0)
    mid_pair(0)
    conv_pair(1)
    fin_pair(0)
    mid_pair(1)
    yield
    conv_pair(2)
    fin_pair(1)
    mid_pair(2)
    yield
    qT = [qkvc[:, 0, :], qkvc[:, 1, :]]
    kT = [qkvc[:, 2, :], qkvc[:, 3, :]]
    vT = [qkvc[:, 4, :], qkvc[:, 5, :]]
    for hh in range(2):
        h = heads[hh]
        rows = slice(64 * hh, 64 * hh + 64)
        P.add("pe", lambda e, h=h, hh=hh, rows=rows: e.matmul(ps[0][rows, 0:W], lhsT=sel[:, 4 + h, 0:64], rhs=self.bg[:, 0:W], start=True, stop=True,
                                                              tile_position=(0, 64 * hh)), r=["cst", "bg"], w=[pk[0]])
        P.add("pe", lambda e, h=h, hh=hh, rows=rows: e.matmul(ps[1][rows, 0:W], lhsT=sel[:, h, 0:64], rhs=self.bg[:, 0:W], start=True, stop=True,
                                                              tile_position=(0, 64 * hh)), r=["cst", "bg"], w=[pk[1]])
        for c in range(nch):
            P.add("pe", lambda e, h=h, hh=hh, rows=rows, c=c: e.matmul(ps[2][rows, 2 * c:2 * c + 2], lhsT=self.bg[:, c * 64:(c + 1) * 64],
                                                                       rhs=selp[:, h, :], start=True, stop=True, tile_position=(0, 64 * hh)),
                  r=["cst", "bg"], w=[pk[2]])
    tok = self.tok[:, pr]
    P.add("dve", lambda e: e.tensor_copy(out=tok[:, 0:nch, 0:2], in_=ps[2][:, 0:2 * nch].rearrange("p (c t) -> p c t", t=2)), r=[pk[2]], w=[ktok])
    E = T[0]
    P.add("dve", lambda e: e.tensor_tensor(out=v3(E), in0=v3(ps[0]), in1=tok[:, 0:nch, 1:2].to_broadcast([128, nch, 64]), op=ALU.subtract),
          r=[pk[0], ktok], w=[tk[0]])
    P.add("dve", lambda e: e.tensor_scalar(out=E[:, 0:W], in0=E[:, 0:W], scalar1=0.0, scalar2=None, op0=ALU.min), r=[tk[0]], w=[tk[0]])
    P.add("act", lambda e: e.activation(out=E[:, 0:W], in_=E[:, 0:W], func=AF.Exp), r=[tk[0]], w=[tk[0]])
    P.add("dve", lambda e: e.tensor_tensor(out=tok[:, 0:nch, 3:4], in0=v3(ps[0])[:, :, 63:64], in1=tok[:, 0:nch, 1:2], op=ALU.subtract),
          r=[pk[0], ktok], w=[ktok])
    P.add("act", lambda e: e.activation(out=tok[:, 0:nch, 3:4], in_=tok[:, 0:nch, 3:4], func=AF.Exp), r=[ktok], w=[ktok])
    P.add("act", lambda e: e.activation(out=tok[:, 0:nch, 2:3], in_=tok[:, 0:nch, 1:2], func=AF.Exp), r=[ktok], w=[ktok])
    P.add("dve", lambda e: e.tensor_tensor(out=tok[:, 0:nch, 2:3], in0=tok[:, 0:nch, 2:3], in1=tok[:, 0:nch, 0:1], op=ALU.mult), r=[ktok], w=[ktok])
    yield
    Bm, Am, Tt = T[1], T[2], T[5]
    mUs_b = mUs.unsqueeze(1).to_broadcast([128, nch, 64])
    mUi_b = mUi.unsqueeze(1).to_broadcast([128, nch, 64])
    eye_b = eye.unsqueeze(1).to_broadcast([128, nch, 64])
    for hh in range(2):
        rows = slice(64 * hh, 64 * hh + 64)
        for c in range(nch):
            cs_ = slice(c * 64, (c + 1) * 64)
            P.add("pe", lambda e, hh=hh, rows=rows, cs_=cs_: e.matmul(ps[2][rows, cs_], lhsT=kT[hh][:, cs_], rhs=kT[hh][:, cs_], start=True, stop=True,
                                                                      tile_position=(0, 64 * hh)), r=k_qkvc, w=[pk[2]])
    P.add("dve", lambda e: e.tensor_tensor(out=Bm[:, 0:W], in0=ps[2][:, 0:W], in1=E[:, 0:W], op=ALU.mult), r=[pk[2], tk[0]], w=[tk[1]])
    for hh in range(2):
        rows = slice(64 * hh, 64 * hh + 64)
        for c in range(nch):
            cs_ = slice(c * 64, (c + 1) * 64)
            P.add("pe", lambda e, hh=hh, rows=rows, cs_=cs_: e.matmul(ps[2][rows, cs_], lhsT=kT[hh][:, cs_], rhs=qT[hh][:, cs_], start=True, stop=True,
                                                                      tile_position=(0, 64 * hh)), r=k_qkvc, w=[pk[2]])
    P.add("dve", lambda e: e.tensor_tensor(out=v3(Bm), in0=v3(Bm), in1=mUs_b, op=ALU.mult), r=[tk[1], "cst"], w=[tk[1]])
    P.add("dve", lambda e: e.tensor_tensor(out=Bm[:, 0:W], in0=Bm[:, 0:W], in1=ps[1][:, 0:W], op=ALU.mult), r=[tk[1], pk[1]], w=[tk[1]])
    P.add("dve", lambda e: e.tensor_tensor(out=Am[:, 0:W], in0=ps[2][:, 0:W], in1=E[:, 0:W], op=ALU.mult), r=[pk[2], tk[0]], w=[tk[2]])
    P.add("dve", lambda e: e.tensor_tensor(out=v3(attnT), in0=v3(Am), in1=mUi_b, op=ALU.mult), r=[tk[2], "cst"], w=k_at)
    yield
    t0b = T[0].bitcast(BF16)
    Bm_b, Am_b = t0b[:, 0:512], t0b[:, 512:1024]
    P.add("act", lambda e: e.activation(out=Bm_b[:, 0:W], in_=Bm[:, 0:W], func=AF.Copy), r=[tk[1], *k_at], w=[tk[0]])
    for hh in range(2):
        rows = slice(64 * hh, 64 * hh + 64)
        for c in range(nch):
            cs_ = slice(c * 64, (c + 1) * 64)
            P.add("pe", lambda e, hh=hh, rows=rows, cs_=cs_: e.matmul(ps[2][rows, cs_], lhsT=Bm_b[rows, cs_], rhs=self.eye64b[rows, :], start=True, stop=True,
                                                                      tile_position=(64 * hh, 64 * hh)), r=[tk[0], "eye64b"], w=[pk[2]])
    P.add("dve", lambda e: e.tensor_copy(out=Am_b[:, 0:W], in_=ps[2][:, 0:W]), r=[pk[2]], w=[tk[0]])
    P.add("dve", lambda e: e.tensor_tensor(out=v3(Tt), in0=eye_b, in1=v3(Bm), op=ALU.subtract), r=[tk[1], "cst"], w=[tk[5]])
    P.add("act", lambda e: e.activation(out=Ttb[:, 0:W], in_=Tt[:, 0:W], func=AF.Copy), r=[tk[5]], w=k_Ttb)
    yield
    Pm, Qm, kP, kQ = Bm_b, Am_b, [tk[0]], [tk[0]]
    sets = [(T[3].bitcast(BF16)[:, 0:512], T[4].bitcast(BF16)[:, 0:512], tk[3:4], tk[4:5]),
            (P2.bitcast(BF16)[:, 0:512], Q2.bitcast(BF16)[:, 0:512], k_P2, k_Q2)]
    for lvl in range(5):
        Pn, Qn, kPn, kQn = sets[lvl % 2]
        for hh in range(2):
            rows = slice(64 * hh, 64 * hh + 64)
            for c in range(nch):
                cs_ = slice(c * 64, (c + 1) * 64)
                tp = (64 * hh, 64 * hh)
                P.add("pe", lambda e, rows=rows, cs_=cs_, tp=tp, Pm=Pm, Qm=Qm: e.matmul(ps[1][rows, cs_], lhsT=Pm[rows, cs_], rhs=Qm[rows, cs_],
                                                                                        start=True, stop=True, tile_position=tp),
                      r=[*kP, *kQ], w=[pk[1]])
                if lvl < 4:
                    P.add("pe", lambda e, rows=rows, cs_=cs_, tp=tp, Pm=Pm, Qm=Qm: e.matmul(ps[0][rows, cs_], lhsT=Qm[rows, cs_], rhs=Pm[rows, cs_],
                                                                                            start=True, stop=True, tile_position=tp),
                          r=[*kP, *kQ], w=[pk[0]])
        yield
        P.add("dve", lambda e, Qn=Qn: e.tensor_copy(out=Qn[:, 0:W], in_=ps[1][:, 0:W]), r=[pk[1]], w=kQn)
        if lvl < 4:
            P.add("act", lambda e, Pn=Pn: e.activation(out=Pn[:, 0:W], in_=ps[0][:, 0:W], func=AF.Copy), r=[pk[0]], w=kPn)
        yield
        for hh in range(2):
            rows = slice(64 * hh, 64 * hh + 64)
            for c in range(nch):
                cs_ = slice(c * 64, (c + 1) * 64)
                P.add("pe", lambda e, rows=rows, cs_=cs_, hh=hh, Qn=Qn: e.matmul(ps[2][rows, cs_], lhsT=Qn[rows, cs_], rhs=Ttb[rows, cs_],
                                                                                 start=True, stop=True, tile_position=(64 * hh, 64 * hh)),
                      r=[*kQn, *k_Ttb], w=[pk[2]])
        yield
        P.add("dve", lambda e: e.tensor_tensor(out=Tt[:, 0:W], in0=Tt[:, 0:W], in1=ps[2][:, 0:W], op=ALU.add), r=[tk[5], pk[2]], w=[tk[5]])
        P.add("act", lambda e: e.activation(out=Ttb[:, 0:W], in_=Tt[:, 0:W], func=AF.Copy), r=[tk[5]], w=k_Ttb)
        Pm, Qm, kP, kQ = Pn, Qn, kPn, kQn
        yield
    pT3 = self.pT[:, 0:nch * 128].rearrange("p (c d) -> p c d", d=128)
    for src, dsts in ((vT, "v"), (kT, "k")):
        for hh in range(2):
            rows = slice(64 * hh, 64 * hh + 64)
            for c in range(nch):
                P.add("pe", lambda e, src=src, hh=hh, rows=rows, c=c: e.transpose(out=self.pT[rows, c * 128:(c + 1) * 128], in_=src[hh][:, c * 64:(c + 1) * 64],
                                                                                   identity=idb[:], tile_position=(0, 64 * hh)),
                      r=[*k_qkvc, "identb"], w=["pT"])
        if dsts == "v":
            P.add("dve", lambda e: e.tensor_tensor(out=v3(vb, 128), in0=pT3, in1=tok[:, 0:nch, 0:1].to_broadcast([128, nch, 128]), op=ALU.mult),
                  r=["pT", ktok], w=k_vb)
        else:
            P.add("dve", lambda e: e.tensor_tensor(out=v3(kbg, 128), in0=pT3, in1=tok[:, 0:nch, 2:3].to_broadcast([128, nch, 128]), op=ALU.mult),
                  r=["pT", ktok], w=k_kbg)
            P.add("dve", lambda e: e.tensor_tensor(out=v3(kdec, 128), in0=pT3, in1=tok[:, 0:nch, 3:4].to_broadcast([128, nch, 128]), op=ALU.mult),
                  r=["pT", ktok], w=k_kdec)
    yield
    u = [T[1], T[2]]
    for hh in range(2):
        rows = slice(64 * hh, 64 * hh + 64)
        for c in range(nch):
            ub_, uc = (0, c) if c < 4 else (1, c - 4)
            P.add("pe", lambda e, hh=hh, rows=rows, c=c, ub_=ub_, uc=uc: e.matmul(
                ps[ub_][rows, uc * 128:(uc + 1) * 128], lhsT=Ttb[rows, c * 64:(c + 1) * 64], rhs=vb[rows, c * 128:(c + 1) * 128],
                start=True, stop=True, tile_position=(64 * hh, 64 * hh)), r=[*k_Ttb, *k_vb], w=[pk[ub_]])
    P.add("act", lambda e: e.activation(out=u[0][:, 0:min(4, nch) * 128], in_=ps[0][:, 0:min(4, nch) * 128], func=AF.Copy), r=[pk[0]], w=[tk[1]])
    if nch > 4:
        P.add("act", lambda e: e.activation(out=u[1][:, :], in_=ps[1][:, :], func=AF.Copy), r=[pk[1]], w=[tk[2]])
    wb = (2, 0)
    for hh in range(2):
        rows = slice(64 * hh, 64 * hh + 64)
        for c in range(nch):
            P.add("pe", lambda e, hh=hh, rows=rows, c=c: e.matmul(
                ps[wb[hh]][:, c * 64:(c + 1) * 64], lhsT=kbg[rows, c * 128:(c + 1) * 128], rhs=Ttb[rows, c * 64:(c + 1) * 64],
                start=True, stop=True, tile_position=(64 * hh, 0)), r=[*k_Ttb, *k_kbg], w=[pk[wb[hh]]])
    for hh in range(2):
        P.add("act", lambda e, hh=hh: e.activation(out=wT[:, hh, 0:W], in_=ps[wb[hh]][:, 0:W], func=AF.Copy), r=[pk[wb[hh]]], w=k_wT)
    yield
    eg = T[0]
    for hh in range(2):
        h = heads[hh]
        P.add("pe", lambda e, h=h: e.matmul(ps[1][:, 0:W], lhsT=sel[:, 4 + h, :], rhs=self.bg[:, 0:W], start=True, stop=True), r=["cst", "bg"], w=[pk[1]])
        P.add("act", lambda e: e.activation(out=eg[:, 0:W], in_=ps[1][:, 0:W], func=AF.Exp), r=[pk[1]], w=[tk[0]])
        P.add("dve", lambda e, hh=hh: e.tensor_tensor(out=qd[:, hh, 0:W], in0=qT[hh][:, 0:W], in1=eg[:, 0:W], op=ALU.mult), r=[*k_qkvc, tk[0]], w=k_qd)
        P.add("dve", lambda e, h=h: e.tensor_copy(out=self.egl[:, h, 0:nch], in_=v3(eg)[:, :, 63]), r=[tk[0]], w=[kegl])
    yield
    o_tm = [T[3], T[4]]
    for c in range(nch):
        slot = c % 2
        for hh in range(2):
            h = heads[hh]
            rows = slice(64 * hh, 64 * hh + 64)
            P.add("pe", lambda e, hh=hh, h=h, rows=rows, c=c: e.matmul(ps[0][rows, 0:128], lhsT=wT[:, hh, c * 64:(c + 1) * 64], rhs=self.Sb[:, h, :],
                                                                       start=True, stop=True, tile_position=(0, 64 * hh)), r=[*k_wT, kSb], w=[pk[0]])
        ub_, uc = (0, c) if c < 4 else (1, c - 4)
        P.add("dve", lambda e, slot=slot, ub_=ub_, uc=uc: e.tensor_tensor(out=vnew[:, slot, :], in0=u[ub_][:, uc * 128:(uc + 1) * 128],
                                                                           in1=ps[0][:, 0:128], op=ALU.subtract),
              r=[tk[1 + ub_], pk[0]], w=[f"vnew{pr}_{slot}"])
        for hh in range(2):
            h = heads[hh]
            rows = slice(64 * hh, 64 * hh + 64)
            P.add("pe", lambda e, hh=hh, h=h, rows=rows, c=c: e.matmul(ps[0][rows, 128:256], lhsT=qd[:, hh, c * 64:(c + 1) * 64], rhs=self.Sb[:, h, :],
                                                                       start=True, stop=False, tile_position=(0, 64 * hh)), r=[*k_qd, kSb], w=[pk[0]])
            P.add("pe", lambda e, hh=hh, rows=rows, c=c, slot=slot: e.matmul(ps[0][rows, 128:256], lhsT=attnT[rows, c * 64:(c + 1) * 64],
                                                                             rhs=vnew[rows, slot, :], start=False, stop=True,
                                                                             tile_position=(64 * hh, 64 * hh)), r=[*k_at, f"vnew{pr}_{slot}"], w=[pk[0]])
            P.add("pe", lambda e, hh=hh, rows=rows, c=c, slot=slot: e.matmul(ps[1 + hh][:, 0:128], lhsT=kdec[rows, c * 128:(c + 1) * 128],
                                                                             rhs=vnew[rows, slot, :], start=True, stop=True,
                                                                             tile_position=(64 * hh, 0)), r=[*k_kdec, f"vnew{pr}_{slot}"], w=[pk[1 + hh]])
        P.add("act", lambda e, ub_=ub_, uc=uc: e.activation(out=o_tm[ub_][:, uc * 128:(uc + 1) * 128], in_=ps[0][:, 128:256], func=AF.Copy),
              r=[pk[0]], w=[tk[3 + ub_]])
        for hh in range(2):
            h = heads[hh]
            P.add("dve", lambda e, hh=hh, h=h, c=c: e.scalar_tensor_tensor(out=self.S32[:, h, :], in0=self.S32[:, h, :], scalar=self.egl[:, h, c:c + 1],
                                                                           in1=ps[1 + hh][:, 0:128], op0=ALU.mult, op1=ALU.add),
                  r=[kS32, kegl, pk[1 + hh]], w=[kS32])
        P.add("act", lambda e, pr=pr: e.activation(out=self.Sb[:, 2 * pr:2 * pr + 2, :], in_=self.S32[:, 2 * pr:2 * pr + 2, :], func=AF.Copy), r=[kS32], w=[kSb])
        yield
    rs = self.rs8[:, pr]
    for ub_ in range(2 if nch > 4 else 1):
        ncc = min(4, nch)
        o3 = o_tm[ub_][:, 0:ncc * 128].rearrange("p (c d) -> p c d", d=128)
        sq3 = T[0][:, 0:ncc * 128].rearrange("p (c d) -> p c d", d=128)
        P.add("dve", lambda e, o3=o3, sq3=sq3: e.tensor_tensor(out=sq3, in0=o3, in1=o3, op=ALU.mult), r=[tk[3 + ub_]], w=[tk[0]])
        P.add("dve", lambda e, sq3=sq3, ub_=ub_, ncc=ncc: e.reduce_sum(out=rs[:, 4 * ub_:4 * ub_ + ncc], in_=sq3, axis=AX.X), r=[tk[0]], w=[krs])
        P.add("act", lambda e, ub_=ub_, ncc=ncc: e.activation(out=rs[:, 8 + 4 * ub_:8 + 4 * ub_ + ncc], in_=rs[:, 4 * ub_:4 * ub_ + ncc], func=AF.Sqrt,
                                                              scale=1.0 / 128, bias=EPS), r=[krs], w=[krs])
        P.add("dve", lambda e, ub_=ub_, ncc=ncc: e.reciprocal(out=rs[:, 8 + 4 * ub_:8 + 4 * ub_ + ncc], in_=rs[:, 8 + 4 * ub_:8 + 4 * ub_ + ncc]), r=[krs], w=[krs])
        P.add("dve", lambda e, o3=o3, ub_=ub_, ncc=ncc: e.tensor_tensor(
            out=on[:, ub_ * 512:ub_ * 512 + ncc * 128].rearrange("p (c d) -> p c d", d=128), in0=o3,
            in1=rs[:, 8 + 4 * ub_:8 + 4 * ub_ + ncc].unsqueeze(2).to_broadcast([128, ncc, 128]), op=ALU.mult), r=[tk[3 + ub_], krs], w=k_on)
    yield
    for hh in range(2):
        rows = slice(64 * hh, 64 * hh + 64)
        for c in range(nch):
            P.add("pe", lambda e, hh=hh, rows=rows, c=c: e.matmul(ps[hh][:, c * 64:(c + 1) * 64], lhsT=on[rows, c * 128:(c + 1) * 128], rhs=self.eye64b[rows, :],
                                                                  start=True, stop=True, tile_position=(64 * hh, 0)), r=[*k_on, "eye64b"], w=[pk[hh]])
        h = heads[hh]
        P.add("dve", lambda e, hh=hh, h=h: e.scalar_tensor_tensor(out=self.mixT[:, 4 + h, 0:W], in0=ps[hh][:, 0:W], scalar=self.gnw[:, 0:1],
                                                                  in1=self.zs[:, h, 0:W], op0=ALU.mult, op1=ALU.mult), r=[pk[hh], "gnw", "zs"], w=["mixT"])


def gdn_tile(self, ntok, nval, state_out=None):
    P, D = self.P, self.D
    nch = ntok // 64
    T = self.g5t
    tk = [f"g5t{i}" for i in range(6)]
    gm = self.cview("gm", 8)
    sel = self.cview("sel", 8).rearrange("k (r m) -> k r m", m=128)
    selp = self.cview("selp", 8).rearrange("k (h t) -> k h t", t=2)
    eye = self.cview("eye64")
    mUs = self.cview("mUs")
    mUi = self.cview("mUi")
    ones = self.cview("ones")
    idb = self.identb
    W = ntok

    def v3(ap, inner=64):
        return ap[:, 0:nch * inner].rearrange("p (c i) -> p c i", i=inner)

    g0, g1, g2 = self.g8t
    ba = self.ba
    P.add("act", lambda e: e.activation(out=g0[:, 0:W], in_=ba[:, 0:W], func=AF.Exp, bias=self.gp8[:, 0:1]), r=["ba", "gp8"], w=["g8t0"])
    P.add("act", lambda e: e.activation(out=g0[:, 0:W], in_=g0[:, 0:W], func=AF.Ln, bias=1.0), r=["g8t0"], w=["g8t0"])
    P.add("dve", lambda e: e.tensor_scalar(out=g1[:, 0:W], in0=g0[:, 0:W], scalar1=self.gp8[:, 2:3], scalar2=None, op0=ALU.mult), r=["g8t0", "gp8"], w=["g8t1"])
    P.add("act", lambda e: e.activation(out=g0[:, 0:W], in_=ba[:, 0:W], func=AF.Sigmoid), r=["ba", "g8t1"], w=["g8t0"])
    P.add("dve", lambda e: e.scalar_tensor_tensor(out=g2[:, 0:W], in0=g0[:, 0:W], scalar=gm[:, 0:1], in1=g1[:, 0:W], op0=ALU.mult, op1=ALU.add),
          r=["g8t0", "g8t1", "cst"], w=["g8t2"])
    if nval < ntok:
        P.add("dve", lambda e: e.memset(g2[:, nval:W], 0.0), w=["g8t2"])
    P.add("dve", lambda e: e.tensor_tensor_scan(out=g0[:, 0:W], data0=self.cview("cmask", 8)[:, 0:W], data1=g2[:, 0:W], initial=0.0,
                                                op0=ALU.mult, op1=ALU.add), r=["g8t2", "cst"], w=["g8t0"])
    P.add("dve", lambda e: e.tensor_scalar(out=g1[:, 0:W], in0=g0[:, 0:W], scalar1=gm[:, 1:2], scalar2=None, op0=ALU.mult), r=["g8t0", "cst"], w=["g8t1"])
    P.add("dve", lambda e: e.scalar_tensor_tensor(out=self.bg[:, 0:W], in0=g2[:, 0:W], scalar=gm[:, 0:1], in1=g1[:, 0:W], op0=ALU.mult, op1=ALU.add),
          r=["g8t2", "g8t1", "cst"], w=["bg"])


    hid = self.hid
    R0 = {"T": self.g5t, "tk": [f"g5t{i}" for i in range(6)], "banks": (0, 1, 2)}
    q_, R0["k_qkvc"] = _carve(hid, 0, 6, BF16)
    R0["qkvc"] = q_.rearrange("p (a t) -> p a t", t=TT)
    q_, R0["k_qd"] = _carve(hid, 6, 2, BF16)
    R0["qd"] = q_.rearrange("p (a t) -> p a t", t=TT)
    R0["P2"], R0["k_P2"] = _carve(hid, 8, 2, F32)
    R0["Q2"], R0["k_Q2"] = _carve(hid, 10, 2, F32)
    R0["attnT"], R0["k_at"] = _carve(hid, 12, 1, BF16)
    R0["vb"], R0["k_vb"] = _carve(hid, 13, 2, BF16)
    R0["kbg"], R0["k_kbg"] = _carve(hid, 17, 2, BF16)
    R0["kdec"], R0["k_kdec"] = _carve(hid, 19, 2, BF16)
    R0["Ttb"], R0["k_Ttb"] = _carve(hid, 21, 1, BF16)
    q_, R0["k_wT"] = _carve(hid, 8, 2, BF16)
    R0["wT"] = q_.rearrange("p (a t) -> p a t", t=TT)
    R0["on"], R0["k_on"] = _carve(hid, 10, 2, BF16)
    R1 = {"T": self.s5t, "tk": [f"s5t{i}" for i in range(6)], "banks": (4, 5, 6)}
    R1["qkvc"], R1["k_qkvc"] = self.hT[:, 0:6, :], ["hT_q"]
    R1["qd"], R1["k_qd"] = self.hT[:, 6:8, :], ["hT_d"]
    R1["P2"], R1["k_P2"] = self.ys[:, 0, :], ["ys_0"]
    R1["Q2"], R1["k_Q2"] = self.ys[:, 1, :], ["ys_1"]
    R1["wT"], R1["k_wT"] = self.ys[:, 2, :].bitcast(BF16).rearrange("p (a t) -> p a t", t=TT), ["ys_2"]
    R1["on"], R1["k_on"] = self.ys[:, 3, :].bitcast(BF16), ["ys_3"]
    R1["attnT"], R1["k_at"] = self.ub[:, 0, :], ["ub_0"]
    R1["Ttb"], R1["k_Ttb"] = self.ub[:, 1, :], ["ub_1"]
    R1["vb"], R1["k_vb"] = self.ub[:, 2:4, :].rearrange("p a t -> p (a t)"), ["ub_23"]
    R1["kbg"], R1["k_kbg"] = self.xrb[:, 0].rearrange("p a t -> p (a t)"), ["xrb0"]
    R1["kdec"], R1["k_kdec"] = self.xrb[:, 1].rearrange("p a t -> p (a t)"), ["xrb1"]
    gens = [gdn_pair(self, 0, R0, ntok, nval), gdn_pair(self, 1, R1, ntok, nval)]
    alive = [True, True]
    while any(alive):
        for gi in range(2):
            if alive[gi]:
                try:
                    next(gens[gi])
                except StopIteration:
                    alive[gi] = False
    if state_out is not None:
        P.add("sp", lambda e: e.dma_start(out=state_out.rearrange("h k v -> k h v"), in_=self.S32[:]), r=["S32"], w=["gdn_out"], dma="out")


Builder.alloc_gdn = alloc_gdn
Builder.gdn_setup = gdn_setup
Builder.gdn_tile = gdn_tile


def alloc_swa(self):
    sb = self.sb
    self.kTd = sb("kTd", [128, 4, 128 + TT], BF16)
    self.vtm = sb("vtm", [128, 5, 256], BF16)
    self.esink = sb("esink", [128, 8], F32)
    self.onesb = sb("onesb", [128, 64], BF16)
    self.rcp = self.sg[:, 0:256]


def swa_setup(self):
    P, D = self.P, self.D
    sk = D["swa_sinks"][0].rearrange("(m two) -> two m", two=2)
    for half in range(2):
        P.add("sp", lambda e, half=half: e.dma_start(out=self.esink[64 * half:64 * half + 64, :], in_=sk[half].partition_broadcast(64),
                                                     allow_slow_non_contiguous=True), w=["esink"], dma="init")
    P.add("act", lambda e: e.activation(out=self.esink[:], in_=self.esink[:], func=AF.Exp), r=["esink"], w=["esink"])
    P.add("dve", lambda e: e.memset(self.onesb[:], 1.0), w=["onesb"])


def swa_init_sample(self, si):
    P, D = self.P, self.D
    ck, kck = _carve(self.hid, 14, 1, BF16)
    ck = ck[:, 0:256]
    P.add("pool", lambda e: e.dma_start(out=self.vtm[:, 0, :], in_=D["st_v"][si]), w=["vtm"], dma="ldkv")
    P.add("pool", lambda e: e.dma_start(out=ck, in_=D["st_k"][si]), w=kck, dma="ldkv")
    for kv in range(4):
        for half in range(2):
            P.add("pe", lambda e, kv=kv, half=half: e.matmul(self.ps[0][64 * half:64 * half + 64, kv * 128:(kv + 1) * 128], lhsT=ck[:, kv * 64:(kv + 1) * 64],
                                                             rhs=self.identb[:], start=True, stop=True, tile_position=(0, 64 * half)),
                  r=[*kck, "identb"], w=["ps0"])
    P.add("act", lambda e: e.activation(out=self.kTd[:, :, 0:128], in_=self.ps[0][:, :].rearrange("p (k t) -> p k t", t=128), func=AF.Copy), r=["ps0"], w=["kTd"])


def swa_tile(self, ntok, nval, first, kind, si, last):
    P, D = self.P, self.D
    nblk = ntok // 128
    nch = ntok // 64 if kind == "p" else 1
    hid = self.hid
    qT, k_qT = _carve(hid, 0, 8, BF16)
    qT = qT.rearrange("p (a t) -> p a t", t=TT)
    ETs, k_ETs = [], []
    for j in range(2):
        et, ke = _carve(hid, 8 + 2 * j, 2, BF16)
        ETs.append(et.rearrange("p (a t) -> p a t", t=TT))
        k_ETs.append(ke)
    st32, k_st = self.sg[:].rearrange("p (a t) -> p a t", t=256), ["sg"]
    Wq, Wk, Wv = D["swa_wq"][0], D["swa_wk"][0], D["swa_wv"][0]

    def evq(ci, m, b):
        P.add("act", lambda e: e.activation(out=qT[:, ci, 0:ntok], in_=self.ps[b][:, 0:ntok], func=AF.Copy, scale=0.125), r=[f"ps{b}"], w=k_qT)
    self.linear_fm(Wq, 0, 1024, lambda k: self.hT[:, k, 0:ntok], ["hT"], ntok, evq)
    if _SWASTOP == 1:
        P.add('dve', lambda e: e.memset(self.mixT[:], 0.0), w=['mixT'])
        return
    vk, kk_ = self.wload(Wk.rearrange("(c p) n -> p c n", p=128))
    for kv in range(4):
        b = self.bank()
        for half in range(2):
            for k in range(8):
                P.add("pe", lambda e, kv=kv, b=b, half=half, k=k: e.matmul(self.ps[b][64 * half:64 * half + 64, 0:ntok], lhsT=vk[:, k, kv * 64:(kv + 1) * 64],
                                                                           rhs=self.hT[:, k, 0:ntok], start=(k == 0), stop=(k == 7), tile_position=(0, 64 * half)),
                      r=[kk_, "hT"], w=[f"ps{b}"])
        P.add("act", lambda e, kv=kv, b=b: e.activation(out=self.kTd[:, kv, 128:128 + ntok], in_=self.ps[b][:, 0:ntok], func=AF.Copy), r=[f"ps{b}"], w=["kTd"])
    if _SWASTOP == 2:
        P.add('dve', lambda e: e.memset(self.mixT[:], 0.0), w=['mixT'])
        return
    if last:
        ob = nblk - 1
        b = self.bank()
        for k in range(8):
            P.add("pe", lambda e, b=b, k=k: e.matmul(self.ps[b][:, 0:256], lhsT=self.hT[:, k, ob * 128:(ob + 1) * 128], rhs=vk[:, k, :], start=(k == 0), stop=(k == 7)),
                  r=[kk_, "hT"], w=[f"ps{b}"])
        P.add("dve", lambda e, b=b: e.tensor_copy(out=st32[:, 0, :], in_=self.ps[b][:, 0:256]), r=[f"ps{b}"], w=k_st)
    if _SWASTOP == 3:
        P.add('dve', lambda e: e.memset(self.mixT[:], 0.0), w=['mixT'])
        return
    vv, kv_ = self.wload(Wv.rearrange("(c p) n -> p c n", p=128))
    for blk in range(min(nblk, int(os.environ.get('VBLK', '9')))):
        b = self.bank()
        for k in range(8):
            P.add("pe", lambda e, b=b, k=k, blk=blk: e.matmul(self.ps[b][:, 0:256], lhsT=self.hT[:, k, blk * 128:(blk + 1) * 128], rhs=vv[:, k, :],
                                                              start=(k == 0), stop=(k == 7)), r=[kv_, "hT"], w=[f"ps{b}"])
        P.add("act", lambda e, b=b, blk=blk: e.activation(out=self.vtm[:, 1 + blk, :], in_=self.ps[b][:, 0:256], func=AF.Copy), r=[f"ps{b}"], w=["vtm"])
        if last and blk == nblk - 1 and not (_RISK & 64):
            P.add("act", lambda e, b=b: e.activation(out=st32[:, 1, :], in_=self.ps[b][:, 0:256], func=AF.Copy), r=[f"ps{b}"], w=k_st)
    if last and not (_RISK & 32):
        if kind == "p":
            P.add("sp", lambda e: e.dma_start(out=D["p_k"][:, :], in_=st32[:, 0, :]), r=k_st, w=["ok"], dma="out")
            P.add("sp", lambda e: e.dma_start(out=D["p_v"][:, :], in_=st32[:, 1, :]), r=k_st, w=["ov"], dma="out")
        else:
            n0 = 128 - DEC_SEQ
            for j, (nm, src) in enumerate((("s_k", "st_k"), ("s_v", "st_v"))):
                P.add("sp", lambda e, nm=nm, src=src: e.dma_start(out=D[nm][si][0:n0, :], in_=D[src][si][DEC_SEQ:128, :]), w=[f"o{nm}a"], dma="out")
                P.add("sp", lambda e, nm=nm, j=j: e.dma_start(out=D[nm][si][n0:128, :], in_=st32[0:DEC_SEQ, j, :]), r=k_st, w=[f"o{nm}b"], dma="out")
    if _SWASTOP == 4:
        P.add('dve', lambda e: e.memset(self.mixT[:], 0.0), w=['mixT'])
        return
    mbc = self.cview("mb")
    steps = []
    for c in range(nch):
        if kind == "p":
            lo, hi = 64 * c, 64 * c + 192
            if first:
                lo = max(lo, 128)
        else:
            lo, hi = 0, 128 + nval
        pieces = []
        for blk in range(lo // 128, (hi - 1) // 128 + 1):
            a = max(lo, blk * 128) - blk * 128
            b_ = min(hi, (blk + 1) * 128) - blk * 128
            pieces.append((blk, a, b_))
        for gq in range(2):
            steps.append((c, gq, pieces))
    pS = [[self.ps[0], self.ps[1]], [self.ps[2], self.ps[3]]]
    kS = [["ps0", "ps1"], ["ps2", "ps3"]]

    def scores_exp(it):
        c, gq, pieces = steps[it]
        ET, k_ET = ETs[it % 2], k_ETs[it % 2]
        for pi, (blk, a, b_) in enumerate(pieces):
            mcol = {(0, 128): 0, (0, 64): 1, (64, 128): 2, (0, 16): 3}[(a, b_)]
            for mi in range(4):
                m = 4 * gq + mi
                kv = m // 2
                for half in range(2):
                    P.add("pe", lambda e, pi=pi, blk=blk, m=m, kv=kv, half=half, mi=mi, c=c: e.matmul(
                        pS[pi][half][:, mi * 64:(mi + 1) * 64], lhsT=self.kTd[64 * half:64 * half + 64, kv, blk * 128:(blk + 1) * 128],
                        rhs=qT[64 * half:64 * half + 64, m, c * 64:(c + 1) * 64], start=True, stop=True, tile_position=(64 * half, 0)),
                        r=["kTd", *k_qT], w=[kS[pi][half]])
            for half in range(2):
                P.add("act", lambda e, pi=pi, mcol=mcol, half=half, ET=ET: e.activation(
                    out=ET[:, pi, half * 256:(half + 1) * 256], in_=pS[pi][half][:, 0:256], func=AF.Exp, bias=mbc[:, mcol:mcol + 1]),
                    r=[kS[pi][half], "cst"], w=k_ET)

    def pv_out(it):
        c, gq, pieces = steps[it]
        ET, k_ET = ETs[it % 2], k_ETs[it % 2]
        pOD = self.ps[4 + it % 2]
        kOD = f"ps{4 + it % 2}"
        np_ = len(pieces)
        for mi in range(4):
            m = 4 * gq + mi
            kv = m // 2
            for half in range(2):
                col = (half * 4 + mi) * 64
                for pi, (blk, a, b_) in enumerate(pieces):
                    P.add("pe", lambda e, pi=pi, blk=blk, kv=kv, half=half, col=col, mi=mi: e.matmul(
                        pOD[64 * half:64 * half + 64, mi * 64:(mi + 1) * 64], lhsT=self.vtm[:, blk, kv * 64:(kv + 1) * 64],
                        rhs=ET[:, pi, col:col + 64], start=(pi == 0), stop=(pi == np_ - 1), tile_position=(0, 64 * half)),
                        r=["vtm", *k_ET], w=[kOD])
                for pi, (blk, a, b_) in enumerate(pieces):
                    P.add("pe", lambda e, pi=pi, half=half, col=col, mi=mi: e.matmul(
                        pOD[64 * half:64 * half + 64, 256 + mi * 64:256 + (mi + 1) * 64], lhsT=self.onesb[:, :],
                        rhs=ET[:, pi, col:col + 64], start=(pi == 0), stop=(pi == np_ - 1), tile_position=(0, 64 * half)),
                        r=["onesb", *k_ET], w=[kOD])
        rc3 = self.rcp.rearrange("p (m i) -> p m i", i=64)
        P.add("dve", lambda e: e.tensor_tensor(out=rc3, in0=pOD[:, 256:512].rearrange("p (m i) -> p m i", i=64),
                                               in1=self.esink[:, 4 * gq:4 * gq + 4].unsqueeze(2).to_broadcast([128, 4, 64]), op=ALU.add),
              r=[kOD, "esink"], w=["sg"])
        P.add("dve", lambda e: e.reciprocal(out=self.rcp, in_=self.rcp), r=["sg"], w=["sg"])
        P.add("dve", lambda e: e.tensor_tensor(out=self.mixT[:, 4 * gq:4 * gq + 4, c * 64:(c + 1) * 64],
                                               in0=pOD[:, 0:256].rearrange("p (m i) -> p m i", i=64), in1=rc3, op=ALU.mult),
              r=[kOD, "sg"], w=["mixT"])

    scores_exp(0)
    for it in range(len(steps)):
        if it + 1 < len(steps):
            scores_exp(it + 1)
        pv_out(it)
    if kind != "p" and ntok > 64:
        P.add("dve", lambda e: e.memset(self.mixT[:, :, 64:ntok], 0.0), w=["mixT"])
    if _SWASTOP == 5:
        P.add('dve', lambda e: e.memset(self.mixT[:], 0.0), w=['mixT'])
        return
    P.add("dve", lambda e: e.tensor_copy(out=self.kTd[:, :, 0:128], in_=self.kTd[:, :, ntok:ntok + 128]), r=["kTd"], w=["kTd"])
    P.add("dve", lambda e: e.tensor_copy(out=self.vtm[:, 0, :], in_=self.vtm[:, nblk, :]), r=["vtm"], w=["vtm"])
    self.linear_tm_res(D["swa_wo"][0], lambda k, b: self.mixT[:, k, b * 128:(b + 1) * 128], ["mixT"], nblk)


Builder.alloc_swa = alloc_swa
Builder.swa_setup = swa_setup
Builder.swa_init_sample = swa_init_sample
Builder.swa_tile = swa_tile
```
